# Optimizing a Trainium2 kernel written in Bass

```python
import jax, jax.numpy as jnp
from jax import lax
import numpy as np

D_MODEL = 1024
BATCH = 32
SEQ = 2048
DEPTH = 4

GRID_W = 64
CTX_LEN = 256
Q_BLOCK = 128
ROPE_THETA = 10000.0
NORM_EPS = 1e-6

GQA_HEADS = 8
GQA_KV_HEADS = 2
GQA_HEAD_DIM = 128
MLA_HEADS = 8
MLA_Q_RANK = 768
MLA_KV_RANK = 256
MLA_NOPE_DIM = 128
MLA_ROPE_DIM = 64
MLA_V_DIM = 128
N_EXPERTS = 16
EXPERT_FF = 1024
CAPACITY_FACTOR = 2

N_MIXERS = 2
N_GQA_LAYERS = (DEPTH + 1) // 2
N_MLA_LAYERS = DEPTH // 2
DEEPNORM_ALPHA = (2 * DEPTH) ** 0.25
DEEPNORM_BETA = (8 * DEPTH) ** -0.25
ADA_INIT = 0.5

kernel_name = "hybrid_gqa_mla_ecmoe_diffusion_trunk"


def layer_norm(x, g, b):
    xf = x.astype(jnp.float32)
    mu = jnp.mean(xf, axis=-1, keepdims=True)
    xc = xf - mu
    var = jnp.mean(xc * xc, axis=-1, keepdims=True)
    return (xc * lax.rsqrt(var + NORM_EPS) * g + b).astype(x.dtype)


def rms_norm(x, g):
    xf = x.astype(jnp.float32)
    return (xf * lax.rsqrt(jnp.mean(xf * xf, axis=-1, keepdims=True) + NORM_EPS) * g).astype(x.dtype)


def axial_rope_tables(n, rot_dim, dtype):
    t = jnp.arange(n, dtype=jnp.int32)
    row = (t // GRID_W).astype(jnp.float32)
    col = (t % GRID_W).astype(jnp.float32)
    axis_dim = rot_dim // 2
    freqs = ROPE_THETA ** (-jnp.arange(0, axis_dim, 2, dtype=jnp.float32) / axis_dim)
    ang = jnp.concatenate([row[:, None] * freqs, col[:, None] * freqs], axis=-1)
    return jnp.cos(ang).astype(dtype), jnp.sin(ang).astype(dtype)


def apply_rope(x, cos, sin):
    xp = x.reshape(*x.shape[:-1], -1, 2)
    x0, x1 = xp[..., 0], xp[..., 1]
    return jnp.stack([x0 * cos - x1 * sin, x0 * sin + x1 * cos], axis=-1).reshape(x.shape)


def split_blocks(a):
    b, n = a.shape[:2]
    return a.reshape(b, n // Q_BLOCK, Q_BLOCK, *a.shape[2:]).swapaxes(0, 1)


def merge_blocks(a):
    nb, b, qb = a.shape[:3]
    return a.swapaxes(0, 1).reshape(b, nb * qb, *a.shape[3:])


def modulate(h, shift, scale):
    return h * (1.0 + scale) + shift


def gqa_attention(q, k, v):
    scale = GQA_HEAD_DIM ** -0.5

    def block(qb):
        s = jnp.einsum('bqhgd,bkhd->bhgqk', qb, k).astype(jnp.float32) * scale
        p = jax.nn.softmax(s, axis=-1).astype(v.dtype)
        return jnp.einsum('bhgqk,bkhd->bqhgd', p, v)

    o = merge_blocks(lax.map(block, split_blocks(q)))
    return o.reshape(*o.shape[:2], -1)


def gqa_project(h, w_qkv, q_g, k_g):
    b, n, _ = h.shape
    nq = GQA_HEADS * GQA_HEAD_DIM
    nk = GQA_KV_HEADS * GQA_HEAD_DIM
    qkv = h @ w_qkv
    q = qkv[..., :nq].reshape(b, n, GQA_KV_HEADS, GQA_HEADS // GQA_KV_HEADS, GQA_HEAD_DIM)
    k = qkv[..., nq:nq + nk].reshape(b, n, GQA_KV_HEADS, GQA_HEAD_DIM)
    v = qkv[..., nq + nk:].reshape(b, n, GQA_KV_HEADS, GQA_HEAD_DIM)
    return rms_norm(q, q_g), rms_norm(k, k_g), v


def gqa_mixer(h_lat, h_ctx, w_qkv, q_g, k_g, w_o, with_ctx_out):
    q_l, k_l, v_l = gqa_project(h_lat, w_qkv, q_g, k_g)
    cos, sin = axial_rope_tables(h_lat.shape[1], GQA_HEAD_DIM, h_lat.dtype)
    q_l = apply_rope(q_l, cos[:, None, None, :], sin[:, None, None, :])
    k_l = apply_rope(k_l, cos[:, None, :], sin[:, None, :])
    q_c, k_c, v_c = gqa_project(h_ctx, w_qkv, q_g, k_g)
    k_all = jnp.concatenate([k_c, k_l], axis=1)
    v_all = jnp.concatenate([v_c, v_l], axis=1)
    o_lat = gqa_attention(q_l, k_all, v_all) @ w_o
    o_ctx = gqa_attention(q_c, k_c, v_c) @ w_o if with_ctx_out else None
    return o_lat, o_ctx


def mla_attention(q_nope, q_rope, k_nope, k_rope, v):
    scale = (MLA_NOPE_DIM + MLA_ROPE_DIM) ** -0.5

    def block(qs):
        qn, qr = qs
        s = (jnp.einsum('bqhd,bkhd->bhqk', qn, k_nope)
             + jnp.einsum('bqhr,bkr->bhqk', qr, k_rope)).astype(jnp.float32) * scale
        p = jax.nn.softmax(s, axis=-1).astype(v.dtype)
        return jnp.einsum('bhqk,bkhd->bqhd', p, v)

    o = merge_blocks(lax.map(block, (split_blocks(q_nope), split_blocks(q_rope))))
    return o.reshape(*o.shape[:2], -1)


def mla_project(h, w_dq, q_g, w_uq, w_dkv, kv_g, w_ukv):
    b, n, _ = h.shape
    q = (rms_norm(h @ w_dq, q_g) @ w_uq).reshape(b, n, MLA_HEADS, MLA_NOPE_DIM + MLA_ROPE_DIM)
    q_nope, q_rope = q[..., :MLA_NOPE_DIM], q[..., MLA_NOPE_DIM:]
    ckv = h @ w_dkv
    c_kv, k_rope = ckv[..., :MLA_KV_RANK], ckv[..., MLA_KV_RANK:]
    kv = (rms_norm(c_kv, kv_g) @ w_ukv).reshape(b, n, MLA_HEADS, MLA_NOPE_DIM + MLA_V_DIM)
    k_nope, v = kv[..., :MLA_NOPE_DIM], kv[..., MLA_NOPE_DIM:]
    return q_nope, q_rope, k_nope, k_rope, v


def mla_mixer(h_lat, h_ctx, w_dq, q_g, w_uq, w_dkv, kv_g, w_ukv, w_o, with_ctx_out):
    qn_l, qr_l, kn_l, kr_l, v_l = mla_project(h_lat, w_dq, q_g, w_uq, w_dkv, kv_g, w_ukv)
    cos, sin = axial_rope_tables(h_lat.shape[1], MLA_ROPE_DIM, h_lat.dtype)
    qr_l = apply_rope(qr_l, cos[:, None, :], sin[:, None, :])
    kr_l = apply_rope(kr_l, cos, sin)
    qn_c, qr_c, kn_c, kr_c, v_c = mla_project(h_ctx, w_dq, q_g, w_uq, w_dkv, kv_g, w_ukv)
    kn_all = jnp.concatenate([kn_c, kn_l], axis=1)
    kr_all = jnp.concatenate([kr_c, kr_l], axis=1)
    v_all = jnp.concatenate([v_c, v_l], axis=1)
    o_lat = mla_attention(qn_l, qr_l, kn_all, kr_all, v_all) @ w_o
    o_ctx = mla_attention(qn_c, qr_c, kn_c, kr_c, v_c) @ w_o if with_ctx_out else None
    return o_lat, o_ctx


def ec_moe(h, w_router, w_gate, w_up, w_down):
    b, n, _ = h.shape
    cap = CAPACITY_FACTOR * n // N_EXPERTS
    aff = jax.nn.softmax((h @ w_router).astype(jnp.float32), axis=-1)
    g, idx = lax.top_k(aff.swapaxes(1, 2), cap)
    bidx = jnp.arange(b)[:, None, None]
    xs = h[bidx, idx]
    a = jnp.einsum('becd,edf->becf', xs, w_gate)
    u = jnp.einsum('becd,edf->becf', xs, w_up)
    y = jnp.einsum('becf,efd->becd', jax.nn.silu(a) * u, w_down)
    y = (y * g[..., None].astype(y.dtype)).astype(h.dtype)
    return jnp.zeros_like(h).at[bidx, idx].add(y)


def setup_inputs(seed: int = 0) -> dict:
    key = jax.random.key(seed)
    ks = iter(jax.random.split(key, 32))

    def nrm(shape, scale):
        return jax.random.normal(next(ks), shape, jnp.float32) * scale

    D, L, E, F = D_MODEL, DEPTH, N_EXPERTS, EXPERT_FF
    NG, NM = N_GQA_LAYERS, N_MLA_LAYERS
    gqa_cols = (GQA_HEADS + 2 * GQA_KV_HEADS) * GQA_HEAD_DIM
    return {
        "x": nrm((BATCH, SEQ, D), 1.0),
        "c": nrm((BATCH, D), 1.0),
        "ctx": nrm((BATCH, CTX_LEN, D), 1.0),
        "c_ctx": nrm((D,), 1.0),
        "ada_w": nrm((L, D, 6 * D), ADA_INIT * D ** -0.5),
        "ada_b": nrm((L, 6 * D), 0.02),
        "ln_mix_g": 1.0 + nrm((L, D), 0.02),
        "ln_mix_b": nrm((L, D), 0.02),
        "ln_ffn_g": 1.0 + nrm((L, D), 0.02),
        "ln_ffn_b": nrm((L, D), 0.02),
        "router_w": nrm((L, D, E), D ** -0.5),
        "expert_w_gate": nrm((L, E, D, F), D ** -0.5),
        "expert_w_up": nrm((L, E, D, F), D ** -0.5),
        "expert_w_down": nrm((L, E, F, D), DEEPNORM_BETA * F ** -0.5),
        "gqa_w_qkv": nrm((NG, D, gqa_cols), D ** -0.5),
        "gqa_q_g": 1.0 + nrm((NG, GQA_HEAD_DIM), 0.02),
        "gqa_k_g": 1.0 + nrm((NG, GQA_HEAD_DIM), 0.02),
        "gqa_w_o": nrm((NG, GQA_HEADS * GQA_HEAD_DIM, D), DEEPNORM_BETA * (GQA_HEADS * GQA_HEAD_DIM) ** -0.5),
        "mla_w_dq": nrm((NM, D, MLA_Q_RANK), D ** -0.5),
        "mla_q_g": 1.0 + nrm((NM, MLA_Q_RANK), 0.02),
        "mla_w_uq": nrm((NM, MLA_Q_RANK, MLA_HEADS * (MLA_NOPE_DIM + MLA_ROPE_DIM)), MLA_Q_RANK ** -0.5),
        "mla_w_dkv": nrm((NM, D, MLA_KV_RANK + MLA_ROPE_DIM), D ** -0.5),
        "mla_kv_g": 1.0 + nrm((NM, MLA_KV_RANK), 0.02),
        "mla_w_ukv": nrm((NM, MLA_KV_RANK, MLA_HEADS * (MLA_NOPE_DIM + MLA_V_DIM)), MLA_KV_RANK ** -0.5),
        "mla_w_o": nrm((NM, MLA_HEADS * MLA_V_DIM, D), DEEPNORM_BETA * (MLA_HEADS * MLA_V_DIM) ** -0.5),
    }


def reference(x, c, ctx, c_ctx, ada_w, ada_b, ln_mix_g, ln_mix_b, ln_ffn_g, ln_ffn_b,
              router_w, expert_w_gate, expert_w_up, expert_w_down,
              gqa_w_qkv, gqa_q_g, gqa_k_g, gqa_w_o,
              mla_w_dq, mla_q_g, mla_w_uq, mla_w_dkv, mla_kv_g, mla_w_ukv, mla_w_o):
    silu_c = jax.nn.silu(c)
    silu_cc = jax.nn.silu(c_ctx)
    for i in range(DEPTH):
        last = i == DEPTH - 1
        j = i // N_MIXERS
        mod_l = (silu_c @ ada_w[i] + ada_b[i])[:, None, :]
        mod_c = silu_cc @ ada_w[i] + ada_b[i]
        sh1_l, sc1_l, g1_l, sh2_l, sc2_l, g2_l = jnp.split(mod_l, 6, axis=-1)
        sh1_c, sc1_c, g1_c, sh2_c, sc2_c, g2_c = jnp.split(mod_c, 6, axis=-1)

        h_l = modulate(x, sh1_l, sc1_l)
        h_c = modulate(ctx, sh1_c, sc1_c)
        if i % N_MIXERS == 0:
            o_l, o_c = gqa_mixer(h_l, h_c, gqa_w_qkv[j], gqa_q_g[j], gqa_k_g[j], gqa_w_o[j], not last)
        else:
            o_l, o_c = mla_mixer(h_l, h_c, mla_w_dq[j], mla_q_g[j], mla_w_uq[j], mla_w_dkv[j],
                                 mla_kv_g[j], mla_w_ukv[j], mla_w_o[j], not last)
        x = layer_norm(DEEPNORM_ALPHA * x + g1_l * o_l, ln_mix_g[i], ln_mix_b[i])

        f_l = ec_moe(modulate(x, sh2_l, sc2_l), router_w[i], expert_w_gate[i], expert_w_up[i], expert_w_down[i])
        x = layer_norm(DEEPNORM_ALPHA * x + g2_l * f_l, ln_ffn_g[i], ln_ffn_b[i])

        if not last:
            ctx = layer_norm(DEEPNORM_ALPHA * ctx + g1_c * o_c, ln_mix_g[i], ln_mix_b[i])
            f_c = ec_moe(modulate(ctx, sh2_c, sc2_c), router_w[i], expert_w_gate[i], expert_w_up[i], expert_w_down[i])
            ctx = layer_norm(DEEPNORM_ALPHA * ctx + g2_c * f_c, ln_ffn_g[i], ln_ffn_b[i])
    return x
```

```python
import numpy as np
from contextlib import ExitStack
import concourse.bass as bass
import concourse.mybir as mybir
from concourse.bass_utils import run_bass_kernel_spmd

F32 = mybir.dt.float32
BF16 = mybir.dt.bfloat16
I32 = mybir.dt.int32
U32 = mybir.dt.uint32
AF = mybir.ActivationFunctionType
ALU = mybir.AluOpType
AX = mybir.AxisListType

ENGS = ("pe", "act", "dve", "pool", "sp")

D = 1024
KC = 8
NCORES = 8
NS = 4
SEQ = 2048
CTX = 256
NR = SEQ + CTX
NT = NR // 128
E = 16
CAPL = 256
CAPC = 32
DEPTH = 4
ALPHA = float((2 * DEPTH) ** 0.25)
EPS = 1e-6
THETA = 10000.0
GRID_W = 64


class Buf:
    __slots__ = ("w", "r")

    def __init__(self):
        self.w = []
        self.r = []


def bufs(n):
    return [Buf() for _ in range(n)]


class DmaGroup:
    __slots__ = ("ring", "sem_i", "total", "prev_total", "first")


class Sched:
    def __init__(self, nc, esems, rings):
        self.nc = nc
        self.esems = esems
        self.rings = rings
        self.cnt = {e: 0 for e in ENGS}
        self.dtot = {q: [0] * len(v) for q, v in rings.items()}
        self.dnext = {q: 0 for q in rings}
        self.seen = {e: {} for e in ENGS}
        self.q = {e: [] for e in ENGS}
        self.ninst = 0
        self.muted = False

    def _deps(self, reads, writes, cowrites=()):
        deps = []
        for b in reads:
            for t in b.w:
                deps.append((t, True))
        for b in writes:
            for t in b.w:
                deps.append((t, False))
            for t in b.r:
                deps.append((t, False))
        for b in cowrites:
            for t in b.r:
                deps.append((t, False))
        return deps

    def op(self, eng, fn, reads=(), writes=()):
        if self.muted:
            return ("dv", "sp", 0, 0)
        deps = self._deps(reads, writes)
        self.cnt[eng] += 1
        tok = ("e", eng, self.cnt[eng])
        self.q[eng].append((deps, fn, ("e", eng)))
        for b in reads:
            b.r.append(tok)
        for b in writes:
            b.w = [tok]
            b.r = []
        self.ninst += 1
        return tok

    def dma_group(self, eng):
        i = self.dnext[eng]
        self.dnext[eng] = (i + 1) % len(self.rings[eng])
        g = DmaGroup()
        g.ring = eng
        g.sem_i = i
        g.total = self.dtot[eng][i]
        g.prev_total = self.dtot[eng][i]
        g.first = True
        return g

    def dma(self, eng, fn, reads=(), writes=(), group=None, cowrites=()):
        if self.muted:
            return ("dv", "sp", 0, 0)
        g = group if group is not None else self.dma_group(eng)
        assert g.ring == eng
        deps = self._deps(reads, writes, cowrites)
        if g.first:
            if g.prev_total > 0:
                deps.append((("dv", eng, g.sem_i, g.prev_total), True))
            g.first = False
        self.dtot[eng][g.sem_i] += 16
        g.total = self.dtot[eng][g.sem_i]
        tok = ("d", g)
        self.q[eng].append((deps, fn, ("d", eng, g.sem_i)))
        for b in reads:
            b.r.append(tok)
        for b in writes:
            b.w = [tok]
            b.r = []
        for b in cowrites:
            b.w.append(tok)
        self.ninst += 1
        return tok

    def _resolve(self, dep):
        if dep[0] == "e":
            return ("e", dep[1]), dep[2]
        if dep[0] == "d":
            g = dep[1]
            return ("d", g.ring, g.sem_i), g.total
        return ("d", dep[1], dep[2]), dep[3]

    def _sem(self, key):
        return self.esems[key[1]] if key[0] == "e" else self.rings[key[1]][key[2]]

    def replay_engine(self, eng, h):
        seen = self.seen[eng]
        for deps, fn, inc in self.q[eng]:
            need = {}
            for d, raw in deps:
                if d[0] == "e" and d[1] == eng and eng == "pe":
                    continue
                key, val = self._resolve(d)
                if val > seen.get(key, 0) and val > need.get(key, 0):
                    need[key] = val
            for key, val in need.items():
                h.wait_ge(self._sem(key), val)
                seen[key] = val
            ins = fn(h)
            if inc[0] == "e":
                ins.then_inc(self.esems[inc[1]], 1)
            else:
                ins.then_inc(self.rings[inc[1]][inc[2]], 16)
        self.q[eng] = []

    def run_block(self, final_waits=()):
        fence = getattr(self, "fence", [])
        for eng in ENGS:
            if fence and self.q[eng]:
                deps, fn, inc = self.q[eng][0]
                self.q[eng][0] = (list(deps) + [(d, True) for d in fence if not (d[0] == "e" and d[1] == eng)], fn, inc)
        with self.nc.Block() as block:
            @block.tensor
            def _(h):
                self.replay_engine("pe", h)

            @block.scalar
            def _(h):
                self.replay_engine("act", h)

            @block.vector
            def _(h):
                self.replay_engine("dve", h)

            @block.gpsimd
            def _(h):
                self.replay_engine("pool", h)

            @block.sync
            def _(h):
                self.replay_engine("sp", h)
                for tok in final_waits:
                    key, val = self._resolve(tok)
                    h.wait_ge(self._sem(key), val)
        self.fence = [("e", e, self.cnt[e]) for e in ENGS if self.cnt[e] > 0]
        for q, tots in self.dtot.items():
            for i, v in enumerate(tots):
                if v > 0:
                    self.fence.append(("dv", q, i, v))


def I_mm(out, lhsT, rhs, start, stop):
    return lambda h: h.matmul(out, lhsT=lhsT, rhs=rhs, start=start, stop=stop)


def I_tr(out, in_, ident):
    return lambda h: h.transpose(out=out, in_=in_, identity=ident)


def I_act(out, in_, func, scale=1.0, bias=0.0, accum_out=None):
    if accum_out is None:
        return lambda h: h.activation(out=out, in_=in_, func=func, bias=bias, scale=scale)
    return lambda h: h.activation(out=out, in_=in_, func=func, bias=bias, scale=scale, accum_out=accum_out)


def I_tt(out, in0, in1, op):
    return lambda h: h.tensor_tensor(out=out, in0=in0, in1=in1, op=op)


def I_ts(out, in0, s1, s2, op0, op1=None):
    if op1 is None:
        return lambda h: h.tensor_scalar(out=out, in0=in0, scalar1=s1, scalar2=None, op0=op0)
    return lambda h: h.tensor_scalar(out=out, in0=in0, scalar1=s1, scalar2=s2, op0=op0, op1=op1)


def I_stt(out, in0, scalar, in1, op0, op1):
    return lambda h: h.scalar_tensor_tensor(out=out, in0=in0, scalar=scalar, in1=in1, op0=op0, op1=op1)


def I_acopy(out, in_):
    return lambda h: h.activation(out=out, in_=in_, func=AF.Copy)


def I_copy(out, in_):
    return lambda h: h.tensor_copy(out=out, in_=in_)


def I_dma(out, in_):
    return lambda h: h.dma_start(out=out, in_=in_)


def I_memset(ap, v):
    return lambda h: h.memset(ap, v)


def rope_tables(rot_dim):
    t = np.arange(SEQ, dtype=np.int32)
    row = (t // GRID_W).astype(np.float32)
    col = (t % GRID_W).astype(np.float32)
    axis_dim = rot_dim // 2
    freqs = (np.float32(THETA) ** (-np.arange(0, axis_dim, 2, dtype=np.float32) / np.float32(axis_dim))).astype(np.float32)
    ang = np.concatenate([row[:, None] * freqs, col[:, None] * freqs], axis=-1).astype(np.float32)
    cos = np.ones((NR, rot_dim // 2), np.float32)
    sin = np.zeros((NR, rot_dim // 2), np.float32)
    cos[CTX:] = np.cos(ang)
    sin[CTX:] = np.sin(ang)
    cos = np.ascontiguousarray(cos.reshape(NT, 128, -1).transpose(1, 0, 2))
    sin = np.ascontiguousarray(sin.reshape(NT, 128, -1).transpose(1, 0, 2))
    return cos, sin


class _Stop(Exception):
    pass


import os
KSTOP = os.environ.get("KSTOP", "")


class Prog:
    def __init__(self, layers, first, final):
        self.layers = layers
        self.first = first
        self.final = final
        nc = self.nc = bass.Bass("TRN2", target_bir_lowering=False)
        ext = self.ext = {}

        def inp(name, shape, dt=F32):
            ext[name] = nc.dram_tensor(name, shape, dt, kind="ExternalInput").ap()
            return ext[name]

        inp("xa_in", [NS * NR, D])
        inp("ccT", [128, KC, NS + 1])
        inp("identf", [128, 128])
        inp("cosg", [128, NT, 64]); inp("sing", [128, NT, 64])
        inp("cosm", [128, NT, 32]); inp("sinm", [128, NT, 32])
        inp("offs", [64, 2])
        for l in layers:
            inp(f"ada_w{l}", [D, 6 * D]); inp(f"ada_b{l}", [1, 6 * D]); inp(f"ada_bT{l}", [128, 48])
            inp(f"lnmg{l}", [1, D]); inp(f"lnmb{l}", [1, D]); inp(f"lnfg{l}", [1, D]); inp(f"lnfb{l}", [1, D])
            inp(f"rw{l}", [D, E])
            inp(f"wg{l}", [E, D, D]); inp(f"wu{l}", [E, D, D]); inp(f"wd{l}", [E, D, D])
            if l % 2 == 0:
                inp(f"wqkv{l}", [D, 1536]); inp(f"qg{l}", [1, 128]); inp(f"kg{l}", [1, 128]); inp(f"wo{l}", [D, D])
            else:
                inp(f"wdq{l}", [D, 768]); inp(f"mqg{l}", [1, 768]); inp(f"wuq{l}", [768, 1536])
                inp(f"wdkv{l}", [D, 320]); inp(f"mkvg{l}", [1, 256]); inp(f"wukv{l}", [256, 2048]); inp(f"wo{l}", [D, D])
        if final:
            self.y = nc.dram_tensor("y", [NS * SEQ, D], F32, kind="ExternalOutput").ap()
        else:
            self.y = nc.dram_tensor("xa_out", [NS * NR, D], F32, kind="ExternalOutput").ap()
        self.XA = nc.dram_tensor("XA", [NS * NR, D], F32).ap()
        self.XB = nc.dram_tensor("XB", [NS * NR, D], F32).ap()
        self.FA = nc.dram_tensor("FA", [NS * NR, D], F32).ap()
        self.KT = nc.dram_tensor("KT", [NS, 9, 128, NR], BF16).ap()
        self.VS = nc.dram_tensor("VS", [NS, 8, 128, NT, 128], BF16).ap()
        self.QT = nc.dram_tensor("QT", [NS, 8, 128, NR], BF16).ap()
        self.QR = nc.dram_tensor("QR", [NS, 8, 128, NR], BF16).ap()
        self.AFL = nc.dram_tensor("AFL", [NS * E, SEQ], F32).ap()
        self.AFC = nc.dram_tensor("AFC", [NS * E, CTX], F32).ap()
        self.GS = nc.dram_tensor("GS", [2, NS + 1, 128, D], F32).ap()
        self.build()

    def build(self):
        nc = self.nc
        with ExitStack() as es:
            esems = {e: es.enter_context(nc.semaphore("s_" + e)) for e in ENGS}
            rings = {q: [es.enter_context(nc.semaphore(f"d_{q}{i}")) for i in range(8)] for q in ("sp", "pool")}
            S = self.S = Sched(nc, esems, rings)
            self.PA = es.enter_context(nc.psum_tensor("PA", [128, 1024], F32))
            self.PB = es.enter_context(nc.psum_tensor("PB", [128, 1024], F32))
            self.PC = es.enter_context(nc.psum_tensor("PC", [128, 1024], F32))
            self.PD = es.enter_context(nc.psum_tensor("PD", [128, 1024], F32))
            self.bPA = bufs(2)
            self.bPB = bufs(2)
            self.bPBh = self.bPB
            self.bPC = bufs(2)
            self.bPD = bufs(2)
            self.identF = es.enter_context(nc.sbuf_tensor("identF", [128, 128], F32))
            self.identB = es.enter_context(nc.sbuf_tensor("identB", [128, 128], BF16))
            self.onesB = es.enter_context(nc.sbuf_tensor("onesB", [128, 128], BF16))
            self.zeros = es.enter_context(nc.sbuf_tensor("zeros", [128, D], F32))
            self.modT = es.enter_context(nc.sbuf_tensor("modT", [128, 4, KC, 8], F32))
            self.bconst, self.bmodT = Buf(), Buf()
            self.bXA = bufs(NS); self.bXB = bufs(NS); self.bFAl = bufs(NS); self.bFAc = Buf()
            self.bKT = bufs(NS); self.bVS = bufs(NS); self.bQT = bufs(NS)
            self.bAF = Buf(); self.bGS = Buf(); self.bY = Buf()
            self.out_toks = []

            S.dma("sp", I_dma(self.identF[:], self.ext["identf"]), writes=[self.bconst])
            S.op("dve", I_copy(self.identB[:], self.identF[:]), reads=[self.bconst], writes=[self.bconst])
            S.op("pool", I_memset(self.onesB[:], 1.0), writes=[self.bconst])
            S.op("pool", I_memset(self.zeros[:], 0.0), writes=[self.bconst])
            for s in range(NS):
                S.dma("sp", I_dma(self.XA[s * NR:(s + 1) * NR, :], self.ext["xa_in"][s * NR:(s + 1) * NR, :]),
                      writes=[self.bXA[s]])
            S.run_block()

            nl = len(self.layers)
            try:
                for li, l in enumerate(self.layers):
                    last = self.final and li == nl - 1
                    self.phase_M(l)
                    self.phase_A1(l)
                    self.phase_A2(l, last)
                    self.stop("A")
                    self.phase_R(l, last)
                    self.stop("R")
                    self.phase_C(l, last)
                    self.stop("C")
                    self.phase_D(l, last, is_out=(li == nl - 1))
            except _Stop:
                pass

    def stop(self, tag):
        if KSTOP == tag and not self.S.muted:
            self.S.muted = True

    def phase_M(self, l):
        nc, S, ext = self.nc, self.S, self.ext
        NJ = NS + 1
        PD, PB = self.PD, self.PB
        with ExitStack() as es:
            T = lambda n, sh, dt: es.enter_context(nc.sbuf_tensor(f"{n}_L{l}", sh, dt))
            cc = T("m_cc", [128, KC, NJ], F32)
            sT = T("m_sT", [128, KC, NJ], F32)
            sTb = T("m_sTb", [128, KC, NJ], BF16)
            onesf = T("m_ones", [128, 128], F32)
            sBC = T("m_sBC", [128, KC, NJ, 128], BF16)
            W = [T(f"m_W{i}", [128, KC, 1024], BF16) for i in range(2)]
            bT = T("m_bT", [128, 48], F32)
            bBC = T("m_bBC", [128, 2, 1024], F32)
            Gst = [T(f"m_G{i}", [128, 1024], F32) for i in range(2)]
            b_cc, b_sT, b_sTb, b_ones, b_sBC, b_bT, b_bBC = bufs(7)
            b_W = bufs(2); b_G = bufs(2)
            S.dma("sp", I_dma(cc[:], ext["ccT"]), writes=[b_cc])
            S.dma("sp", I_dma(bT[:], ext[f"ada_bT{l}"]), writes=[b_bT])
            for gi, c0 in enumerate((2 * D, 5 * D)):
                S.dma("sp", I_dma(bBC[:, gi, :], ext[f"ada_b{l}"][0, c0:c0 + D].partition_broadcast(128)), writes=[b_bBC])
            S.op("act", I_act(sT[:], cc[:], AF.Silu), reads=[b_cc], writes=[b_sT])
            S.op("dve", I_copy(sTb[:], sT[:]), reads=[b_sT], writes=[b_sTb])
            S.op("pool", I_memset(onesf[:], 1.0), writes=[b_ones])
            for k in range(KC):
                for j in range(NJ):
                    S.op("dve", I_ts(sBC[:, k, j, :], onesf[:], sT[:, k, j:j + 1], None, ALU.mult),
                         reads=[b_ones, b_sT], writes=[b_sBC])
            wv = ext[f"ada_w{l}"].rearrange("(k p) c -> p k c", p=128)
            kinds = {0: 0, 1: 1, 3: 2, 4: 3}
            gcount = 0
            for cb in range(6):
                wb = W[cb % 2]; bw = b_W[cb % 2]
                S.dma("pool", I_dma(wb[:], wv[:, :, cb * 1024:(cb + 1) * 1024]), writes=[bw])
                if cb in kinds:
                    kind = kinds[cb]
                    for oc in range(KC):
                        for k in range(KC):
                            S.op("pe", I_mm(PD[:, oc * 8:oc * 8 + NJ], wb[:, k, oc * 128:(oc + 1) * 128], sTb[:, k, :],
                                            k == 0, k == KC - 1), reads=[bw, b_sTb], writes=[self.bPD[0]])
                    pv = PD[:, 0:64].rearrange("p (o j) -> p o j", j=8)[:, :, 0:NJ]
                    bb = bT[:, cb * 8:(cb + 1) * 8].unsqueeze(2).to_broadcast([128, KC, NJ])
                    S.op("dve", I_tt(self.modT[:, kind, :, 0:NJ], pv, bb, ALU.add),
                         reads=[self.bPD[0], b_bT], writes=[self.bmodT])
                    if kind in (1, 3):
                        S.op("dve", I_ts(self.modT[:, kind, :, 0:NJ], self.modT[:, kind, :, 0:NJ], 1.0, None, ALU.add),
                             reads=[self.bmodT], writes=[self.bmodT])
                else:
                    gi = 0 if cb == 2 else 1
                    for j in range(NJ):
                        gs = Gst[gcount % 2]; bg = b_G[gcount % 2]; gcount += 1
                        for half in range(2):
                            for k in range(KC):
                                S.op("pe", I_mm(PB[:, half * 512:(half + 1) * 512], sBC[:, k, j, :],
                                                wb[:, k, half * 512:(half + 1) * 512], k == 0, k == KC - 1),
                                     reads=[bw, b_sBC], writes=self.bPB)
                        S.op("dve", I_tt(gs[:], PB[:], bBC[:, gi, :], ALU.add), reads=self.bPB + [b_bBC], writes=[bg])
                        S.dma("sp", I_dma(self.GS[gi, j], gs[:]), reads=[bg], cowrites=[self.bGS])
            S.run_block()

    def transpose_mod(self, src, kind_sc, kind_sh, j, dst, b_src, b_dst, split=False):
        S, PA = self.S, self.PA
        bd = b_dst if isinstance(b_dst, list) else [b_dst] * KC
        for k in range(KC):
            S.op("pe", I_tr(PA[:, k * 128:(k + 1) * 128], src[:, k * 128:(k + 1) * 128], self.identF[:]),
                 reads=[b_src, self.bconst], writes=[self.bPA[k // 4]])
        for k in range(KC):
            sc = self.modT[:, kind_sc, k, j:j + 1]; sh = self.modT[:, kind_sh, k, j:j + 1]
            if split and k % 2 == 1:
                S.op("dve", I_ts(dst[:, k, :], PA[:, k * 128:(k + 1) * 128], sc, sh, ALU.mult, ALU.add),
                     reads=[self.bPA[k // 4], self.bmodT], writes=[bd[k]])
            else:
                S.op("act", I_act(dst[:, k, :], PA[:, k * 128:(k + 1) * 128], AF.Identity, scale=sc, bias=sh),
                     reads=[self.bPA[k // 4], self.bmodT], writes=[bd[k]])

    def rms_rstd(self, W, src_f, nh, hd, b_src):
        S = self.S
        sq, ss, lnv, rstd = W["sq"], W["ss"], W["lnv"], W["rstd"]
        n = nh * hd
        S.op("dve", I_tt(sq[:, 0:n], src_f[:, 0:n], src_f[:, 0:n], ALU.mult), reads=[b_src], writes=[W["b_sq"]])
        S.op("dve", lambda h: h.tensor_reduce(out=ss[:, 0:nh], in_=sq[:, 0:n].rearrange("p (h d) -> p h d", d=hd),
                                              axis=AX.X, op=ALU.add), reads=[W["b_sq"]], writes=[W["b_ss"]])
        S.op("act", I_act(lnv[:, 0:nh], ss[:, 0:nh], AF.Ln, scale=1.0 / hd, bias=W["epsc"][:, 0:1]),
             reads=[W["b_ss"], W["b_eps"]], writes=[W["b_lnv"]])
        S.op("act", I_act(rstd[:, 0:nh], lnv[:, 0:nh], AF.Exp, scale=-0.5), reads=[W["b_lnv"]], writes=[W["b_rstd"]])

    def rope(self, W, src, nh, hd, cos, sin, dst, b_src, b_dst, b_tab):
        S = self.S
        hp = hd // 2
        x0 = src[:, :, 0::2]; x1 = src[:, :, 1::2]
        cb = cos.unsqueeze(1).to_broadcast([128, nh, hp]); sb = sin.unsqueeze(1).to_broadcast([128, nh, hp])
        ra = W["ra"][:, 0:nh * hp].rearrange("p (h d) -> p h d", d=hp)
        rb = W["rb"][:, 0:nh * hp].rearrange("p (h d) -> p h d", d=hp)
        rc = W["rc"][:, 0:nh * hp].rearrange("p (h d) -> p h d", d=hp)
        rd = W["rd"][:, 0:nh * hp].rearrange("p (h d) -> p h d", d=hp)
        S.op("dve", I_tt(ra, x0, cb, ALU.mult), reads=[b_src, b_tab], writes=[W["b_ra"]])
        S.op("dve", I_tt(rb, x1, sb, ALU.mult), reads=[b_src, b_tab], writes=[W["b_rb"]])
        S.op("dve", I_tt(rc, x0, sb, ALU.mult), reads=[b_src, b_tab], writes=[W["b_rc"]])
        S.op("dve", I_tt(rd, x1, cb, ALU.mult), reads=[b_src, b_tab], writes=[W["b_rd"]])
        S.op("dve", I_tt(dst[:, :, 0::2], ra, rb, ALU.subtract), reads=[W["b_ra"], W["b_rb"]], writes=[b_dst])
        S.op("dve", I_tt(dst[:, :, 1::2], rc, rd, ALU.add), reads=[W["b_rc"], W["b_rd"]], writes=[b_dst])

    def resid_ln(self, W, P, x, G, lng, lnb, out, reads, b_out):
        S = self.S
        t = W["t"]; st = W["st"]; mv = W["mv"]; lnv = W["lnv1"]; rstd = W["rstd1"]; nmr = W["nmr"]
        S.op("dve", I_tt(t[:], P, G, ALU.mult), reads=reads, writes=[W["b_t"]])
        S.op("dve", I_stt(t[:], x, ALPHA, t[:], ALU.mult, ALU.add), reads=reads + [W["b_t"]], writes=[W["b_t"]])
        S.op("dve", lambda h: h.bn_stats(out=st[:, 0, :], in_=t[:, 0:512]), reads=[W["b_t"]], writes=[W["b_st"]])
        S.op("dve", lambda h: h.bn_stats(out=st[:, 1, :], in_=t[:, 512:1024]), reads=[W["b_t"]], writes=[W["b_st"]])
        S.op("dve", lambda h: h.bn_aggr(out=mv[:], in_=st[:].rearrange("p a b -> p (a b)")), reads=[W["b_st"]], writes=[W["b_mv"]])
        S.op("act", I_act(lnv[:], mv[:, 1:2], AF.Ln, scale=1.0, bias=W["epsc"][:, 0:1]), reads=[W["b_mv"], W["b_eps"]], writes=[W["b_lnv1"]])
        S.op("act", I_act(rstd[:], lnv[:], AF.Exp, scale=-0.5), reads=[W["b_lnv1"]], writes=[W["b_rstd1"]])
        S.op("dve", I_ts(nmr[:], mv[:, 0:1], rstd[:, 0:1], -1.0, ALU.mult, ALU.mult), reads=[W["b_mv"], W["b_rstd1"]], writes=[W["b_nmr"]])
        S.op("act", I_act(t[:], t[:], AF.Identity, scale=rstd[:, 0:1], bias=nmr[:, 0:1]),
             reads=[W["b_t"], W["b_rstd1"], W["b_nmr"]], writes=[W["b_t"]])
        S.op("dve", I_tt(t[:], t[:], lng, ALU.mult), reads=[W["b_t"], W["b_ln"]], writes=[W["b_t"]])
        S.op("pool", I_tt(out, t[:], lnb, ALU.add), reads=[W["b_t"], W["b_ln"]], writes=[b_out])

    def phase_A(self, l, last):
        nc, S, ext = self.nc, self.S, self.ext
        gqa = (l % 2 == 0)
        PA, PB, PC, PD = self.PA, self.PB, self.PC, self.PD
        PAb = PA[:].bitcast(BF16)
        SCALE = 128 ** -0.5 if gqa else 192 ** -0.5
        NKH = 2 if gqa else 8
        with ExitStack() as es:
            T = lambda n, sh, dt: es.enter_context(nc.sbuf_tensor(f"{n}_L{l}", sh, dt))
            wo = T("a_wo", [128, 8, D], BF16)
            b_w = Buf()
            S.dma("pool", I_dma(wo[:], ext[f"wo{l}"].rearrange("(h p) c -> p h c", p=128)), writes=[b_w])
            if gqa:
                wqkv = T("a_wqkv", [128, KC, 1536], BF16)
                for hf in range(2):
                    S.dma("pool", I_dma(wqkv[:, :, hf * 768:(hf + 1) * 768],
                                        ext[f"wqkv{l}"].rearrange("(k p) c -> p k c", p=128)[:, :, hf * 768:(hf + 1) * 768]), writes=[b_w])
                qg = T("a_qg", [128, 128], F32); kg = T("a_kg", [128, 128], F32)
                S.dma("sp", I_dma(qg[:], ext[f"qg{l}"][0, :].partition_broadcast(128)), writes=[b_w])
                S.dma("sp", I_dma(kg[:], ext[f"kg{l}"][0, :].partition_broadcast(128)), writes=[b_w])
                cosT = T("a_cos", [128, NT, 64], F32); sinT = T("a_sin", [128, NT, 64], F32)
                S.dma("sp", I_dma(cosT[:], ext["cosg"]), writes=[b_w])
                S.dma("sp", I_dma(sinT[:], ext["sing"]), writes=[b_w])
            else:
                wdq = T("a_wdq", [128, KC, 768], BF16)
                wuq = T("a_wuq", [128, 6, 1536], BF16)
                wdkv = T("a_wdkv", [128, KC, 320], BF16)
                wukv = T("a_wukv", [128, 2, 2048], BF16)
                S.dma("pool", I_dma(wdq[:], ext[f"wdq{l}"].rearrange("(k p) c -> p k c", p=128)), writes=[b_w])
                for hf in range(2):
                    S.dma("pool", I_dma(wuq[:, :, hf * 768:(hf + 1) * 768],
                                        ext[f"wuq{l}"].rearrange("(k p) c -> p k c", p=128)[:, :, hf * 768:(hf + 1) * 768]), writes=[b_w])
                S.dma("pool", I_dma(wdkv[:], ext[f"wdkv{l}"].rearrange("(k p) c -> p k c", p=128)), writes=[b_w])
                for hf in range(2):
                    S.dma("pool", I_dma(wukv[:, :, hf * 1024:(hf + 1) * 1024],
                                        ext[f"wukv{l}"].rearrange("(k p) c -> p k c", p=128)[:, :, hf * 1024:(hf + 1) * 1024]), writes=[b_w])
                mqg = T("a_mqg", [128, 768], F32); mkvg = T("a_mkvg", [128, 256], F32)
                S.dma("sp", I_dma(mqg[:], ext[f"mqg{l}"][0, :].partition_broadcast(128)), writes=[b_w])
                S.dma("sp", I_dma(mkvg[:], ext[f"mkvg{l}"][0, :].partition_broadcast(128)), writes=[b_w])
                cosT = T("a_cos", [128, NT, 32], F32); sinT = T("a_sin", [128, NT, 32], F32)
                S.dma("sp", I_dma(cosT[:], ext["cosm"]), writes=[b_w])
                S.dma("sp", I_dma(sinT[:], ext["sinm"]), writes=[b_w])
            lng = T("a_lng", [128, D], F32); lnb = T("a_lnb", [128, D], F32)
            wr = T("a_wr", [128, KC, E], F32)
            S.dma("sp", I_dma(lng[:], ext[f"lnmg{l}"][0, :].partition_broadcast(128)), writes=[b_w])
            S.dma("sp", I_dma(lnb[:], ext[f"lnmb{l}"][0, :].partition_broadcast(128)), writes=[b_w])
            S.dma("sp", I_dma(wr[:], ext[f"rw{l}"].rearrange("(k p) e -> p k e", p=128)), writes=[b_w])
            G1c = T("a_G1c", [128, D], F32); G1l = T("a_G1l", [128, D], F32)
            b_G1c, b_G1l = bufs(2)
            S.dma("sp", I_dma(G1c[:], self.GS[0, NS]), reads=[self.bGS], writes=[b_G1c])
            W = {}
            for n, sh in (("sq", [128, 1024]), ("ss", [128, 8]), ("lnv", [128, 8]), ("rstd", [128, 8]),
                          ("t", [128, D]), ("st", [128, 2, 6]), ("mv", [128, 2]), ("lnv1", [128, 1]),
                          ("rstd1", [128, 1]), ("nmr", [128, 1]), ("epsc", [128, 1])):
                W[n] = T("a_" + n, sh, F32)
            for n in ("sq", "ss", "lnv", "rstd", "t", "st", "mv", "lnv1", "rstd1", "nmr"):
                W["b_" + n] = Buf()
            W["b_ln"] = b_w
            W["b_eps"] = Buf()
            S.op("pool", I_memset(W["epsc"][:], EPS), writes=[W["b_eps"]])
            hTt = T("a_hTt", [128, KC, 128], BF16); b_hTt = bufs(KC)
            pf = T("a_pf", [128, 1024], F32); b_pf = Buf()
            pn = W["sq"]; b_pn = W["b_sq"]
            pr = T("a_pr", [128, 1024], BF16); b_pr = Buf()
            vt = T("a_vt", [128, 1024], BF16); b_vt = Buf()
            ktt = T("a_ktt", [128, 9, 128], BF16); b_ktt = Buf()
            xblk = T("a_xblk", [128, 4, D], F32); b_xblk = bufs(4)
            xin = [xblk[:, 0, :], xblk[:, 1, :]]; b_xin = b_xblk[0:2]
            qTb = T("a_qTb", [128, 8, 512], BF16); b_qTb = Buf()
            oTraw = T("a_oTb", [128, 8 * 512], BF16); b_oTb = Buf()
            oTb = oTraw[:].rearrange("p (h n) -> p h n", n=512)
            rcrd = oTraw[:, 0:2048].bitcast(F32)
            kTg = [T(f"a_kTg{i}", [128, NR], BF16) for i in range(2)]; b_kTg = bufs(2)
            Vg = [T(f"a_Vg{i}", [128, NT, 128], BF16) for i in range(2)]; b_Vg = bufs(2)
            QS = 512 if gqa else 256
            PT = [T(f"a_PT{i}", [128, NT, QS], BF16) for i in range(2)]; b_PT = [bufs(NT) for _ in range(2)]
            rden = T("a_rden", [128, 512], F32); b_rden = Buf()
            h2T = pf[:].rearrange("p (k n) -> p k n", n=128); b_h2T = b_pf
            W["ra"] = W["t"][:, 0:512]; W["rb"] = W["t"][:, 512:1024]; W["b_ra"] = W["b_rb"] = W["b_t"]
            W["rc"] = rcrd[:, 0:512]; W["rd"] = rcrd[:, 512:1024]; W["b_rc"] = W["b_rd"] = b_oTb
            sm = {n: T("a_sm_" + n, sh, F32) for n, sh in (("mx", [128, 1]), ("nmx", [128, 1]), ("ex", [128, E]),
                                                           ("ssum", [128, 1]), ("rs", [128, 1]), ("aff", [128, E]))}
            b_sm = {n: Buf() for n in sm}
            affT = T("a_affT", [16, 128], F32); b_affT = Buf()
            if not gqa:
                cqT = T("a_cqT", [128, 6, 512], BF16); b_cqT = Buf()
                ckT = T("a_ckT", [128, 2, 128], BF16); b_ckT = Buf()
                qrT = T("a_qrT", [128, 8, 512], BF16); b_qrT = Buf()
                krT = T("a_krT", [128, NR], BF16); b_krT = Buf()
                kr2 = T("a_kr2", [128, 128], BF16); b_kr2 = Buf()
                qr2 = T("a_qr2", [128, 1024], BF16); b_qr2 = Buf()
                S.op("pool", I_memset(qr2[:], 0.0), writes=[b_qr2])
            kvload = 0
            x1cnt = 0
            self.stop("w")
            for s in range(NS):
                par = s % 2
                S.dma("sp", I_dma(G1l[:], self.GS[0, s]), reads=[self.bGS], writes=[b_G1l])
                for t in range(NT):
                    j = NS if t < 2 else s
                    xi = xin[t % 2]; bx = b_xin[t % 2]
                    r0 = s * NR + t * 128
                    S.dma("sp", I_dma(xi, self.XA[r0:r0 + 128, :]), reads=[self.bXA[s]], writes=[bx])
                    self.transpose_mod(xi, 1, 0, j, hTt, bx, b_hTt)
                    cos_t = cosT[:, t, :]; sin_t = sinT[:, t, :]
                    if gqa:
                        for k in range(KC):
                            S.op("pe", I_mm(PB[:, 0:512], hTt[:, k, :], wqkv[:, k, 1024:1536], k == 0, k == KC - 1),
                                 reads=[b_hTt[k], b_w], writes=self.bPB)
                        S.op("act", I_acopy(pf[:, 0:256], PB[:, 0:256]), reads=self.bPB, writes=[b_pf])
                        S.op("act", I_acopy(vt[:, 0:256], PB[:, 256:512]), reads=self.bPB, writes=[b_vt])
                        self.rms_rstd(W, pf, 2, 128, b_pf)
                        pf3 = pf[:, 0:256].rearrange("p (h d) -> p h d", d=128)
                        pn3 = pn[:, 0:256].rearrange("p (h d) -> p h d", d=128)
                        S.op("dve", I_tt(pn3, pf3, W["rstd"][:, 0:2].unsqueeze(2).to_broadcast([128, 2, 128]), ALU.mult),
                             reads=[b_pf, W["b_rstd"]], writes=[b_pn])
                        S.op("pool", I_tt(pn3, pn3, kg[:].unsqueeze(1).to_broadcast([128, 2, 128]), ALU.mult),
                             reads=[b_pn, b_w], writes=[b_pn])
                        pr3 = pr[:, 0:256].rearrange("p (h d) -> p h d", d=128)
                        self.rope(W, pn3, 2, 128, cos_t, sin_t, pr3, b_pn, b_pr, b_w)
                        for g in range(2):
                            S.op("pe", I_tr(PAb[:, g * 128:(g + 1) * 128], pr[:, g * 128:(g + 1) * 128], self.identB[:]),
                                 reads=[b_pr, self.bconst], writes=self.bPA)
                        S.op("dve", I_copy(ktt[:, 0:2, :], PAb[:, 0:256].rearrange("p (g n) -> p g n", n=128)),
                             reads=self.bPA, writes=[b_ktt])
                        S.dma("sp", I_dma(self.KT[par, 0:2, :, t * 128:(t + 1) * 128].rearrange("g d n -> d g n"), ktt[:, 0:2, :]),
                              reads=[b_ktt], cowrites=[self.bKT[par]])
                        S.dma("sp", I_dma(self.VS[par, 0:2, :, t, :].rearrange("g p d -> p g d"),
                                          vt[:, 0:256].rearrange("p (g d) -> p g d", d=128)),
                              reads=[b_vt], cowrites=[self.bVS[par]])
                    else:
                        for k in range(KC):
                            S.op("pe", I_mm(PB[:, 0:320], hTt[:, k, :], wdkv[:, k, :], k == 0, k == KC - 1),
                                 reads=[b_hTt[k], b_w], writes=self.bPB)
                        S.op("act", I_acopy(pf[:, 0:320], PB[:, 0:320]), reads=self.bPB, writes=[b_pf])
                        self.stop("p1a")
                        self.rms_rstd(W, pf, 1, 256, b_pf)
                        S.op("dve", I_ts(pn[:, 0:256], pf[:, 0:256], W["rstd"][:, 0:1], None, ALU.mult),
                             reads=[b_pf, W["b_rstd"]], writes=[b_pn])
                        S.op("pool", I_tt(pr[:, 0:256], pn[:, 0:256], mkvg[:], ALU.mult), reads=[b_pn, b_w], writes=[b_pr])
                        kr3 = kr2[:, 0:64].rearrange("p (h d) -> p h d", d=64)
                        self.rope(W, pf[:, 256:320].rearrange("p (h d) -> p h d", d=64), 1, 64, cos_t, sin_t, kr3, b_pf, b_kr2, b_w)
                        S.op("pool", I_copy(kr2[:, 64:128], kr2[:, 0:64]), reads=[b_kr2], writes=[b_kr2])
                        self.stop("p1b")
                        for c in range(2):
                            S.op("pe", I_tr(PAb[:, c * 128:(c + 1) * 128], pr[:, c * 128:(c + 1) * 128], self.identB[:]),
                                 reads=[b_pr, self.bconst], writes=self.bPA)
                        S.op("pe", I_tr(PAb[:, 256:384], kr2[:], self.identB[:]), reads=[b_kr2, self.bconst], writes=self.bPA)
                        S.op("dve", I_copy(ckT[:], PAb[:, 0:256].rearrange("p (c n) -> p c n", n=128)), reads=self.bPA, writes=[b_ckT])
                        S.op("dve", I_copy(krT[:, t * 128:(t + 1) * 128], PAb[:, 256:384]), reads=self.bPA, writes=[b_krT])
                        self.stop("p1c")
                        wukv4 = wukv[:].rearrange("p c (h x) -> p c h x", x=256)
                        for hh in range(8):
                            for c in range(2):
                                S.op("pe", I_mm(PC[:, hh * 128:(hh + 1) * 128], wukv4[:, c, hh, 0:128], ckT[:, c, :], c == 0, c == 1),
                                     reads=[b_ckT, b_w], writes=[self.bPC[hh // 4]])
                        S.op("dve", I_copy(ktt[:, 0:8, :], PC[:].rearrange("p (h n) -> p h n", n=128)),
                             reads=self.bPC, writes=[b_ktt])
                        self.stop("p1d")
                        for half in range(2):
                            for c in range(2):
                                S.op("pe", I_mm(PB[:, half * 512:(half + 1) * 512], ckT[:, c, :],
                                                wukv4[:, c, half * 4:(half + 1) * 4, 128:256], c == 0, c == 1),
                                     reads=[b_ckT, b_w], writes=self.bPB)
                        S.op("act", I_acopy(vt[:], PB[:]), reads=self.bPB, writes=[b_vt])
                        self.stop("p1e")
                        S.dma("sp", I_dma(self.KT[par, 0:8, :, t * 128:(t + 1) * 128].rearrange("g d n -> d g n"), ktt[:, 0:8, :]),
                              reads=[b_ktt], cowrites=[self.bKT[par]])
                        S.dma("sp", I_dma(self.VS[par, :, :, t, :].rearrange("g p d -> p g d"),
                                          vt[:].rearrange("p (g d) -> p g d", d=128)),
                              reads=[b_vt], cowrites=[self.bVS[par]])
                self.stop("p1")
                blocks = ([] if last else [[0, 1]]) + [[2 + 4 * b + i for i in range(4)] for b in range(4)]
                for tiles in blocks:
                    isctx = tiles[0] < 2
                    j = NS if isctx else s
                    nq = len(tiles) * 128
                    nkc = 2 if isctx else NT
                    G1 = G1c if isctx else G1l
                    bG1 = b_G1c if isctx else b_G1l
                    for ti, t in enumerate(tiles):
                        r0 = s * NR + t * 128
                        S.dma("sp", I_dma(xblk[:, ti, :], self.XA[r0:r0 + 128, :]), reads=[self.bXA[s]], writes=[b_xblk[ti]])
                        self.transpose_mod(xblk[:, ti, :], 1, 0, j, hTt, b_xblk[ti], b_hTt)
                        cos_t = cosT[:, t, :]; sin_t = sinT[:, t, :]
                        if gqa:
                            for half in range(2):
                                for k in range(KC):
                                    S.op("pe", I_mm(PB[:, half * 512:(half + 1) * 512], hTt[:, k, :],
                                                    wqkv[:, k, half * 512:(half + 1) * 512], k == 0, k == KC - 1),
                                         reads=[b_hTt[k], b_w], writes=self.bPB)
                            S.op("act", I_acopy(pf[:], PB[:]), reads=self.bPB, writes=[b_pf])
                            self.rms_rstd(W, pf, 8, 128, b_pf)
                            pf3 = pf[:].rearrange("p (h d) -> p h d", d=128)
                            pn3 = pn[:].rearrange("p (h d) -> p h d", d=128)
                            S.op("dve", I_tt(pn3, pf3, W["rstd"][:, 0:8].unsqueeze(2).to_broadcast([128, 8, 128]), ALU.mult),
                                 reads=[b_pf, W["b_rstd"]], writes=[b_pn])
                            S.op("pool", I_tt(pn3, pn3, qg[:].unsqueeze(1).to_broadcast([128, 8, 128]), ALU.mult),
                                 reads=[b_pn, b_w], writes=[b_pn])
                            pr3 = pr[:].rearrange("p (h d) -> p h d", d=128)
                            self.rope(W, pn3, 8, 128, cos_t, sin_t, pr3, b_pn, b_pr, b_w)
                            for hh in range(8):
                                S.op("pe", I_tr(PAb[:, hh * 128:(hh + 1) * 128], pr[:, hh * 128:(hh + 1) * 128], self.identB[:]),
                                     reads=[b_pr, self.bconst], writes=self.bPA)
                            S.op("dve", I_copy(qTb[:, :, ti * 128:(ti + 1) * 128], PAb[:, 0:1024].rearrange("p (h n) -> p h n", n=128)),
                                 reads=self.bPA, writes=[b_qTb])
                        else:
                            for c0, c1 in ((0, 512), (512, 768)):
                                for k in range(KC):
                                    S.op("pe", I_mm(PB[:, c0:c1], hTt[:, k, :], wdq[:, k, c0:c1], k == 0, k == KC - 1),
                                         reads=[b_hTt[k], b_w], writes=self.bPB)
                            S.op("act", I_acopy(pf[:, 0:768], PB[:, 0:768]), reads=self.bPB, writes=[b_pf])
                            self.rms_rstd(W, pf, 1, 768, b_pf)
                            S.op("dve", I_ts(pn[:, 0:768], pf[:, 0:768], W["rstd"][:, 0:1], None, ALU.mult),
                                 reads=[b_pf, W["b_rstd"]], writes=[b_pn])
                            S.op("pool", I_tt(pr[:, 0:768], pn[:, 0:768], mqg[:], ALU.mult), reads=[b_pn, b_w], writes=[b_pr])
                            for c in range(6):
                                S.op("pe", I_tr(PAb[:, c * 128:(c + 1) * 128], pr[:, c * 128:(c + 1) * 128], self.identB[:]),
                                     reads=[b_pr, self.bconst], writes=self.bPA)
                            S.op("dve", I_copy(cqT[:, :, ti * 128:(ti + 1) * 128], PAb[:, 0:768].rearrange("p (c n) -> p c n", n=128)),
                                 reads=self.bPA, writes=[b_cqT])
                            wuq4 = wuq[:].rearrange("p c (h x) -> p c h x", x=192)
                            for c in range(6):
                                S.op("pe", I_mm(PB[:, 0:512], cqT[:, c, ti * 128:(ti + 1) * 128], wuq4[:, c, :, 128:192], c == 0, c == 5),
                                     reads=[b_cqT, b_w], writes=self.bPB)
                            S.op("act", I_acopy(pf[:, 0:512], PB[:, 0:512]), reads=self.bPB, writes=[b_pf])
                            src8 = pf[:, 0:512].rearrange("p (h d) -> p h d", d=64)
                            dst4 = qr2[:].rearrange("p (c x) -> p c x", x=256)
                            self.rope(W, src8[:, 0::2, :], 4, 64, cos_t, sin_t, dst4[:, :, 0:64], b_pf, b_qr2, b_w)
                            self.rope(W, src8[:, 1::2, :], 4, 64, cos_t, sin_t, dst4[:, :, 192:256], b_pf, b_qr2, b_w)
                            for c in range(8):
                                S.op("pe", I_tr(PAb[:, c * 128:(c + 1) * 128], qr2[:, c * 128:(c + 1) * 128], self.identB[:]),
                                     reads=[b_qr2, self.bconst], writes=self.bPA)
                            S.op("dve", I_copy(qrT[:, :, ti * 128:(ti + 1) * 128], PAb[:, 0:1024].rearrange("p (c n) -> p c n", n=128)),
                                 reads=self.bPA, writes=[b_qrT])
                    if not gqa:
                        wuq4 = wuq[:].rearrange("p c (h x) -> p c h x", x=192)
                        for hh in range(8):
                            pc = PC[:, (hh % 2) * 512:(hh % 2) * 512 + nq]
                            for c in range(6):
                                S.op("pe", I_mm(pc, wuq4[:, c, hh, 0:128], cqT[:, c, 0:nq], c == 0, c == 5),
                                     reads=[b_cqT, b_w], writes=[self.bPC[hh % 2]])
                            S.op("act", I_acopy(qTb[:, hh, 0:nq], pc), reads=[self.bPC[hh % 2]], writes=[b_qTb])
                    self.stop("q")
                    nk = nkc * 128
                    kvslot = {}

                    def load_kv(g):
                        nonlocal kvload
                        kb = kvload % 2; kvload += 1
                        kvslot[g] = kb
                        S.dma("sp", I_dma(kTg[kb][:, 0:nk], self.KT[par, g, :, 0:nk]), reads=[self.bKT[par]], writes=[b_kTg[kb]])
                        S.dma("sp", I_dma(Vg[kb][:, 0:nkc, :], self.VS[par, g, :, 0:nkc, :]), reads=[self.bVS[par]], writes=[b_Vg[kb]])

                    units = []
                    for g in range(NKH):
                        for hh in ([g * 4 + i for i in range(4)] if gqa else [g]):
                            for q0 in range(0, nq, QS):
                                units.append((g, hh, q0, min(QS, nq - q0)))

                    def emit_S(u, i, kc):
                        g, hh, q0, qn = u
                        kb = kvslot[g]
                        pc = PC[:, (kc % 2) * 512:(kc % 2) * 512 + qn]
                        if gqa:
                            S.op("pe", I_mm(pc, kTg[kb][:, kc * 128:(kc + 1) * 128], qTb[:, hh, q0:q0 + qn], True, True),
                                 reads=[b_kTg[kb], b_qTb], writes=[self.bPC[kc % 2]])
                        else:
                            S.op("pe", I_mm(pc, kTg[kb][:, kc * 128:(kc + 1) * 128], qTb[:, hh, q0:q0 + qn], True, False),
                                 reads=[b_kTg[kb], b_qTb], writes=[self.bPC[kc % 2]])
                            S.op("pe", I_mm(pc, krT[:, kc * 128:(kc + 1) * 128], qrT[:, hh, q0:q0 + qn], False, True),
                                 reads=[b_krT, b_qrT], writes=[self.bPC[kc % 2]])
                        S.op("act", I_act(PT[i % 2][:, kc, 0:qn], pc, AF.Exp, scale=SCALE), reads=[self.bPC[kc % 2]],
                             writes=[b_PT[i % 2][kc]])

                    def emit_PV(u, i, kc):
                        g, hh, q0, qn = u
                        kb = kvslot[g]
                        acc, bacc = (PD, self.bPD) if i % 2 == 0 else (PB, self.bPB)
                        S.op("pe", I_mm(acc[:, 0:qn], Vg[kb][:, kc, :], PT[i % 2][:, kc, 0:qn], kc == 0, kc == nkc - 1),
                             reads=[b_Vg[kb], b_PT[i % 2][kc]], writes=[bacc[0]])
                        S.op("pe", I_mm(acc[:, 512:512 + qn], self.onesB[:], PT[i % 2][:, kc, 0:qn], kc == 0, kc == nkc - 1),
                             reads=[self.bconst, b_PT[i % 2][kc]], writes=[bacc[1]])

                    def emit_norm(u, i):
                        g, hh, q0, qn = u
                        acc, bacc = (PD, self.bPD) if i % 2 == 0 else (PB, self.bPB)
                        S.op("dve", lambda h: h.reciprocal(out=rden[:, 0:qn], in_=acc[:, 512:512 + qn]),
                             reads=[bacc[1]], writes=[b_rden])
                        S.op("dve", I_tt(oTb[:, hh, q0:q0 + qn], acc[:, 0:qn], rden[:, 0:qn], ALU.mult),
                             reads=[bacc[0], b_rden], writes=[b_oTb])

                    load_kv(0)
                    for kc in range(nkc):
                        emit_S(units[0], 0, kc)
                    for i, u in enumerate(units):
                        nxt = units[i + 1] if i + 1 < len(units) else None
                        if (i == 0 or units[i - 1][0] != u[0]) and u[0] + 1 < NKH:
                            load_kv(u[0] + 1)
                        for kc in range(nkc):
                            if nxt is not None:
                                emit_S(nxt, i + 1, kc)
                            emit_PV(u, i, kc)
                        emit_norm(u, i)
                    self.stop("att")
                    for ti, t in enumerate(tiles):
                        r0 = s * NR + t * 128
                        for half in range(2):
                            for hh in range(8):
                                S.op("pe", I_mm(PB[:, half * 512:(half + 1) * 512], oTb[:, hh, ti * 128:(ti + 1) * 128],
                                                wo[:, hh, half * 512:(half + 1) * 512], hh == 0, hh == 7),
                                     reads=[b_oTb, b_w], writes=self.bPB)
                        xo = xblk[:, ti, :]; bxo = b_xblk[ti]
                        self.resid_ln(W, PB[:], xblk[:, ti, :], G1[:], lng[:], lnb[:], xo,
                                      self.bPB + [b_xblk[ti], bG1], bxo)
                        S.dma("sp", I_dma(self.XB[r0:r0 + 128, :], xo), reads=[bxo], cowrites=[self.bXB[s]])
                        S.dma("sp", I_dma(self.FA[r0:r0 + 128, :], self.zeros[:]), reads=[self.bconst],
                              cowrites=[self.bFAc if isctx else self.bFAl[s]])
                        self.transpose_mod(xo, 3, 2, j, h2T, bxo, b_h2T, split=False)
                        for k in range(KC):
                            S.op("pe", I_mm(PB[:, 0:E], h2T[:, k, :], wr[:, k, :], k == 0, k == KC - 1),
                                 reads=[b_h2T, b_w], writes=self.bPB)
                        S.op("dve", lambda h: h.tensor_reduce(out=sm["mx"][:], in_=PB[:, 0:E], axis=AX.X, op=ALU.max),
                             reads=self.bPB, writes=[b_sm["mx"]])
                        S.op("dve", I_ts(sm["nmx"][:], sm["mx"][:], -1.0, None, ALU.mult), reads=[b_sm["mx"]], writes=[b_sm["nmx"]])
                        S.op("act", I_act(sm["ex"][:], PB[:, 0:E], AF.Exp, scale=1.0, bias=sm["nmx"][:, 0:1], accum_out=sm["ssum"][:]),
                             reads=self.bPB + [b_sm["nmx"]], writes=[b_sm["ex"], b_sm["ssum"]])
                        S.op("dve", lambda h: h.reciprocal(out=sm["rs"][:], in_=sm["ssum"][:]), reads=[b_sm["ssum"]], writes=[b_sm["rs"]])
                        S.op("dve", I_ts(sm["aff"][:], sm["ex"][:], sm["rs"][:, 0:1], None, ALU.mult),
                             reads=[b_sm["ex"], b_sm["rs"]], writes=[b_sm["aff"]])
                        S.op("pe", I_tr(PA[0:E, 0:128], sm["aff"][:], self.identF[:]), reads=[b_sm["aff"], self.bconst], writes=self.bPA)
                        S.op("act", I_acopy(affT[:], PA[0:E, 0:128]), reads=self.bPA, writes=[b_affT])
                        if isctx:
                            dst = self.AFC[s * E:(s + 1) * E, t * 128:(t + 1) * 128]
                        else:
                            dst = self.AFL[s * E:(s + 1) * E, (t - 2) * 128:(t - 1) * 128]
                        S.dma("sp", I_dma(dst, affT[:]), reads=[b_affT], cowrites=[self.bAF])
            S.run_block()

    def phase_A1(self, l):
        from itertools import zip_longest
        nc, S, ext = self.nc, self.S, self.ext
        gqa = (l % 2 == 0)
        PA, PB, PC, PD = self.PA, self.PB, self.PC, self.PD
        PAb = PA[:].bitcast(BF16)
        with ExitStack() as es:
            T = lambda n, sh, dt: es.enter_context(nc.sbuf_tensor(f"{n}_L{l}", sh, dt))
            b_w = Buf()
            if gqa:
                wqkv = T("p_wqkv", [128, KC, 1536], BF16)
                for hf in range(2):
                    S.dma("pool", I_dma(wqkv[:, :, hf * 768:(hf + 1) * 768],
                                        ext[f"wqkv{l}"].rearrange("(k p) c -> p k c", p=128)[:, :, hf * 768:(hf + 1) * 768]), writes=[b_w])
                qkg = T("p_qkg", [128, 10, 128], F32)
                for hh in range(10):
                    src = ext[f"qg{l}"] if hh < 8 else ext[f"kg{l}"]
                    S.dma("sp", I_dma(qkg[:, hh, :], src[0, :].partition_broadcast(128)), writes=[b_w])
                cosT = T("p_cos", [128, NT, 64], F32); sinT = T("p_sin", [128, NT, 64], F32)
                S.dma("sp", I_dma(cosT[:], ext["cosg"]), writes=[b_w]); S.dma("sp", I_dma(sinT[:], ext["sing"]), writes=[b_w])
            else:
                wdq = T("p_wdq", [128, KC, 768], BF16); wuq = T("p_wuq", [128, 6, 1536], BF16)
                wdkv = T("p_wdkv", [128, KC, 320], BF16); wukv = T("p_wukv", [128, 2, 2048], BF16)
                S.dma("pool", I_dma(wdq[:], ext[f"wdq{l}"].rearrange("(k p) c -> p k c", p=128)), writes=[b_w])
                for hf in range(2):
                    S.dma("pool", I_dma(wuq[:, :, hf * 768:(hf + 1) * 768],
                                        ext[f"wuq{l}"].rearrange("(k p) c -> p k c", p=128)[:, :, hf * 768:(hf + 1) * 768]), writes=[b_w])
                S.dma("pool", I_dma(wdkv[:], ext[f"wdkv{l}"].rearrange("(k p) c -> p k c", p=128)), writes=[b_w])
                for hf in range(2):
                    S.dma("pool", I_dma(wukv[:, :, hf * 1024:(hf + 1) * 1024],
                                        ext[f"wukv{l}"].rearrange("(k p) c -> p k c", p=128)[:, :, hf * 1024:(hf + 1) * 1024]), writes=[b_w])
                mqg = T("p_mqg", [128, 768], F32); mkvg = T("p_mkvg", [128, 256], F32)
                S.dma("sp", I_dma(mqg[:], ext[f"mqg{l}"][0, :].partition_broadcast(128)), writes=[b_w])
                S.dma("sp", I_dma(mkvg[:], ext[f"mkvg{l}"][0, :].partition_broadcast(128)), writes=[b_w])
                cosT = T("p_cos", [128, NT, 32], F32); sinT = T("p_sin", [128, NT, 32], F32)
                S.dma("sp", I_dma(cosT[:], ext["cosm"]), writes=[b_w]); S.dma("sp", I_dma(sinT[:], ext["sinm"]), writes=[b_w])
                wukv4 = wukv[:].rearrange("p c (h x) -> p c h x", x=256)
                wuq4 = wuq[:].rearrange("p c (h x) -> p c h x", x=192)
            epsc = T("p_eps", [128, 1], F32); b_eps = Buf()
            S.op("pool", I_memset(epsc[:], EPS), writes=[b_eps])

            def mkW(tag, n):
                W = {"epsc": epsc, "b_eps": b_eps}
                for nm, sh in (("sq", [128, n]), ("ss", [128, 16]), ("lnv", [128, 16]), ("rstd", [128, 16]),
                               ("ra", [128, n // 2]), ("rb", [128, n // 2]), ("rc", [128, n // 2]), ("rd", [128, n // 2])):
                    W[nm] = T(f"p_{tag}_{nm}", sh, F32)
                    W["b_" + nm] = Buf()
                return W

            NSLOT = 4 if gqa else 3
            slots = []
            for i in range(NSLOT):
                d = {"xin": T(f"p_xin{i}", [128, D], F32), "b_xin": Buf(),
                     "hTt": T(f"p_hTt{i}", [128, KC, 128], BF16), "b_hTt": bufs(KC),
                     "vt": T(f"p_vt{i}", [128, 1024], BF16), "b_vt": Buf(),
                     "qtt": T(f"p_qtt{i}", [128, 8, 128], BF16), "b_qtt": Buf(),
                     "ktt": T(f"p_ktt{i}", [128, 9, 128], BF16), "b_ktt": Buf()}
                if gqa:
                    d["W"] = mkW(f"w{i}", 1280)
                    d["pf"] = T(f"p_pf{i}", [128, 1280], F32); d["b_pf"] = Buf()
                    d["pr"] = T(f"p_pr{i}", [128, 1280], BF16); d["b_pr"] = Buf()
                else:
                    d["Wk"] = mkW(f"wk{i}", 256); d["Wq"] = mkW(f"wq{i}", 768)
                    d["pfk"] = T(f"p_pfk{i}", [128, 320], F32); d["b_pfk"] = Buf()
                    d["prk"] = T(f"p_prk{i}", [128, 256], BF16); d["b_prk"] = Buf()
                    d["kr2"] = T(f"p_kr2{i}", [128, 128], BF16); d["b_kr2"] = Buf()
                    d["ckT"] = T(f"p_ckT{i}", [128, 2, 128], BF16); d["b_ckT"] = Buf()
                    d["pfq"] = T(f"p_pfq{i}", [128, 768], F32); d["b_pfq"] = Buf()
                    d["prq"] = T(f"p_prq{i}", [128, 768], BF16); d["b_prq"] = Buf()
                    d["cqT"] = T(f"p_cqT{i}", [128, 6, 128], BF16); d["b_cqT"] = Buf()
                    d["pfr"] = T(f"p_pfr{i}", [128, 512], F32); d["b_pfr"] = Buf()
                    d["qr2"] = T(f"p_qr2{i}", [128, 1024], BF16); d["b_qr2"] = Buf()
                    d["qrt"] = T(f"p_qrt{i}", [128, 8, 128], BF16); d["b_qrt"] = Buf()
                    S.op("pool", I_memset(d["qr2"][:], 0.0), writes=[d["b_qr2"]])
                slots.append(d)

            def chain_gqa(s, t, d):
                j = NS if t < 2 else s
                r0 = s * NR + t * 128
                W = d["W"]; pf = d["pf"]; pr = d["pr"]; sq = W["sq"]
                S.dma("sp", I_dma(d["xin"][:], self.XA[r0:r0 + 128, :]), reads=[self.bXA[s]], writes=[d["b_xin"]])
                self.transpose_mod(d["xin"], 1, 0, j, d["hTt"], d["b_xin"], d["b_hTt"])
                yield
                for half in range(2):
                    for k in range(KC):
                        S.op("pe", I_mm(PB[:, half * 512:(half + 1) * 512], d["hTt"][:, k, :], wqkv[:, k, half * 512:(half + 1) * 512],
                                        k == 0, k == KC - 1), reads=[d["b_hTt"][k], b_w], writes=[self.bPB[half]])
                for k in range(KC):
                    S.op("pe", I_mm(PC[:, 0:512], d["hTt"][:, k, :], wqkv[:, k, 1024:1536], k == 0, k == KC - 1),
                         reads=[d["b_hTt"][k], b_w], writes=[self.bPC[0]])
                S.op("act", I_acopy(pf[:, 0:1024], PB[:]), reads=self.bPB, writes=[d["b_pf"]])
                S.op("act", I_acopy(pf[:, 1024:1280], PC[:, 0:256]), reads=[self.bPC[0]], writes=[d["b_pf"]])
                S.op("act", I_acopy(d["vt"][:, 0:256], PC[:, 256:512]), reads=[self.bPC[0]], writes=[d["b_vt"]])
                yield
                self.rms_rstd(W, pf, 10, 128, d["b_pf"])
                yield
                pf3 = pf[:].rearrange("p (h d) -> p h d", d=128); pn3 = sq[:].rearrange("p (h d) -> p h d", d=128)
                S.op("dve", I_tt(pn3, pf3, W["rstd"][:, 0:10].unsqueeze(2).to_broadcast([128, 10, 128]), ALU.mult),
                     reads=[d["b_pf"], W["b_rstd"]], writes=[W["b_sq"]])
                S.op("dve", I_tt(pn3, pn3, qkg[:], ALU.mult), reads=[W["b_sq"], b_w], writes=[W["b_sq"]])
                yield
                pr3 = pr[:].rearrange("p (h d) -> p h d", d=128)
                self.rope(W, pn3, 10, 128, cosT[:, t, :], sinT[:, t, :], pr3, W["b_sq"], d["b_pr"], b_w)
                yield
                for hh in range(10):
                    S.op("pe", I_tr(PAb[:, hh * 128:(hh + 1) * 128], pr[:, hh * 128:(hh + 1) * 128], self.identB[:]),
                         reads=[d["b_pr"], self.bconst], writes=self.bPA)
                S.op("dve", I_copy(d["qtt"][:], PAb[:, 0:1024].rearrange("p (h n) -> p h n", n=128)), reads=self.bPA, writes=[d["b_qtt"]])
                S.op("dve", I_copy(d["ktt"][:, 0:2, :], PAb[:, 1024:1280].rearrange("p (h n) -> p h n", n=128)), reads=self.bPA, writes=[d["b_ktt"]])
                yield
                cs = slice(t * 128, (t + 1) * 128)
                S.dma("sp", I_dma(self.QT[s, :, :, cs].rearrange("h d n -> d h n"), d["qtt"][:]), reads=[d["b_qtt"]], cowrites=[self.bQT[s]])
                S.dma("sp", I_dma(self.KT[s, 0:2, :, cs].rearrange("g d n -> d g n"), d["ktt"][:, 0:2, :]), reads=[d["b_ktt"]], cowrites=[self.bKT[s]])
                S.dma("sp", I_dma(self.VS[s, 0:2, :, t, :].rearrange("g p d -> p g d"), d["vt"][:, 0:256].rearrange("p (g d) -> p g d", d=128)),
                      reads=[d["b_vt"]], cowrites=[self.bVS[s]])

            def chain_mla_k(s, t, d):
                W = d["Wk"]; pf = d["pfk"]; pr = d["prk"]; kr2 = d["kr2"]; ckT = d["ckT"]; ktt = d["ktt"]; vt = d["vt"]
                PBb = PB[:, 512:1024].bitcast(BF16)
                for k in range(KC):
                    S.op("pe", I_mm(PB[:, 0:320], d["hTt"][:, k, :], wdkv[:, k, :], k == 0, k == KC - 1),
                         reads=[d["b_hTt"][k], b_w], writes=[self.bPB[0]])
                S.op("act", I_acopy(pf[:], PB[:, 0:320]), reads=[self.bPB[0]], writes=[d["b_pfk"]])
                yield
                self.rms_rstd(W, pf, 1, 256, d["b_pfk"])
                yield
                S.op("dve", I_ts(W["sq"][:, 0:256], pf[:, 0:256], W["rstd"][:, 0:1], None, ALU.mult), reads=[d["b_pfk"], W["b_rstd"]], writes=[W["b_sq"]])
                S.op("dve", I_tt(pr[:], W["sq"][:, 0:256], mkvg[:], ALU.mult), reads=[W["b_sq"], b_w], writes=[d["b_prk"]])
                yield
                kr3 = kr2[:, 0:64].rearrange("p (h d) -> p h d", d=64)
                self.rope(W, pf[:, 256:320].rearrange("p (h d) -> p h d", d=64), 1, 64, cosT[:, t, :], sinT[:, t, :], kr3, d["b_pfk"], d["b_kr2"], b_w)
                S.op("pool", I_copy(kr2[:, 64:128], kr2[:, 0:64]), reads=[d["b_kr2"]], writes=[d["b_kr2"]])
                yield
                for c in range(2):
                    S.op("pe", I_tr(PBb[:, c * 128:(c + 1) * 128], pr[:, c * 128:(c + 1) * 128], self.identB[:]),
                         reads=[d["b_prk"], self.bconst], writes=[self.bPB[1]])
                S.op("pe", I_tr(PBb[:, 256:384], kr2[:], self.identB[:]), reads=[d["b_kr2"], self.bconst], writes=[self.bPB[1]])
                S.op("dve", I_copy(ckT[:], PBb[:, 0:256].rearrange("p (c n) -> p c n", n=128)), reads=[self.bPB[1]], writes=[d["b_ckT"]])
                S.op("dve", I_copy(ktt[:, 8, :], PBb[:, 256:384]), reads=[self.bPB[1]], writes=[d["b_ktt"]])
                yield
                for hh in range(8):
                    for c in range(2):
                        S.op("pe", I_mm(PD[:, hh * 128:(hh + 1) * 128], wukv4[:, c, hh, 0:128], ckT[:, c, :], c == 0, c == 1),
                             reads=[d["b_ckT"], b_w], writes=[self.bPD[hh // 4]])
                S.op("dve", I_copy(ktt[:, 0:8, :], PD[:].rearrange("p (h n) -> p h n", n=128)), reads=self.bPD, writes=[d["b_ktt"]])
                yield
                for half in range(2):
                    for c in range(2):
                        S.op("pe", I_mm(PB[:, half * 512:(half + 1) * 512], ckT[:, c, :], wukv4[:, c, half * 4:(half + 1) * 4, 128:256], c == 0, c == 1),
                             reads=[d["b_ckT"], b_w], writes=[self.bPB[half]])
                S.op("act", I_acopy(vt[:], PB[:]), reads=self.bPB, writes=[d["b_vt"]])
                yield
                cs = slice(t * 128, (t + 1) * 128)
                S.dma("sp", I_dma(self.KT[s, :, :, cs].rearrange("g d n -> d g n"), ktt[:]), reads=[d["b_ktt"]], cowrites=[self.bKT[s]])
                S.dma("sp", I_dma(self.VS[s, :, :, t, :].rearrange("g p d -> p g d"), vt[:].rearrange("p (g d) -> p g d", d=128)),
                      reads=[d["b_vt"]], cowrites=[self.bVS[s]])

            def chain_mla_q(s, t, d):
                W = d["Wq"]; pf = d["pfq"]; pr = d["prq"]; cqT = d["cqT"]; pfr = d["pfr"]; qr2 = d["qr2"]
                for c0, c1 in ((0, 512), (512, 768)):
                    for k in range(KC):
                        S.op("pe", I_mm(PC[:, c0:c1], d["hTt"][:, k, :], wdq[:, k, c0:c1], k == 0, k == KC - 1),
                             reads=[d["b_hTt"][k], b_w], writes=[self.bPC[c0 // 512]])
                S.op("act", I_acopy(pf[:], PC[:, 0:768]), reads=self.bPC, writes=[d["b_pfq"]])
                yield
                self.rms_rstd(W, pf, 1, 768, d["b_pfq"])
                yield
                S.op("dve", I_ts(W["sq"][:, 0:768], pf[:], W["rstd"][:, 0:1], None, ALU.mult), reads=[d["b_pfq"], W["b_rstd"]], writes=[W["b_sq"]])
                S.op("dve", I_tt(pr[:], W["sq"][:, 0:768], mqg[:], ALU.mult), reads=[W["b_sq"], b_w], writes=[d["b_prq"]])
                yield
                for c in range(6):
                    S.op("pe", I_tr(PAb[:, c * 128:(c + 1) * 128], pr[:, c * 128:(c + 1) * 128], self.identB[:]),
                         reads=[d["b_prq"], self.bconst], writes=[self.bPA[0]])
                S.op("dve", I_copy(cqT[:], PAb[:, 0:768].rearrange("p (c n) -> p c n", n=128)), reads=[self.bPA[0]], writes=[d["b_cqT"]])
                yield
                for hh in range(8):
                    for c in range(6):
                        S.op("pe", I_mm(PC[:, hh * 128:(hh + 1) * 128], wuq4[:, c, hh, 0:128], cqT[:, c, :], c == 0, c == 5),
                             reads=[d["b_cqT"], b_w], writes=[self.bPC[hh // 4]])
                S.op("act", I_acopy(d["qtt"][:], PC[:].rearrange("p (h n) -> p h n", n=128)), reads=self.bPC, writes=[d["b_qtt"]])
                for c in range(6):
                    S.op("pe", I_mm(PD[:, 0:512], cqT[:, c, :], wuq4[:, c, :, 128:192], c == 0, c == 5),
                         reads=[d["b_cqT"], b_w], writes=[self.bPD[0]])
                S.op("act", I_acopy(pfr[:], PD[:, 0:512]), reads=[self.bPD[0]], writes=[d["b_pfr"]])
                yield
                src8 = pfr[:].rearrange("p (h d) -> p h d", d=64)
                dst4 = qr2[:].rearrange("p (c x) -> p c x", x=256)
                self.rope(W, src8[:, 0::2, :], 4, 64, cosT[:, t, :], sinT[:, t, :], dst4[:, :, 0:64], d["b_pfr"], d["b_qr2"], b_w)
                yield
                self.rope(W, src8[:, 1::2, :], 4, 64, cosT[:, t, :], sinT[:, t, :], dst4[:, :, 192:256], d["b_pfr"], d["b_qr2"], b_w)
                yield
                for c in range(8):
                    S.op("pe", I_tr(PAb[:, 1024 + c * 128:1024 + (c + 1) * 128], qr2[:, c * 128:(c + 1) * 128], self.identB[:]),
                         reads=[d["b_qr2"], self.bconst], writes=[self.bPA[1]])
                S.op("dve", I_copy(d["qrt"][:], PAb[:, 1024:2048].rearrange("p (c n) -> p c n", n=128)), reads=[self.bPA[1]], writes=[d["b_qrt"]])
                yield
                cs = slice(t * 128, (t + 1) * 128)
                S.dma("sp", I_dma(self.QT[s, :, :, cs].rearrange("h d n -> d h n"), d["qtt"][:]), reads=[d["b_qtt"]], cowrites=[self.bQT[s]])
                S.dma("sp", I_dma(self.QR[s, :, :, cs].rearrange("h d n -> d h n"), d["qrt"][:]), reads=[d["b_qrt"]], cowrites=[self.bQT[s]])

            def chain_mla(s, t, d):
                j = NS if t < 2 else s
                r0 = s * NR + t * 128
                S.dma("sp", I_dma(d["xin"][:], self.XA[r0:r0 + 128, :]), reads=[self.bXA[s]], writes=[d["b_xin"]])
                self.transpose_mod(d["xin"], 1, 0, j, d["hTt"], d["b_xin"], d["b_hTt"])
                yield
                for _ in zip_longest(chain_mla_k(s, t, d), chain_mla_q(s, t, d)):
                    yield

            chain = chain_gqa if gqa else chain_mla
            work = [(s, t) for s in range(NS) for t in range(NT)]
            for w0 in range(0, len(work), NSLOT):
                gens = [chain(s, t, slots[i]) for i, (s, t) in enumerate(work[w0:w0 + NSLOT])]
                for _ in zip_longest(*gens):
                    pass
            S.run_block()

    def phase_A2(self, l, last):
        nc, S, ext = self.nc, self.S, self.ext
        gqa = (l % 2 == 0)
        PA, PB, PC, PD = self.PA, self.PB, self.PC, self.PD
        SCALE = 128 ** -0.5 if gqa else 192 ** -0.5
        NKH = 2 if gqa else 8
        QS = 512
        with ExitStack() as es:
            T = lambda n, sh, dt: es.enter_context(nc.sbuf_tensor(f"{n}_L{l}", sh, dt))
            wo = T("a_wo", [128, 8, D], BF16)
            b_w = Buf()
            S.dma("pool", I_dma(wo[:], ext[f"wo{l}"].rearrange("(h p) c -> p h c", p=128)), writes=[b_w])
            lng = T("a_lng", [128, D], F32); lnb = T("a_lnb", [128, D], F32)
            wr = T("a_wr", [128, KC, E], F32)
            S.dma("sp", I_dma(lng[:], ext[f"lnmg{l}"][0, :].partition_broadcast(128)), writes=[b_w])
            S.dma("sp", I_dma(lnb[:], ext[f"lnmb{l}"][0, :].partition_broadcast(128)), writes=[b_w])
            S.dma("sp", I_dma(wr[:], ext[f"rw{l}"].rearrange("(k p) e -> p k e", p=128)), writes=[b_w])
            G1c = T("a_G1c", [128, D], F32); G1l = T("a_G1l", [128, D], F32)
            b_G1c, b_G1l = bufs(2)
            S.dma("sp", I_dma(G1c[:], self.GS[0, NS]), reads=[self.bGS], writes=[b_G1c])
            W = {}
            for n, sh in (("t", [128, D]), ("st", [128, 2, 6]), ("mv", [128, 2]), ("lnv1", [128, 1]),
                          ("rstd1", [128, 1]), ("nmr", [128, 1]), ("epsc", [128, 1])):
                W[n] = T("a_" + n, sh, F32)
            for n in ("t", "st", "mv", "lnv1", "rstd1", "nmr", "eps"):
                W["b_" + n] = Buf()
            W["b_ln"] = b_w
            S.op("pool", I_memset(W["epsc"][:], EPS), writes=[W["b_eps"]])
            xblk = T("a_xblk", [128, 4, D], F32); b_xblk = bufs(4)
            qTb = [T(f"a_qTb{i}", [128, 8, 512], BF16) for i in range(2)]; b_qTb = bufs(2)
            oTb = T("a_oTb", [128, 8, 512], BF16); b_oTb = Buf()
            kTg = [T(f"a_kTg{i}", [128, NR], BF16) for i in range(2)]; b_kTg = bufs(2)
            Vg = [T(f"a_Vg{i}", [128, NT, 128], BF16) for i in range(2)]; b_Vg = bufs(2)
            PT = [T(f"a_PT{i}", [128, NT, QS], BF16) for i in range(2)]; b_PT = [bufs(NT) for _ in range(2)]
            rden = T("a_rden", [128, 512], F32); b_rden = Buf()
            h2T = T("a_h2T", [128, KC, 128], F32); b_h2T = Buf()
            sm = {n: T("a_sm_" + n, sh, F32) for n, sh in (("mx", [128, 1]), ("nmx", [128, 1]), ("ex", [128, E]),
                                                           ("ssum", [128, 1]), ("rs", [128, 1]), ("aff", [128, E]))}
            b_sm = {n: Buf() for n in sm}
            affT = T("a_affT", [16, 128], F32); b_affT = Buf()
            if not gqa:
                qrT = [T(f"a_qrT{i}", [128, 8, 512], BF16) for i in range(2)]; b_qrT = bufs(2)
                krT = T("a_krT", [128, NR], BF16); b_krT = Buf()
            CW = []
            for i in range(4):
                d = {n: T(f"a_c{i}_{n}", sh, F32) for n, sh in (("t", [128, D]), ("st", [128, 2, 6]), ("mv", [128, 2]), ("lnv", [128, 1]),
                                                               ("rstd", [128, 1]), ("nmr", [128, 1]), ("h2T", [128, KC, 128]),
                                                               ("mx", [128, 1]), ("nmx", [128, 1]), ("ex", [128, E]), ("ssum", [128, 1]),
                                                               ("rs", [128, 1]), ("aff", [128, E]), ("affT", [16, 128]))}
                for n in ("t", "st", "mv", "lnv", "rstd", "nmr", "mx", "nmx", "ex", "ssum", "rs", "aff", "affT"):
                    d["b_" + n] = Buf()
                d["b_h2T"] = bufs(KC)
                CW.append(d)

            def chain_block(s, tiles, isctx, j, G1, bG1):
                n = len(tiles)
                info = []
                for ti, t in enumerate(tiles):
                    P, bP = (PB, self.bPB) if ti % 2 == 0 else (PD, self.bPD)
                    PT_, bPT_ = (PA, self.bPA) if ti % 2 == 0 else (PC, self.bPC)
                    info.append((ti, t, s * NR + t * 128, CW[ti], P, bP, PT_, bPT_))
                for ti, t, r0, d, P, bP, PX, bPX in info:
                    for half in range(2):
                        for hh in range(8):
                            S.op("pe", I_mm(P[:, half * 512:(half + 1) * 512], oTb[:, hh, ti * 128:(ti + 1) * 128],
                                            wo[:, hh, half * 512:(half + 1) * 512], hh == 0, hh == 7),
                                 reads=[b_oTb, b_w], writes=[bP[half]])
                    S.op("dve", I_tt(d["t"][:], P[:], G1[:], ALU.mult), reads=bP + [bG1], writes=[d["b_t"]])
                yield
                for ti, t, r0, d, P, bP, PX, bPX in info:
                    S.op("dve", I_stt(d["t"][:], xblk[:, ti, :], ALPHA, d["t"][:], ALU.mult, ALU.add), reads=[b_xblk[ti], d["b_t"]], writes=[d["b_t"]])
                yield
                for ti, t, r0, d, P, bP, PX, bPX in info:
                    S.op("dve", (lambda d: lambda h: h.bn_stats(out=d["st"][:, 0, :], in_=d["t"][:, 0:512]))(d), reads=[d["b_t"]], writes=[d["b_st"]])
                    S.op("dve", (lambda d: lambda h: h.bn_stats(out=d["st"][:, 1, :], in_=d["t"][:, 512:1024]))(d), reads=[d["b_t"]], writes=[d["b_st"]])
                yield
                for ti, t, r0, d, P, bP, PX, bPX in info:
                    S.op("dve", (lambda d: lambda h: h.bn_aggr(out=d["mv"][:], in_=d["st"][:].rearrange("p a b -> p (a b)")))(d), reads=[d["b_st"]], writes=[d["b_mv"]])
                yield
                for ti, t, r0, d, P, bP, PX, bPX in info:
                    S.op("act", I_act(d["lnv"][:], d["mv"][:, 1:2], AF.Ln, scale=1.0, bias=W["epsc"][:, 0:1]), reads=[d["b_mv"], W["b_eps"]], writes=[d["b_lnv"]])
                yield
                for ti, t, r0, d, P, bP, PX, bPX in info:
                    S.op("act", I_act(d["rstd"][:], d["lnv"][:], AF.Exp, scale=-0.5), reads=[d["b_lnv"]], writes=[d["b_rstd"]])
                yield
                for ti, t, r0, d, P, bP, PX, bPX in info:
                    S.op("dve", I_ts(d["nmr"][:], d["mv"][:, 0:1], d["rstd"][:, 0:1], -1.0, ALU.mult, ALU.mult), reads=[d["b_mv"], d["b_rstd"]], writes=[d["b_nmr"]])
                yield
                for ti, t, r0, d, P, bP, PX, bPX in info:
                    S.op("act", I_act(d["t"][:], d["t"][:], AF.Identity, scale=d["rstd"][:, 0:1], bias=d["nmr"][:, 0:1]),
                         reads=[d["b_t"], d["b_rstd"], d["b_nmr"]], writes=[d["b_t"]])
                yield
                for ti, t, r0, d, P, bP, PX, bPX in info:
                    S.op("dve", I_tt(d["t"][:], d["t"][:], lng[:], ALU.mult), reads=[d["b_t"], b_w], writes=[d["b_t"]])
                yield
                for ti, t, r0, d, P, bP, PX, bPX in info:
                    S.op("pool", I_tt(d["t"][:], d["t"][:], lnb[:], ALU.add), reads=[d["b_t"], b_w], writes=[d["b_t"]])
                yield
                for ti, t, r0, d, P, bP, PX, bPX in info:
                    S.dma("sp", I_dma(self.XB[r0:r0 + 128, :], d["t"][:]), reads=[d["b_t"]], cowrites=[self.bXB[s]])
                    S.dma("sp", I_dma(self.FA[r0:r0 + 128, :], self.zeros[:]), reads=[self.bconst],
                          cowrites=[self.bFAc if isctx else self.bFAl[s]])
                yield
                for ti, t, r0, d, P, bP, PX, bPX in info:
                    for k in range(KC):
                        S.op("pe", I_tr(PX[:, k * 128:(k + 1) * 128], d["t"][:, k * 128:(k + 1) * 128], self.identF[:]),
                             reads=[d["b_t"], self.bconst], writes=[bPX[k // 4]])
                    for k in range(KC):
                        S.op("act", I_act(d["h2T"][:, k, :], PX[:, k * 128:(k + 1) * 128], AF.Identity,
                                          scale=self.modT[:, 3, k, j:j + 1], bias=self.modT[:, 2, k, j:j + 1]),
                             reads=[bPX[k // 4], self.bmodT], writes=[d["b_h2T"][k]])
                yield
                for ti, t, r0, d, P, bP, PX, bPX in info:
                    for k in range(KC):
                        S.op("pe", I_mm(P[:, 32 * (ti // 2):32 * (ti // 2) + E], d["h2T"][:, k, :], wr[:, k, :], k == 0, k == KC - 1),
                             reads=[d["b_h2T"][k], b_w], writes=[bP[0]])
                    S.op("dve", (lambda d, P, lc: lambda h: h.tensor_reduce(out=d["mx"][:], in_=P[:, lc:lc + E], axis=AX.X, op=ALU.max))(d, P, 32 * (ti // 2)),
                         reads=[bP[0]], writes=[d["b_mx"]])
                yield
                for ti, t, r0, d, P, bP, PX, bPX in info:
                    S.op("dve", I_ts(d["nmx"][:], d["mx"][:], -1.0, None, ALU.mult), reads=[d["b_mx"]], writes=[d["b_nmx"]])
                yield
                for ti, t, r0, d, P, bP, PX, bPX in info:
                    S.op("act", I_act(d["ex"][:], P[:, 32 * (ti // 2):32 * (ti // 2) + E], AF.Exp, scale=1.0, bias=d["nmx"][:, 0:1], accum_out=d["ssum"][:]),
                         reads=[bP[0], d["b_nmx"]], writes=[d["b_ex"], d["b_ssum"]])
                yield
                for ti, t, r0, d, P, bP, PX, bPX in info:
                    S.op("dve", (lambda d: lambda h: h.reciprocal(out=d["rs"][:], in_=d["ssum"][:]))(d), reads=[d["b_ssum"]], writes=[d["b_rs"]])
                yield
                for ti, t, r0, d, P, bP, PX, bPX in info:
                    S.op("dve", I_ts(d["aff"][:], d["ex"][:], d["rs"][:, 0:1], None, ALU.mult), reads=[d["b_ex"], d["b_rs"]], writes=[d["b_aff"]])
                yield
                for ti, t, r0, d, P, bP, PX, bPX in info:
                    S.op("pe", I_tr(P[0:E, 512:640], d["aff"][:], self.identF[:]), reads=[d["b_aff"], self.bconst], writes=[bP[1]])
                    S.op("act", I_acopy(d["affT"][:], P[0:E, 512:640]), reads=[bP[1]], writes=[d["b_affT"]])
                yield
                for ti, t, r0, d, P, bP, PX, bPX in info:
                    if isctx:
                        dst = self.AFC[s * E:(s + 1) * E, t * 128:(t + 1) * 128]
                    else:
                        dst = self.AFL[s * E:(s + 1) * E, (t - 2) * 128:(t - 1) * 128]
                    S.dma("sp", I_dma(dst, d["affT"][:]), reads=[d["b_affT"]], cowrites=[self.bAF])

            kvload = 0
            blkcnt = 0
            for s in range(NS):
                S.dma("sp", I_dma(G1l[:], self.GS[0, s]), reads=[self.bGS], writes=[b_G1l])
                if not gqa:
                    S.dma("sp", I_dma(krT[:], self.KT[s, 8, :, :]), reads=[self.bKT[s]], writes=[b_krT])
                blocks = ([] if last else [[0, 1]]) + [[2 + 4 * b + i for i in range(4)] for b in range(4)]
                for tiles in blocks:
                    isctx = tiles[0] < 2
                    j = NS if isctx else s
                    nq = len(tiles) * 128
                    nkc = 2 if isctx else NT
                    G1 = G1c if isctx else G1l
                    bG1 = b_G1c if isctx else b_G1l
                    qb = blkcnt % 2; blkcnt += 1
                    c0 = tiles[0] * 128
                    S.dma("sp", I_dma(qTb[qb][:, :, 0:nq], self.QT[s, :, :, c0:c0 + nq].rearrange("h d n -> d h n")),
                          reads=[self.bQT[s]], writes=[b_qTb[qb]])
                    if not gqa:
                        S.dma("sp", I_dma(qrT[qb][:, :, 0:nq], self.QR[s, :, :, c0:c0 + nq].rearrange("h d n -> d h n")),
                              reads=[self.bQT[s]], writes=[b_qrT[qb]])
                    for ti, t in enumerate(tiles):
                        r0 = s * NR + t * 128
                        S.dma("sp", I_dma(xblk[:, ti, :], self.XA[r0:r0 + 128, :]), reads=[self.bXA[s]], writes=[b_xblk[ti]])
                    nk = nkc * 128
                    kvslot = {}

                    def load_kv(g):
                        nonlocal kvload
                        kb = kvload % 2; kvload += 1
                        kvslot[g] = kb
                        S.dma("sp", I_dma(kTg[kb][:, 0:nk], self.KT[s, g, :, 0:nk]), reads=[self.bKT[s]], writes=[b_kTg[kb]])
                        S.dma("sp", I_dma(Vg[kb][:, 0:nkc, :], self.VS[s, g, :, 0:nkc, :]), reads=[self.bVS[s]], writes=[b_Vg[kb]])

                    units = []
                    for g in range(NKH):
                        for hh in ([g * 4 + i for i in range(4)] if gqa else [g]):
                            for q0 in range(0, nq, QS):
                                units.append((g, hh, q0, min(QS, nq - q0)))

                    def emit_S(u, i, kc):
                        g, hh, q0, qn = u
                        kb = kvslot[g]
                        pc = PC[:, (kc % 2) * 512:(kc % 2) * 512 + qn]
                        if gqa:
                            S.op("pe", I_mm(pc, kTg[kb][:, kc * 128:(kc + 1) * 128], qTb[qb][:, hh, q0:q0 + qn], True, True),
                                 reads=[b_kTg[kb], b_qTb[qb]], writes=[self.bPC[kc % 2]])
                        else:
                            S.op("pe", I_mm(pc, kTg[kb][:, kc * 128:(kc + 1) * 128], qTb[qb][:, hh, q0:q0 + qn], True, False),
                                 reads=[b_kTg[kb], b_qTb[qb]], writes=[self.bPC[kc % 2]])
                            S.op("pe", I_mm(pc, krT[:, kc * 128:(kc + 1) * 128], qrT[qb][:, hh, q0:q0 + qn], False, True),
                                 reads=[b_krT, b_qrT[qb]], writes=[self.bPC[kc % 2]])
                        S.op("act", I_act(PT[i % 2][:, kc, 0:qn], pc, AF.Exp, scale=SCALE), reads=[self.bPC[kc % 2]],
                             writes=[b_PT[i % 2][kc]])

                    def emit_PV(u, i, kc):
                        g, hh, q0, qn = u
                        kb = kvslot[g]
                        acc, bacc = (PD, self.bPD) if i % 2 == 0 else (PB, self.bPB)
                        S.op("pe", I_mm(acc[:, 0:qn], Vg[kb][:, kc, :], PT[i % 2][:, kc, 0:qn], kc == 0, kc == nkc - 1),
                             reads=[b_Vg[kb], b_PT[i % 2][kc]], writes=[bacc[0]])
                        S.op("pe", I_mm(acc[:, 512:512 + qn], self.onesB[:], PT[i % 2][:, kc, 0:qn], kc == 0, kc == nkc - 1),
                             reads=[self.bconst, b_PT[i % 2][kc]], writes=[bacc[1]])

                    def emit_norm(u, i):
                        g, hh, q0, qn = u
                        acc, bacc = (PD, self.bPD) if i % 2 == 0 else (PB, self.bPB)
                        S.op("dve", lambda h: h.reciprocal(out=rden[:, 0:qn], in_=acc[:, 512:512 + qn]),
                             reads=[bacc[1]], writes=[b_rden])
                        S.op("dve", I_tt(oTb[:, hh, q0:q0 + qn], acc[:, 0:qn], rden[:, 0:qn], ALU.mult),
                             reads=[bacc[0], b_rden], writes=[b_oTb])

                    load_kv(0)
                    for kc in range(nkc):
                        emit_S(units[0], 0, kc)
                    for i, u in enumerate(units):
                        nxt = units[i + 1] if i + 1 < len(units) else None
                        if (i == 0 or units[i - 1][0] != u[0]) and u[0] + 1 < NKH:
                            load_kv(u[0] + 1)
                        for kc in range(nkc):
                            if nxt is not None:
                                emit_S(nxt, i + 1, kc)
                            emit_PV(u, i, kc)
                        emit_norm(u, i)
                    for _ in chain_block(s, tiles, isctx, j, G1, bG1):
                        pass
            S.run_block()

    def phase_R(self, l, last):
        nc, S, ext = self.nc, self.S, self.ext
        PA = self.PA
        es = self.es_R = ExitStack()
        T = lambda n, sh, dt: es.enter_context(nc.sbuf_tensor(f"{n}_L{l}", sh, dt))
        self.idxT = T("r_idxT", [128, 2, 64], I32)
        self.gT = T("r_gT", [128, 2, 64], F32)
        self.idxC = T("r_idxC", [128, E], I32)
        self.gC = T("r_gC", [128, E], F32)
        self.b_idx = Buf()
        with ExitStack() as es2:
            T2 = lambda n, sh, dt: es2.enter_context(nc.sbuf_tensor(f"{n}_L{l}", sh, dt))
            aw = T2("r_aw", [64, SEQ], F32); tv = T2("r_tv", [64, CAPL], F32); ti = T2("r_ti", [64, CAPL], U32)
            tf = T2("r_tf", [64, CAPL], F32); offs = T2("r_offs", [64, 2], F32)
            b_aw, b_tv, b_ti, b_tf, b_offs = bufs(5)
            S.dma("sp", I_dma(offs[:], ext["offs"]), writes=[b_offs])
            S.dma("sp", I_dma(aw[:], self.AFL), reads=[self.bAF], writes=[b_aw])
            for it in range(CAPL // 8):
                sl = slice(it * 8, it * 8 + 8)
                S.op("dve", lambda h, sl=sl: h.max(out=tv[:, sl], in_=aw[:]), reads=[b_aw], writes=[b_tv])
                S.op("dve", lambda h, sl=sl: h.max_index(out=ti[:, sl], in_max=tv[:, sl], in_values=aw[:]), reads=[b_aw, b_tv], writes=[b_ti])
                S.op("dve", lambda h, sl=sl: h.match_replace(out=aw[:], in_to_replace=tv[:, sl], in_values=aw[:], imm_value=-1.0),
                     reads=[b_tv, b_ti, b_aw], writes=[b_aw])
            S.op("dve", I_copy(tf[:], ti[:]), reads=[b_ti], writes=[b_tf])
            S.op("dve", I_ts(tf[:], tf[:], offs[:, 0:1], None, ALU.add), reads=[b_tf, b_offs], writes=[b_tf])
            for rb in range(2):
                S.op("pe", I_tr(PA[:, rb * 64:(rb + 1) * 64], tf[:, rb * 128:(rb + 1) * 128], self.identF[0:64, 0:64]),
                     reads=[b_tf, self.bconst], writes=self.bPA)
                S.op("pe", I_tr(PA[:, 128 + rb * 64:128 + (rb + 1) * 64], tv[:, rb * 128:(rb + 1) * 128], self.identF[0:64, 0:64]),
                     reads=[b_tv, self.bconst], writes=self.bPA)
            S.op("dve", I_copy(self.idxT[:], PA[:, 0:128].rearrange("p (r c) -> p r c", c=64)), reads=self.bPA, writes=[self.b_idx])
            S.op("act", I_acopy(self.gT[:], PA[:, 128:256].rearrange("p (r c) -> p r c", c=64)), reads=self.bPA, writes=[self.b_idx])
            if not last:
                ac = T2("r_ac", [64, CTX], F32); cv = T2("r_cv", [64, CAPC], F32); ci = T2("r_ci", [64, CAPC], U32)
                cf = T2("r_cf", [64, CAPC], F32); ciT = T2("r_ciT", [32, 64], I32); cgT = T2("r_cgT", [32, 64], F32)
                b_ac, b_cv, b_ci, b_cf, b_ciT = bufs(5)
                S.dma("sp", I_dma(ac[:], self.AFC), reads=[self.bAF], writes=[b_ac])
                for it in range(CAPC // 8):
                    sl = slice(it * 8, it * 8 + 8)
                    S.op("dve", lambda h, sl=sl: h.max(out=cv[:, sl], in_=ac[:]), reads=[b_ac], writes=[b_cv])
                    S.op("dve", lambda h, sl=sl: h.max_index(out=ci[:, sl], in_max=cv[:, sl], in_values=ac[:]), reads=[b_ac, b_cv], writes=[b_ci])
                    S.op("dve", lambda h, sl=sl: h.match_replace(out=ac[:], in_to_replace=cv[:, sl], in_values=ac[:], imm_value=-1.0),
                         reads=[b_cv, b_ci, b_ac], writes=[b_ac])
                S.op("dve", I_copy(cf[:], ci[:]), reads=[b_ci], writes=[b_cf])
                S.op("dve", I_ts(cf[:], cf[:], offs[:, 1:2], None, ALU.add), reads=[b_cf, b_offs], writes=[b_cf])
                S.op("pe", I_tr(PA[0:32, 256:320], cf[:], self.identF[0:64, 0:64]), reads=[b_cf, self.bconst], writes=self.bPA)
                S.op("pe", I_tr(PA[0:32, 320:384], cv[:], self.identF[0:64, 0:64]), reads=[b_cv, self.bconst], writes=self.bPA)
                S.op("dve", I_copy(ciT[:], PA[0:32, 256:320]), reads=self.bPA, writes=[b_ciT])
                S.op("act", I_acopy(cgT[:], PA[0:32, 320:384]), reads=self.bPA, writes=[b_ciT])
                for s in range(NS):
                    S.dma("sp", I_dma(self.idxC[s * 32:(s + 1) * 32, :], ciT[:, s * E:(s + 1) * E]), reads=[b_ciT], writes=[self.b_idx])
                    S.dma("sp", I_dma(self.gC[s * 32:(s + 1) * 32, :], cgT[:, s * E:(s + 1) * E]), reads=[b_ciT], writes=[self.b_idx])
            S.run_block()

    def phase_C(self, l, last):
        nc, S, ext = self.nc, self.S, self.ext
        PA, PB, PC, PD = self.PA, self.PB, self.PC, self.PD
        ntile = 2 * NS + (0 if last else 1)
        ncols = ntile * 128
        cblocks = [(c0, min(c0 + 512, ncols)) for c0 in range(0, ncols, 512)]
        with ExitStack() as es:
            T = lambda n, sh, dt: es.enter_context(nc.sbuf_tensor(f"{n}_L{l}", sh, dt))
            wg = [T(f"c_wg{i}", [128, KC, D], BF16) for i in range(2)]
            wu = [T(f"c_wu{i}", [128, KC, D], BF16) for i in range(2)]
            wd = [T(f"c_wd{i}", [128, KC, D], BF16) for i in range(2)]
            b_wt = bufs(2)
            xs = [T(f"c_xs{i}", [128, D], F32) for i in range(9)]; b_xs = bufs(9)
            xsT2 = [T(f"c_xsT{i}", [128, KC, 9 * 128], BF16) for i in range(2)]; b_xsT2 = [bufs(KC) for _ in range(2)]
            hT = T("c_hT", [128, KC, 9 * 128], BF16); b_hT = Buf()
            sil = [T(f"c_sil{i}", [128, 512], F32) for i in range(2)]; b_sil = bufs(2)
            yg = [T(f"c_yg{i}", [128, D], F32) for i in range(2)]; b_yg = bufs(2)

            def load_w(e):
                i = e % 2
                for wt, nm in ((wg[i], "wg"), (wu[i], "wu"), (wd[i], "wd")):
                    S.dma("pool", I_dma(wt[:], ext[f"{nm}{l}"][e].rearrange("(k p) f -> p k f", p=128)), writes=[b_wt[i]])

            def tile_info(jt):
                if jt < 2 * NS:
                    s, rb = jt // 2, jt % 2
                    return s, self.idxT[:, rb, s * E:(s + 1) * E], self.gT[:, rb, s * E:(s + 1) * E], self.bFAl[s], self.bXB[s:s + 1]
                return NS, self.idxC[:, :], self.gC[:, :], self.bFAc, self.bXB

            def gathers(e):
                for jt in range(ntile):
                    jmod, idx, gate, bfa, bxb = tile_info(jt)
                    S.dma("pool", (lambda x, idx, e: (lambda h: h.indirect_dma_start(
                        out=x[:], out_offset=None, in_=self.XB,
                        in_offset=bass.IndirectOffsetOnAxis(ap=idx[:, e:e + 1], axis=0))))(xs[jt], idx, e),
                          reads=[self.b_idx] + list(bxb), writes=[b_xs[jt]])

            def prep(e):
                xsT = xsT2[e % 2]; b_xsT = b_xsT2[e % 2]
                for jt in range(ntile):
                    jmod, idx, gate, bfa, bxb = tile_info(jt)
                    x = xs[jt]; bx = b_xs[jt]
                    for k in range(KC):
                        S.op("pe", I_tr(PA[:, k * 128:(k + 1) * 128], x[:, k * 128:(k + 1) * 128], self.identF[:]),
                             reads=[bx, self.bconst], writes=[self.bPA[k // 4]])
                    for k in range(KC):
                        sc = self.modT[:, 3, k, jmod:jmod + 1]; sh = self.modT[:, 2, k, jmod:jmod + 1]
                        S.op("act", I_act(xsT[:, k, jt * 128:(jt + 1) * 128], PA[:, k * 128:(k + 1) * 128], AF.Identity, scale=sc, bias=sh),
                             reads=[self.bPA[k // 4], self.bmodT], writes=[b_xsT[k]])
                    yield

            load_w(0)
            gathers(0)
            for _ in prep(0):
                pass
            cnt = 0
            for e in range(E):
                i = e % 2
                xsT = xsT2[e % 2]; b_xsT = b_xsT2[e % 2]
                if e + 1 < E:
                    gathers(e + 1)
                    load_w(e + 1)
                pg = prep(e + 1) if e + 1 < E else iter(())
                gcnt = 0
                for f in range(KC):
                    for (c0, c1) in cblocks:
                        w_ = c1 - c0
                        pb = gcnt % 2; gcnt += 1
                        pa_ = PB[:, pb * 512:pb * 512 + w_]; pu_ = PC[:, pb * 512:pb * 512 + w_]
                        for k in range(KC):
                            S.op("pe", I_mm(pa_, wg[i][:, k, f * 128:(f + 1) * 128], xsT[:, k, c0:c1], k == 0, k == KC - 1),
                                 reads=[b_wt[i], b_xsT[k]], writes=[self.bPBh[pb]])
                        for k in range(KC):
                            S.op("pe", I_mm(pu_, wu[i][:, k, f * 128:(f + 1) * 128], xsT[:, k, c0:c1], k == 0, k == KC - 1),
                                 reads=[b_wt[i], b_xsT[k]], writes=[self.bPC[pb]])
                        S.op("act", I_act(sil[pb][:, 0:w_], pa_, AF.Silu), reads=[self.bPBh[pb]], writes=[b_sil[pb]])
                        S.op("dve", I_tt(hT[:, f, c0:c1], pu_, sil[pb][:, 0:w_], ALU.mult),
                             reads=[self.bPC[pb], b_sil[pb]], writes=[b_hT])
                        if gcnt % 2 == 0:
                            next(pg, None)
                for _ in pg:
                    pass
                for jt in range(ntile):
                    jmod, idx, gate, bfa, bxb = tile_info(jt)
                    for half in range(2):
                        for f in range(KC):
                            S.op("pe", I_mm(PD[:, half * 512:(half + 1) * 512], hT[:, f, jt * 128:(jt + 1) * 128],
                                            wd[i][:, f, half * 512:(half + 1) * 512], f == 0, f == KC - 1),
                                 reads=[b_hT, b_wt[i]], writes=[self.bPD[half]])
                    y = yg[jt % 2]; by = b_yg[jt % 2]
                    S.op("act", I_act(y[:, 0:512], PD[:, 0:512], AF.Identity, scale=gate[:, e:e + 1]),
                         reads=[self.bPD[0], self.b_idx], writes=[by])
                    S.op("dve", I_ts(y[:, 512:1024], PD[:, 512:1024], gate[:, e:e + 1], None, ALU.mult),
                         reads=[self.bPD[1], self.b_idx], writes=[by])
                    S.dma("pool", (lambda y, idx, e: (lambda h: h.indirect_dma_start(
                        out=self.FA, out_offset=bass.IndirectOffsetOnAxis(ap=idx[:, e:e + 1], axis=0),
                        in_=y[:], in_offset=None, compute_op=ALU.add)))(y, idx, e),
                          reads=[by, self.b_idx], writes=[bfa])
            S.run_block()
        self.es_R.close()

    def phase_D(self, l, last, is_out):
        nc, S, ext = self.nc, self.S, self.ext
        GD = 6
        with ExitStack() as es:
            T = lambda n, sh, dt: es.enter_context(nc.sbuf_tensor(f"{n}_L{l}", sh, dt))
            lng = T("d_lng", [128, D], F32); lnb = T("d_lnb", [128, D], F32)
            G2c = T("d_G2c", [128, D], F32); G2l = T("d_G2l", [128, D], F32)
            epsc = T("d_eps", [128, 1], F32)
            b_w, b_G2c, b_G2l, b_eps = bufs(4)
            S.dma("sp", I_dma(lng[:], ext[f"lnfg{l}"][0, :].partition_broadcast(128)), writes=[b_w])
            S.dma("sp", I_dma(lnb[:], ext[f"lnfb{l}"][0, :].partition_broadcast(128)), writes=[b_w])
            S.dma("sp", I_dma(G2c[:], self.GS[1, NS]), reads=[self.bGS], writes=[b_G2c])
            S.op("pool", I_memset(epsc[:], EPS), writes=[b_eps])
            slots = []
            for i in range(GD):
                d = {n: T(f"d_{n}{i}", sh, F32) for n, sh in (("x1", [128, D]), ("f", [128, D]), ("t", [128, D]), ("st", [128, 2, 6]),
                                                              ("mv", [128, 2]), ("lnv", [128, 1]), ("rstd", [128, 1]), ("nmr", [128, 1]))}
                for n in ("x1", "f", "t", "st", "mv", "lnv", "rstd", "nmr"):
                    d["b_" + n] = Buf()
                slots.append(d)
            for s in range(NS):
                S.dma("sp", I_dma(G2l[:], self.GS[1, s]), reads=[self.bGS], writes=[b_G2l])
                tl = list(range(2 if last else 0, NT))
                for g0 in range(0, len(tl), GD):
                    grp = [(tl[g0 + i], slots[i]) for i in range(min(GD, len(tl) - g0))]
                    info = []
                    for t, d in grp:
                        isctx = t < 2
                        r0 = s * NR + t * 128
                        info.append((t, d, isctx, r0, (G2c if isctx else G2l), (b_G2c if isctx else b_G2l)))
                    for t, d, isctx, r0, G2, bG2 in info:
                        S.dma("sp", I_dma(d["x1"][:], self.XB[r0:r0 + 128, :]), reads=[self.bXB[s]], writes=[d["b_x1"]])
                        S.dma("sp", I_dma(d["f"][:], self.FA[r0:r0 + 128, :]), reads=[self.bFAc if isctx else self.bFAl[s]], writes=[d["b_f"]])
                    for t, d, isctx, r0, G2, bG2 in info:
                        S.op("pool", I_tt(d["t"][:], d["f"][:], G2[:], ALU.mult), reads=[d["b_f"], bG2], writes=[d["b_t"]])
                    for t, d, isctx, r0, G2, bG2 in info:
                        S.op("dve", I_stt(d["t"][:], d["x1"][:], ALPHA, d["t"][:], ALU.mult, ALU.add), reads=[d["b_x1"], d["b_t"]], writes=[d["b_t"]])
                    for t, d, isctx, r0, G2, bG2 in info:
                        S.op("dve", (lambda d: lambda h: h.bn_stats(out=d["st"][:, 0, :], in_=d["t"][:, 0:512]))(d), reads=[d["b_t"]], writes=[d["b_st"]])
                        S.op("dve", (lambda d: lambda h: h.bn_stats(out=d["st"][:, 1, :], in_=d["t"][:, 512:1024]))(d), reads=[d["b_t"]], writes=[d["b_st"]])
                    for t, d, isctx, r0, G2, bG2 in info:
                        S.op("dve", (lambda d: lambda h: h.bn_aggr(out=d["mv"][:], in_=d["st"][:].rearrange("p a b -> p (a b)")))(d), reads=[d["b_st"]], writes=[d["b_mv"]])
                    for t, d, isctx, r0, G2, bG2 in info:
                        S.op("act", I_act(d["lnv"][:], d["mv"][:, 1:2], AF.Ln, scale=1.0, bias=epsc[:, 0:1]), reads=[d["b_mv"], b_eps], writes=[d["b_lnv"]])
                    for t, d, isctx, r0, G2, bG2 in info:
                        S.op("act", I_act(d["rstd"][:], d["lnv"][:], AF.Exp, scale=-0.5), reads=[d["b_lnv"]], writes=[d["b_rstd"]])
                    for t, d, isctx, r0, G2, bG2 in info:
                        S.op("dve", I_ts(d["nmr"][:], d["mv"][:, 0:1], d["rstd"][:, 0:1], -1.0, ALU.mult, ALU.mult), reads=[d["b_mv"], d["b_rstd"]], writes=[d["b_nmr"]])
                    for t, d, isctx, r0, G2, bG2 in info:
                        S.op("act", I_act(d["t"][:], d["t"][:], AF.Identity, scale=d["rstd"][:, 0:1], bias=d["nmr"][:, 0:1]),
                             reads=[d["b_t"], d["b_rstd"], d["b_nmr"]], writes=[d["b_t"]])
                    for t, d, isctx, r0, G2, bG2 in info:
                        S.op("dve", I_tt(d["t"][:], d["t"][:], lng[:], ALU.mult), reads=[d["b_t"], b_w], writes=[d["b_t"]])
                    for t, d, isctx, r0, G2, bG2 in info:
                        S.op("pool", I_tt(d["t"][:], d["t"][:], lnb[:], ALU.add), reads=[d["b_t"], b_w], writes=[d["b_t"]])
                    for t, d, isctx, r0, G2, bG2 in info:
                        if is_out:
                            if self.final:
                                if isctx:
                                    continue
                                dst = self.y[s * SEQ + (t - 2) * 128: s * SEQ + (t - 1) * 128, :]
                            else:
                                dst = self.y[r0:r0 + 128, :]
                            self.out_toks.append(S.dma("sp", I_dma(dst, d["t"][:]), reads=[d["b_t"]], cowrites=[self.bY]))
                        else:
                            S.dma("sp", I_dma(self.XA[r0:r0 + 128, :], d["t"][:]), reads=[d["b_t"]], cowrites=[self.bXA[s]])
            S.run_block(final_waits=self.out_toks if is_out else ())


_PROG_CACHE = {}


def get_prog(layers, first, final):
    key = (tuple(layers), first, final)
    if key not in _PROG_CACHE:
        _PROG_CACHE[key] = Prog(list(layers), first, final)
    return _PROG_CACHE[key]


def const_inputs():
    cg, sg = rope_tables(128)
    cm, sm = rope_tables(64)
    offs = np.zeros((64, 2), np.float32)
    for s in range(NS):
        offs[s * E:(s + 1) * E, 0] = s * NR + CTX
        offs[s * E:(s + 1) * E, 1] = s * NR
    return dict(identf=np.eye(128, dtype=np.float32), cosg=cg, sing=sg, cosm=cm, sinm=sm, offs=offs)


def layer_inputs(l, inp):
    j = l // 2
    m = {}
    m[f"ada_w{l}"] = inp["ada_w"][l]
    m[f"ada_b{l}"] = inp["ada_b"][l][None, :]
    m[f"ada_bT{l}"] = np.ascontiguousarray(inp["ada_b"][l].reshape(48, 128).T)
    m[f"lnmg{l}"] = inp["ln_mix_g"][l][None, :]; m[f"lnmb{l}"] = inp["ln_mix_b"][l][None, :]
    m[f"lnfg{l}"] = inp["ln_ffn_g"][l][None, :]; m[f"lnfb{l}"] = inp["ln_ffn_b"][l][None, :]
    m[f"rw{l}"] = inp["router_w"][l]
    m[f"wg{l}"] = inp["expert_w_gate"][l]; m[f"wu{l}"] = inp["expert_w_up"][l]; m[f"wd{l}"] = inp["expert_w_down"][l]
    if l % 2 == 0:
        m[f"wqkv{l}"] = inp["gqa_w_qkv"][j]; m[f"qg{l}"] = inp["gqa_q_g"][j][None, :]
        m[f"kg{l}"] = inp["gqa_k_g"][j][None, :]; m[f"wo{l}"] = inp["gqa_w_o"][j]
    else:
        m[f"wdq{l}"] = inp["mla_w_dq"][j]; m[f"mqg{l}"] = inp["mla_q_g"][j][None, :]; m[f"wuq{l}"] = inp["mla_w_uq"][j]
        m[f"wdkv{l}"] = inp["mla_w_dkv"][j]; m[f"mkvg{l}"] = inp["mla_kv_g"][j][None, :]
        m[f"wukv{l}"] = inp["mla_w_ukv"][j]; m[f"wo{l}"] = inp["mla_w_o"][j]
    return {k: np.ascontiguousarray(v, dtype=np.float32) for k, v in m.items()}


LAUNCH_GROUPS = [[0, 1, 2, 3]]


def kernel(**inp):
    inp = {k: np.asarray(v) for k, v in inp.items()}
    consts = const_inputs()
    xa = []
    ccs = []
    for c in range(NCORES):
        rows = []
        for s in range(NS):
            b = c * NS + s
            rows.append(inp["ctx"][b]); rows.append(inp["x"][b])
        xa.append(np.ascontiguousarray(np.concatenate(rows, 0), dtype=np.float32))
        cc = np.concatenate([inp["c"][c * NS:(c + 1) * NS], inp["c_ctx"][None, :]], 0)
        ccs.append(np.ascontiguousarray(cc.reshape(NS + 1, KC, 128).transpose(2, 1, 0), dtype=np.float32))
    out = None
    for gi, layers in enumerate(LAUNCH_GROUPS):
        final = layers[-1] == DEPTH - 1
        prog = get_prog(layers, gi == 0, final)
        wl = {}
        for l in layers:
            wl.update(layer_inputs(l, inp))
        in_maps = []
        for c in range(NCORES):
            m = dict(consts); m.update(wl)
            m["xa_in"] = xa[c]; m["ccT"] = ccs[c]
            in_maps.append(m)
        res = run_bass_kernel_spmd(prog.nc, in_maps, core_ids=list(range(NCORES)))
        if final:
            out = np.concatenate([r["y"].reshape(NS, SEQ, D) for r in res.results], 0)
        else:
            xa = [np.ascontiguousarray(r["xa_out"]) for r in res.results]
    return out.astype(np.float32)
```

```python
import numpy as np
from contextlib import ExitStack
import concourse.bass as bass
import concourse.mybir as mybir
from concourse.bass_utils import run_bass_kernel_spmd

F32 = mybir.dt.float32
BF16 = mybir.dt.bfloat16
I32 = mybir.dt.int32
U32 = mybir.dt.uint32
AF = mybir.ActivationFunctionType
ALU = mybir.AluOpType
AX = mybir.AxisListType

ENGS = ("pe", "act", "dve", "pool", "sp")

D = 1024
KC = 8
NCORES = 8
NS = 4
SEQ = 2048
CTX = 256
NR = SEQ + CTX
NT = NR // 128
E = 16
CAPL = 256
CAPC = 32
DEPTH = 4
ALPHA = float((2 * DEPTH) ** 0.25)
EPS = 1e-6
THETA = 10000.0
GRID_W = 64


class Buf:
    __slots__ = ("w", "r")

    def __init__(self):
        self.w = []
        self.r = []


def bufs(n):
    return [Buf() for _ in range(n)]


class DmaGroup:
    __slots__ = ("ring", "sem_i", "total", "prev_total", "first")


class Sched:
    def __init__(self, nc, esems, rings):
        self.nc = nc
        self.esems = esems
        self.rings = rings
        self.cnt = {e: 0 for e in ENGS}
        self.dtot = {q: [0] * len(v) for q, v in rings.items()}
        self.dnext = {q: 0 for q in rings}
        self.seen = {e: {} for e in ENGS}
        self.q = {e: [] for e in ENGS}
        self.ninst = 0
        self.muted = False

    def _deps(self, reads, writes, cowrites=()):
        deps = []
        for b in reads:
            for t in b.w:
                deps.append((t, True))
        for b in writes:
            for t in b.w:
                deps.append((t, False))
            for t in b.r:
                deps.append((t, False))
        for b in cowrites:
            for t in b.r:
                deps.append((t, False))
        return deps

    def op(self, eng, fn, reads=(), writes=()):
        if self.muted:
            return ("dv", "sp", 0, 0)
        deps = self._deps(reads, writes)
        self.cnt[eng] += 1
        tok = ("e", eng, self.cnt[eng])
        self.q[eng].append((deps, fn, ("e", eng)))
        for b in reads:
            b.r.append(tok)
        for b in writes:
            b.w = [tok]
            b.r = []
        self.ninst += 1
        return tok

    def dma_group(self, eng):
        i = self.dnext[eng]
        self.dnext[eng] = (i + 1) % len(self.rings[eng])
        g = DmaGroup()
        g.ring = eng
        g.sem_i = i
        g.total = self.dtot[eng][i]
        g.prev_total = self.dtot[eng][i]
        g.first = True
        return g

    def dma(self, eng, fn, reads=(), writes=(), group=None, cowrites=()):
        if self.muted:
            return ("dv", "sp", 0, 0)
        g = group if group is not None else self.dma_group(eng)
        assert g.ring == eng
        deps = self._deps(reads, writes, cowrites)
        if g.first:
            if g.prev_total > 0:
                deps.append((("dv", eng, g.sem_i, g.prev_total), True))
            g.first = False
        self.dtot[eng][g.sem_i] += 16
        g.total = self.dtot[eng][g.sem_i]
        tok = ("d", g)
        self.q[eng].append((deps, fn, ("d", eng, g.sem_i)))
        for b in reads:
            b.r.append(tok)
        for b in writes:
            b.w = [tok]
            b.r = []
        for b in cowrites:
            b.w.append(tok)
        self.ninst += 1
        return tok

    def _resolve(self, dep):
        if dep[0] == "e":
            return ("e", dep[1]), dep[2]
        if dep[0] == "d":
            g = dep[1]
            return ("d", g.ring, g.sem_i), g.total
        return ("d", dep[1], dep[2]), dep[3]

    def _sem(self, key):
        return self.esems[key[1]] if key[0] == "e" else self.rings[key[1]][key[2]]

    def replay_engine(self, eng, h):
        seen = self.seen[eng]
        for deps, fn, inc in self.q[eng]:
            need = {}
            for d, raw in deps:
                if d[0] == "e" and d[1] == eng and eng == "pe":
                    continue
                key, val = self._resolve(d)
                if val > seen.get(key, 0) and val > need.get(key, 0):
                    need[key] = val
            for key, val in need.items():
                h.wait_ge(self._sem(key), val)
                seen[key] = val
            ins = fn(h)
            if inc[0] == "e":
                ins.then_inc(self.esems[inc[1]], 1)
            else:
                ins.then_inc(self.rings[inc[1]][inc[2]], 16)
        self.q[eng] = []

    def run_block(self, final_waits=()):
        fence = getattr(self, "fence", [])
        for eng in ENGS:
            if fence and self.q[eng]:
                deps, fn, inc = self.q[eng][0]
                self.q[eng][0] = (list(deps) + [(d, True) for d in fence if not (d[0] == "e" and d[1] == eng)], fn, inc)
        with self.nc.Block() as block:
            @block.tensor
            def _(h):
                self.replay_engine("pe", h)

            @block.scalar
            def _(h):
                self.replay_engine("act", h)

            @block.vector
            def _(h):
                self.replay_engine("dve", h)

            @block.gpsimd
            def _(h):
                self.replay_engine("pool", h)

            @block.sync
            def _(h):
                self.replay_engine("sp", h)
                for tok in final_waits:
                    key, val = self._resolve(tok)
                    h.wait_ge(self._sem(key), val)
        self.fence = [("e", e, self.cnt[e]) for e in ENGS if self.cnt[e] > 0]
        for q, tots in self.dtot.items():
            for i, v in enumerate(tots):
                if v > 0:
                    self.fence.append(("dv", q, i, v))


def I_mm(out, lhsT, rhs, start, stop):
    return lambda h: h.matmul(out, lhsT=lhsT, rhs=rhs, start=start, stop=stop)


def I_tr(out, in_, ident):
    return lambda h: h.transpose(out=out, in_=in_, identity=ident)


def I_act(out, in_, func, scale=1.0, bias=0.0, accum_out=None):
    if accum_out is None:
        return lambda h: h.activation(out=out, in_=in_, func=func, bias=bias, scale=scale)
    return lambda h: h.activation(out=out, in_=in_, func=func, bias=bias, scale=scale, accum_out=accum_out)


def I_tt(out, in0, in1, op):
    return lambda h: h.tensor_tensor(out=out, in0=in0, in1=in1, op=op)


def I_ts(out, in0, s1, s2, op0, op1=None):
    if op1 is None:
        return lambda h: h.tensor_scalar(out=out, in0=in0, scalar1=s1, scalar2=None, op0=op0)
    return lambda h: h.tensor_scalar(out=out, in0=in0, scalar1=s1, scalar2=s2, op0=op0, op1=op1)


def I_stt(out, in0, scalar, in1, op0, op1):
    return lambda h: h.scalar_tensor_tensor(out=out, in0=in0, scalar=scalar, in1=in1, op0=op0, op1=op1)


def I_acopy(out, in_):
    return lambda h: h.activation(out=out, in_=in_, func=AF.Copy)


def I_copy(out, in_):
    return lambda h: h.tensor_copy(out=out, in_=in_)


def I_dma(out, in_):
    return lambda h: h.dma_start(out=out, in_=in_)


def I_memset(ap, v):
    return lambda h: h.memset(ap, v)


def rope_tables(rot_dim):
    t = np.arange(SEQ, dtype=np.int32)
    row = (t // GRID_W).astype(np.float32)
    col = (t % GRID_W).astype(np.float32)
    axis_dim = rot_dim // 2
    freqs = (np.float32(THETA) ** (-np.arange(0, axis_dim, 2, dtype=np.float32) / np.float32(axis_dim))).astype(np.float32)
    ang = np.concatenate([row[:, None] * freqs, col[:, None] * freqs], axis=-1).astype(np.float32)
    cos = np.ones((NR, rot_dim // 2), np.float32)
    sin = np.zeros((NR, rot_dim // 2), np.float32)
    cos[CTX:] = np.cos(ang)
    sin[CTX:] = np.sin(ang)
    cos = np.ascontiguousarray(cos.reshape(NT, 128, -1).transpose(1, 0, 2))
    sin = np.ascontiguousarray(sin.reshape(NT, 128, -1).transpose(1, 0, 2))
    return cos, sin


class _Stop(Exception):
    pass


import os
KSTOP = os.environ.get("KSTOP", "")


class Prog:
    def __init__(self, layers, first, final):
        self.layers = layers
        self.first = first
        self.final = final
        nc = self.nc = bass.Bass("TRN2", target_bir_lowering=False)
        ext = self.ext = {}

        def inp(name, shape, dt=F32):
            ext[name] = nc.dram_tensor(name, shape, dt, kind="ExternalInput").ap()
            return ext[name]

        inp("xa_in", [NS * NR, D])
        inp("ccT", [128, KC, NS + 1])
        inp("identf", [128, 128])
        inp("cosg", [128, NT, 64]); inp("sing", [128, NT, 64])
        inp("cosm", [128, NT, 32]); inp("sinm", [128, NT, 32])
        inp("offs", [64, 2])
        for l in layers:
            inp(f"ada_w{l}", [D, 6 * D]); inp(f"ada_b{l}", [1, 6 * D]); inp(f"ada_bT{l}", [128, 48])
            inp(f"lnmg{l}", [1, D]); inp(f"lnmb{l}", [1, D]); inp(f"lnfg{l}", [1, D]); inp(f"lnfb{l}", [1, D])
            inp(f"rw{l}", [D, E])
            inp(f"wg{l}", [E, D, D]); inp(f"wu{l}", [E, D, D]); inp(f"wd{l}", [E, D, D])
            if l % 2 == 0:
                inp(f"wqkv{l}", [D, 1536]); inp(f"qg{l}", [1, 128]); inp(f"kg{l}", [1, 128]); inp(f"wo{l}", [D, D])
            else:
                inp(f"wdq{l}", [D, 768]); inp(f"mqg{l}", [1, 768]); inp(f"wuq{l}", [768, 1536])
                inp(f"wdkv{l}", [D, 320]); inp(f"mkvg{l}", [1, 256]); inp(f"wukv{l}", [256, 2048]); inp(f"wo{l}", [D, D])
        if final:
            self.y = nc.dram_tensor("y", [NS * SEQ, D], F32, kind="ExternalOutput").ap()
        else:
            self.y = nc.dram_tensor("xa_out", [NS * NR, D], F32, kind="ExternalOutput").ap()
        self.XA = nc.dram_tensor("XA", [NS * NR, D], F32).ap()
        self.XB = nc.dram_tensor("XB", [NS * NR, D], F32).ap()
        self.FA = nc.dram_tensor("FA", [NS * NR, D], F32).ap()
        self.KT = nc.dram_tensor("KT", [NS, 9, 128, NR], BF16).ap()
        self.VS = nc.dram_tensor("VS", [NS, 8, 128, NT, 128], BF16).ap()
        self.QT = nc.dram_tensor("QT", [NS, 8, 128, NR], BF16).ap()
        self.QR = nc.dram_tensor("QR", [NS, 8, 128, NR], BF16).ap()
        self.AFL = nc.dram_tensor("AFL", [NS * E, SEQ], F32).ap()
        self.AFC = nc.dram_tensor("AFC", [NS * E, CTX], F32).ap()
        self.GS = nc.dram_tensor("GS", [2, NS + 1, 128, D], F32).ap()
        self.build()

    def build(self):
        nc = self.nc
        with ExitStack() as es:
            esems = {e: es.enter_context(nc.semaphore("s_" + e)) for e in ENGS}
            rings = {q: [es.enter_context(nc.semaphore(f"d_{q}{i}")) for i in range(8)] for q in ("sp", "pool")}
            S = self.S = Sched(nc, esems, rings)
            self.PA = es.enter_context(nc.psum_tensor("PA", [128, 1024], F32))
            self.PB = es.enter_context(nc.psum_tensor("PB", [128, 1024], F32))
            self.PC = es.enter_context(nc.psum_tensor("PC", [128, 1024], F32))
            self.PD = es.enter_context(nc.psum_tensor("PD", [128, 1024], F32))
            self.bPA = bufs(2)
            self.bPB = bufs(2)
            self.bPBh = self.bPB
            self.bPC = bufs(2)
            self.bPD = bufs(2)
            self.identF = es.enter_context(nc.sbuf_tensor("identF", [128, 128], F32))
            self.identB = es.enter_context(nc.sbuf_tensor("identB", [128, 128], BF16))
            self.onesB = es.enter_context(nc.sbuf_tensor("onesB", [128, 128], BF16))
            self.zeros = es.enter_context(nc.sbuf_tensor("zeros", [128, D], F32))
            self.modT = es.enter_context(nc.sbuf_tensor("modT", [128, 4, KC, 8], F32))
            self.bconst, self.bmodT = Buf(), Buf()
            self.bXA = bufs(NS); self.bXB = bufs(NS); self.bFAl = bufs(NS); self.bFAc = Buf()
            self.bKT = bufs(NS); self.bVS = bufs(NS); self.bQT = bufs(NS)
            self.bAF = Buf(); self.bGS = Buf(); self.bY = Buf()
            self.out_toks = []

            S.dma("sp", I_dma(self.identF[:], self.ext["identf"]), writes=[self.bconst])
            S.op("dve", I_copy(self.identB[:], self.identF[:]), reads=[self.bconst], writes=[self.bconst])
            S.op("pool", I_memset(self.onesB[:], 1.0), writes=[self.bconst])
            S.op("pool", I_memset(self.zeros[:], 0.0), writes=[self.bconst])
            for s in range(NS):
                S.dma("sp", I_dma(self.XA[s * NR:(s + 1) * NR, :], self.ext["xa_in"][s * NR:(s + 1) * NR, :]),
                      writes=[self.bXA[s]])
            S.run_block()

            nl = len(self.layers)
            try:
                for li, l in enumerate(self.layers):
                    last = self.final and li == nl - 1
                    self.phase_M(l)
                    self.phase_A1(l)
                    self.phase_A2(l, last)
                    self.stop("A")
                    self.phase_R(l, last)
                    self.stop("R")
                    self.phase_C(l, last)
                    self.stop("C")
                    self.phase_D(l, last, is_out=(li == nl - 1))
            except _Stop:
                pass

    def stop(self, tag):
        if KSTOP == tag and not self.S.muted:
            self.S.muted = True

    def phase_M(self, l):
        nc, S, ext = self.nc, self.S, self.ext
        NJ = NS + 1
        PD, PB = self.PD, self.PB
        with ExitStack() as es:
            T = lambda n, sh, dt: es.enter_context(nc.sbuf_tensor(f"{n}_L{l}", sh, dt))
            cc = T("m_cc", [128, KC, NJ], F32)
            sT = T("m_sT", [128, KC, NJ], F32)
            sTb = T("m_sTb", [128, KC, NJ], BF16)
            onesf = T("m_ones", [128, 128], F32)
            sBC = T("m_sBC", [128, KC, NJ, 128], BF16)
            W = [T(f"m_W{i}", [128, KC, 1024], BF16) for i in range(2)]
            bT = T("m_bT", [128, 48], F32)
            bBC = T("m_bBC", [128, 2, 1024], F32)
            Gst = [T(f"m_G{i}", [128, 1024], F32) for i in range(2)]
            b_cc, b_sT, b_sTb, b_ones, b_sBC, b_bT, b_bBC = bufs(7)
            b_W = bufs(2); b_G = bufs(2)
            S.dma("sp", I_dma(cc[:], ext["ccT"]), writes=[b_cc])
            S.dma("sp", I_dma(bT[:], ext[f"ada_bT{l}"]), writes=[b_bT])
            for gi, c0 in enumerate((2 * D, 5 * D)):
                S.dma("sp", I_dma(bBC[:, gi, :], ext[f"ada_b{l}"][0, c0:c0 + D].partition_broadcast(128)), writes=[b_bBC])
            S.op("act", I_act(sT[:], cc[:], AF.Silu), reads=[b_cc], writes=[b_sT])
            S.op("dve", I_copy(sTb[:], sT[:]), reads=[b_sT], writes=[b_sTb])
            S.op("pool", I_memset(onesf[:], 1.0), writes=[b_ones])
            for k in range(KC):
                for j in range(NJ):
                    S.op("dve", I_ts(sBC[:, k, j, :], onesf[:], sT[:, k, j:j + 1], None, ALU.mult),
                         reads=[b_ones, b_sT], writes=[b_sBC])
            wv = ext[f"ada_w{l}"].rearrange("(k p) c -> p k c", p=128)
            kinds = {0: 0, 1: 1, 3: 2, 4: 3}
            gcount = 0
            for cb in range(6):
                wb = W[cb % 2]; bw = b_W[cb % 2]
                S.dma("pool", I_dma(wb[:], wv[:, :, cb * 1024:(cb + 1) * 1024]), writes=[bw])
                if cb in kinds:
                    kind = kinds[cb]
                    for oc in range(KC):
                        for k in range(KC):
                            S.op("pe", I_mm(PD[:, oc * 8:oc * 8 + NJ], wb[:, k, oc * 128:(oc + 1) * 128], sTb[:, k, :],
                                            k == 0, k == KC - 1), reads=[bw, b_sTb], writes=[self.bPD[0]])
                    pv = PD[:, 0:64].rearrange("p (o j) -> p o j", j=8)[:, :, 0:NJ]
                    bb = bT[:, cb * 8:(cb + 1) * 8].unsqueeze(2).to_broadcast([128, KC, NJ])
                    S.op("dve", I_tt(self.modT[:, kind, :, 0:NJ], pv, bb, ALU.add),
                         reads=[self.bPD[0], b_bT], writes=[self.bmodT])
                    if kind in (1, 3):
                        S.op("dve", I_ts(self.modT[:, kind, :, 0:NJ], self.modT[:, kind, :, 0:NJ], 1.0, None, ALU.add),
                             reads=[self.bmodT], writes=[self.bmodT])
                else:
                    gi = 0 if cb == 2 else 1
                    for j in range(NJ):
                        gs = Gst[gcount % 2]; bg = b_G[gcount % 2]; gcount += 1
                        for half in range(2):
                            for k in range(KC):
                                S.op("pe", I_mm(PB[:, half * 512:(half + 1) * 512], sBC[:, k, j, :],
                                                wb[:, k, half * 512:(half + 1) * 512], k == 0, k == KC - 1),
                                     reads=[bw, b_sBC], writes=self.bPB)
                        S.op("dve", I_tt(gs[:], PB[:], bBC[:, gi, :], ALU.add), reads=self.bPB + [b_bBC], writes=[bg])
                        S.dma("sp", I_dma(self.GS[gi, j], gs[:]), reads=[bg], cowrites=[self.bGS])
            S.run_block()

    def transpose_mod(self, src, kind_sc, kind_sh, j, dst, b_src, b_dst, split=False):
        S, PA = self.S, self.PA
        bd = b_dst if isinstance(b_dst, list) else [b_dst] * KC
        for k in range(KC):
            S.op("pe", I_tr(PA[:, k * 128:(k + 1) * 128], src[:, k * 128:(k + 1) * 128], self.identF[:]),
                 reads=[b_src, self.bconst], writes=[self.bPA[k // 4]])
        for k in range(KC):
            sc = self.modT[:, kind_sc, k, j:j + 1]; sh = self.modT[:, kind_sh, k, j:j + 1]
            if split and k % 2 == 1:
                S.op("dve", I_ts(dst[:, k, :], PA[:, k * 128:(k + 1) * 128], sc, sh, ALU.mult, ALU.add),
                     reads=[self.bPA[k // 4], self.bmodT], writes=[bd[k]])
            else:
                S.op("act", I_act(dst[:, k, :], PA[:, k * 128:(k + 1) * 128], AF.Identity, scale=sc, bias=sh),
                     reads=[self.bPA[k // 4], self.bmodT], writes=[bd[k]])

    def rms_rstd(self, W, src_f, nh, hd, b_src):
        S = self.S
        sq, ss, lnv, rstd = W["sq"], W["ss"], W["lnv"], W["rstd"]
        n = nh * hd
        S.op("dve", I_tt(sq[:, 0:n], src_f[:, 0:n], src_f[:, 0:n], ALU.mult), reads=[b_src], writes=[W["b_sq"]])
        S.op("dve", lambda h: h.tensor_reduce(out=ss[:, 0:nh], in_=sq[:, 0:n].rearrange("p (h d) -> p h d", d=hd),
                                              axis=AX.X, op=ALU.add), reads=[W["b_sq"]], writes=[W["b_ss"]])
        S.op("act", I_act(lnv[:, 0:nh], ss[:, 0:nh], AF.Ln, scale=1.0 / hd, bias=W["epsc"][:, 0:1]),
             reads=[W["b_ss"], W["b_eps"]], writes=[W["b_lnv"]])
        S.op("act", I_act(rstd[:, 0:nh], lnv[:, 0:nh], AF.Exp, scale=-0.5), reads=[W["b_lnv"]], writes=[W["b_rstd"]])

    def rope(self, W, src, nh, hd, cos, sin, dst, b_src, b_dst, b_tab):
        S = self.S
        hp = hd // 2
        x0 = src[:, :, 0::2]; x1 = src[:, :, 1::2]
        cb = cos.unsqueeze(1).to_broadcast([128, nh, hp]); sb = sin.unsqueeze(1).to_broadcast([128, nh, hp])
        ra = W["ra"][:, 0:nh * hp].rearrange("p (h d) -> p h d", d=hp)
        rb = W["rb"][:, 0:nh * hp].rearrange("p (h d) -> p h d", d=hp)
        rc = W["rc"][:, 0:nh * hp].rearrange("p (h d) -> p h d", d=hp)
        rd = W["rd"][:, 0:nh * hp].rearrange("p (h d) -> p h d", d=hp)
        S.op("dve", I_tt(ra, x0, cb, ALU.mult), reads=[b_src, b_tab], writes=[W["b_ra"]])
        S.op("dve", I_tt(rb, x1, sb, ALU.mult), reads=[b_src, b_tab], writes=[W["b_rb"]])
        S.op("dve", I_tt(rc, x0, sb, ALU.mult), reads=[b_src, b_tab], writes=[W["b_rc"]])
        S.op("dve", I_tt(rd, x1, cb, ALU.mult), reads=[b_src, b_tab], writes=[W["b_rd"]])
        S.op("dve", I_tt(dst[:, :, 0::2], ra, rb, ALU.subtract), reads=[W["b_ra"], W["b_rb"]], writes=[b_dst])
        S.op("dve", I_tt(dst[:, :, 1::2], rc, rd, ALU.add), reads=[W["b_rc"], W["b_rd"]], writes=[b_dst])

    def resid_ln(self, W, P, x, G, lng, lnb, out, reads, b_out):
        S = self.S
        t = W["t"]; st = W["st"]; mv = W["mv"]; lnv = W["lnv1"]; rstd = W["rstd1"]; nmr = W["nmr"]
        S.op("dve", I_tt(t[:], P, G, ALU.mult), reads=reads, writes=[W["b_t"]])
        S.op("dve", I_stt(t[:], x, ALPHA, t[:], ALU.mult, ALU.add), reads=reads + [W["b_t"]], writes=[W["b_t"]])
        S.op("dve", lambda h: h.bn_stats(out=st[:, 0, :], in_=t[:, 0:512]), reads=[W["b_t"]], writes=[W["b_st"]])
        S.op("dve", lambda h: h.bn_stats(out=st[:, 1, :], in_=t[:, 512:1024]), reads=[W["b_t"]], writes=[W["b_st"]])
        S.op("dve", lambda h: h.bn_aggr(out=mv[:], in_=st[:].rearrange("p a b -> p (a b)")), reads=[W["b_st"]], writes=[W["b_mv"]])
        S.op("act", I_act(lnv[:], mv[:, 1:2], AF.Ln, scale=1.0, bias=W["epsc"][:, 0:1]), reads=[W["b_mv"], W["b_eps"]], writes=[W["b_lnv1"]])
        S.op("act", I_act(rstd[:], lnv[:], AF.Exp, scale=-0.5), reads=[W["b_lnv1"]], writes=[W["b_rstd1"]])
        S.op("dve", I_ts(nmr[:], mv[:, 0:1], rstd[:, 0:1], -1.0, ALU.mult, ALU.mult), reads=[W["b_mv"], W["b_rstd1"]], writes=[W["b_nmr"]])
        S.op("act", I_act(t[:], t[:], AF.Identity, scale=rstd[:, 0:1], bias=nmr[:, 0:1]),
             reads=[W["b_t"], W["b_rstd1"], W["b_nmr"]], writes=[W["b_t"]])
        S.op("dve", I_tt(t[:], t[:], lng, ALU.mult), reads=[W["b_t"], W["b_ln"]], writes=[W["b_t"]])
        S.op("pool", I_tt(out, t[:], lnb, ALU.add), reads=[W["b_t"], W["b_ln"]], writes=[b_out])

    def phase_A(self, l, last):
        nc, S, ext = self.nc, self.S, self.ext
        gqa = (l % 2 == 0)
        PA, PB, PC, PD = self.PA, self.PB, self.PC, self.PD
        PAb = PA[:].bitcast(BF16)
        SCALE = 128 ** -0.5 if gqa else 192 ** -0.5
        NKH = 2 if gqa else 8
        with ExitStack() as es:
            T = lambda n, sh, dt: es.enter_context(nc.sbuf_tensor(f"{n}_L{l}", sh, dt))
            wo = T("a_wo", [128, 8, D], BF16)
            b_w = Buf()
            S.dma("pool", I_dma(wo[:], ext[f"wo{l}"].rearrange("(h p) c -> p h c", p=128)), writes=[b_w])
            if gqa:
                wqkv = T("a_wqkv", [128, KC, 1536], BF16)
                for hf in range(2):
                    S.dma("pool", I_dma(wqkv[:, :, hf * 768:(hf + 1) * 768],
                                        ext[f"wqkv{l}"].rearrange("(k p) c -> p k c", p=128)[:, :, hf * 768:(hf + 1) * 768]), writes=[b_w])
                qg = T("a_qg", [128, 128], F32); kg = T("a_kg", [128, 128], F32)
                S.dma("sp", I_dma(qg[:], ext[f"qg{l}"][0, :].partition_broadcast(128)), writes=[b_w])
                S.dma("sp", I_dma(kg[:], ext[f"kg{l}"][0, :].partition_broadcast(128)), writes=[b_w])
                cosT = T("a_cos", [128, NT, 64], F32); sinT = T("a_sin", [128, NT, 64], F32)
                S.dma("sp", I_dma(cosT[:], ext["cosg"]), writes=[b_w])
                S.dma("sp", I_dma(sinT[:], ext["sing"]), writes=[b_w])
            else:
                wdq = T("a_wdq", [128, KC, 768], BF16)
                wuq = T("a_wuq", [128, 6, 1536], BF16)
                wdkv = T("a_wdkv", [128, KC, 320], BF16)
                wukv = T("a_wukv", [128, 2, 2048], BF16)
                S.dma("pool", I_dma(wdq[:], ext[f"wdq{l}"].rearrange("(k p) c -> p k c", p=128)), writes=[b_w])
                for hf in range(2):
                    S.dma("pool", I_dma(wuq[:, :, hf * 768:(hf + 1) * 768],
                                        ext[f"wuq{l}"].rearrange("(k p) c -> p k c", p=128)[:, :, hf * 768:(hf + 1) * 768]), writes=[b_w])
                S.dma("pool", I_dma(wdkv[:], ext[f"wdkv{l}"].rearrange("(k p) c -> p k c", p=128)), writes=[b_w])
                for hf in range(2):
                    S.dma("pool", I_dma(wukv[:, :, hf * 1024:(hf + 1) * 1024],
                                        ext[f"wukv{l}"].rearrange("(k p) c -> p k c", p=128)[:, :, hf * 1024:(hf + 1) * 1024]), writes=[b_w])
                mqg = T("a_mqg", [128, 768], F32); mkvg = T("a_mkvg", [128, 256], F32)
                S.dma("sp", I_dma(mqg[:], ext[f"mqg{l}"][0, :].partition_broadcast(128)), writes=[b_w])
                S.dma("sp", I_dma(mkvg[:], ext[f"mkvg{l}"][0, :].partition_broadcast(128)), writes=[b_w])
                cosT = T("a_cos", [128, NT, 32], F32); sinT = T("a_sin", [128, NT, 32], F32)
                S.dma("sp", I_dma(cosT[:], ext["cosm"]), writes=[b_w])
                S.dma("sp", I_dma(sinT[:], ext["sinm"]), writes=[b_w])
            lng = T("a_lng", [128, D], F32); lnb = T("a_lnb", [128, D], F32)
            wr = T("a_wr", [128, KC, E], F32)
            S.dma("sp", I_dma(lng[:], ext[f"lnmg{l}"][0, :].partition_broadcast(128)), writes=[b_w])
            S.dma("sp", I_dma(lnb[:], ext[f"lnmb{l}"][0, :].partition_broadcast(128)), writes=[b_w])
            S.dma("sp", I_dma(wr[:], ext[f"rw{l}"].rearrange("(k p) e -> p k e", p=128)), writes=[b_w])
            G1c = T("a_G1c", [128, D], F32); G1l = T("a_G1l", [128, D], F32)
            b_G1c, b_G1l = bufs(2)
            S.dma("sp", I_dma(G1c[:], self.GS[0, NS]), reads=[self.bGS], writes=[b_G1c])
            W = {}
            for n, sh in (("sq", [128, 1024]), ("ss", [128, 8]), ("lnv", [128, 8]), ("rstd", [128, 8]),
                          ("t", [128, D]), ("st", [128, 2, 6]), ("mv", [128, 2]), ("lnv1", [128, 1]),
                          ("rstd1", [128, 1]), ("nmr", [128, 1]), ("epsc", [128, 1])):
                W[n] = T("a_" + n, sh, F32)
            for n in ("sq", "ss", "lnv", "rstd", "t", "st", "mv", "lnv1", "rstd1", "nmr"):
                W["b_" + n] = Buf()
            W["b_ln"] = b_w
            W["b_eps"] = Buf()
            S.op("pool", I_memset(W["epsc"][:], EPS), writes=[W["b_eps"]])
            hTt = T("a_hTt", [128, KC, 128], BF16); b_hTt = bufs(KC)
            pf = T("a_pf", [128, 1024], F32); b_pf = Buf()
            pn = W["sq"]; b_pn = W["b_sq"]
            pr = T("a_pr", [128, 1024], BF16); b_pr = Buf()
            vt = T("a_vt", [128, 1024], BF16); b_vt = Buf()
            ktt = T("a_ktt", [128, 9, 128], BF16); b_ktt = Buf()
            xblk = T("a_xblk", [128, 4, D], F32); b_xblk = bufs(4)
            xin = [xblk[:, 0, :], xblk[:, 1, :]]; b_xin = b_xblk[0:2]
            qTb = T("a_qTb", [128, 8, 512], BF16); b_qTb = Buf()
            oTraw = T("a_oTb", [128, 8 * 512], BF16); b_oTb = Buf()
            oTb = oTraw[:].rearrange("p (h n) -> p h n", n=512)
            rcrd = oTraw[:, 0:2048].bitcast(F32)
            kTg = [T(f"a_kTg{i}", [128, NR], BF16) for i in range(2)]; b_kTg = bufs(2)
            Vg = [T(f"a_Vg{i}", [128, NT, 128], BF16) for i in range(2)]; b_Vg = bufs(2)
            QS = 512 if gqa else 256
            PT = [T(f"a_PT{i}", [128, NT, QS], BF16) for i in range(2)]; b_PT = [bufs(NT) for _ in range(2)]
            rden = T("a_rden", [128, 512], F32); b_rden = Buf()
            h2T = pf[:].rearrange("p (k n) -> p k n", n=128); b_h2T = b_pf
            W["ra"] = W["t"][:, 0:512]; W["rb"] = W["t"][:, 512:1024]; W["b_ra"] = W["b_rb"] = W["b_t"]
            W["rc"] = rcrd[:, 0:512]; W["rd"] = rcrd[:, 512:1024]; W["b_rc"] = W["b_rd"] = b_oTb
            sm = {n: T("a_sm_" + n, sh, F32) for n, sh in (("mx", [128, 1]), ("nmx", [128, 1]), ("ex", [128, E]),
                                                           ("ssum", [128, 1]), ("rs", [128, 1]), ("aff", [128, E]))}
            b_sm = {n: Buf() for n in sm}
            affT = T("a_affT", [16, 128], F32); b_affT = Buf()
            if not gqa:
                cqT = T("a_cqT", [128, 6, 512], BF16); b_cqT = Buf()
                ckT = T("a_ckT", [128, 2, 128], BF16); b_ckT = Buf()
                qrT = T("a_qrT", [128, 8, 512], BF16); b_qrT = Buf()
                krT = T("a_krT", [128, NR], BF16); b_krT = Buf()
                kr2 = T("a_kr2", [128, 128], BF16); b_kr2 = Buf()
                qr2 = T("a_qr2", [128, 1024], BF16); b_qr2 = Buf()
                S.op("pool", I_memset(qr2[:], 0.0), writes=[b_qr2])
            kvload = 0
            x1cnt = 0
            self.stop("w")
            for s in range(NS):
                par = s % 2
                S.dma("sp", I_dma(G1l[:], self.GS[0, s]), reads=[self.bGS], writes=[b_G1l])
                for t in range(NT):
                    j = NS if t < 2 else s
                    xi = xin[t % 2]; bx = b_xin[t % 2]
                    r0 = s * NR + t * 128
                    S.dma("sp", I_dma(xi, self.XA[r0:r0 + 128, :]), reads=[self.bXA[s]], writes=[bx])
                    self.transpose_mod(xi, 1, 0, j, hTt, bx, b_hTt)
                    cos_t = cosT[:, t, :]; sin_t = sinT[:, t, :]
                    if gqa:
                        for k in range(KC):
                            S.op("pe", I_mm(PB[:, 0:512], hTt[:, k, :], wqkv[:, k, 1024:1536], k == 0, k == KC - 1),
                                 reads=[b_hTt[k], b_w], writes=self.bPB)
                        S.op("act", I_acopy(pf[:, 0:256], PB[:, 0:256]), reads=self.bPB, writes=[b_pf])
                        S.op("act", I_acopy(vt[:, 0:256], PB[:, 256:512]), reads=self.bPB, writes=[b_vt])
                        self.rms_rstd(W, pf, 2, 128, b_pf)
                        pf3 = pf[:, 0:256].rearrange("p (h d) -> p h d", d=128)
                        pn3 = pn[:, 0:256].rearrange("p (h d) -> p h d", d=128)
                        S.op("dve", I_tt(pn3, pf3, W["rstd"][:, 0:2].unsqueeze(2).to_broadcast([128, 2, 128]), ALU.mult),
                             reads=[b_pf, W["b_rstd"]], writes=[b_pn])
                        S.op("pool", I_tt(pn3, pn3, kg[:].unsqueeze(1).to_broadcast([128, 2, 128]), ALU.mult),
                             reads=[b_pn, b_w], writes=[b_pn])
                        pr3 = pr[:, 0:256].rearrange("p (h d) -> p h d", d=128)
                        self.rope(W, pn3, 2, 128, cos_t, sin_t, pr3, b_pn, b_pr, b_w)
                        for g in range(2):
                            S.op("pe", I_tr(PAb[:, g * 128:(g + 1) * 128], pr[:, g * 128:(g + 1) * 128], self.identB[:]),
                                 reads=[b_pr, self.bconst], writes=self.bPA)
                        S.op("dve", I_copy(ktt[:, 0:2, :], PAb[:, 0:256].rearrange("p (g n) -> p g n", n=128)),
                             reads=self.bPA, writes=[b_ktt])
                        S.dma("sp", I_dma(self.KT[par, 0:2, :, t * 128:(t + 1) * 128].rearrange("g d n -> d g n"), ktt[:, 0:2, :]),
                              reads=[b_ktt], cowrites=[self.bKT[par]])
                        S.dma("sp", I_dma(self.VS[par, 0:2, :, t, :].rearrange("g p d -> p g d"),
                                          vt[:, 0:256].rearrange("p (g d) -> p g d", d=128)),
                              reads=[b_vt], cowrites=[self.bVS[par]])
                    else:
                        for k in range(KC):
                            S.op("pe", I_mm(PB[:, 0:320], hTt[:, k, :], wdkv[:, k, :], k == 0, k == KC - 1),
                                 reads=[b_hTt[k], b_w], writes=self.bPB)
                        S.op("act", I_acopy(pf[:, 0:320], PB[:, 0:320]), reads=self.bPB, writes=[b_pf])
                        self.stop("p1a")
                        self.rms_rstd(W, pf, 1, 256, b_pf)
                        S.op("dve", I_ts(pn[:, 0:256], pf[:, 0:256], W["rstd"][:, 0:1], None, ALU.mult),
                             reads=[b_pf, W["b_rstd"]], writes=[b_pn])
                        S.op("pool", I_tt(pr[:, 0:256], pn[:, 0:256], mkvg[:], ALU.mult), reads=[b_pn, b_w], writes=[b_pr])
                        kr3 = kr2[:, 0:64].rearrange("p (h d) -> p h d", d=64)
                        self.rope(W, pf[:, 256:320].rearrange("p (h d) -> p h d", d=64), 1, 64, cos_t, sin_t, kr3, b_pf, b_kr2, b_w)
                        S.op("pool", I_copy(kr2[:, 64:128], kr2[:, 0:64]), reads=[b_kr2], writes=[b_kr2])
                        self.stop("p1b")
                        for c in range(2):
                            S.op("pe", I_tr(PAb[:, c * 128:(c + 1) * 128], pr[:, c * 128:(c + 1) * 128], self.identB[:]),
                                 reads=[b_pr, self.bconst], writes=self.bPA)
                        S.op("pe", I_tr(PAb[:, 256:384], kr2[:], self.identB[:]), reads=[b_kr2, self.bconst], writes=self.bPA)
                        S.op("dve", I_copy(ckT[:], PAb[:, 0:256].rearrange("p (c n) -> p c n", n=128)), reads=self.bPA, writes=[b_ckT])
                        S.op("dve", I_copy(krT[:, t * 128:(t + 1) * 128], PAb[:, 256:384]), reads=self.bPA, writes=[b_krT])
                        self.stop("p1c")
                        wukv4 = wukv[:].rearrange("p c (h x) -> p c h x", x=256)
                        for hh in range(8):
                            for c in range(2):
                                S.op("pe", I_mm(PC[:, hh * 128:(hh + 1) * 128], wukv4[:, c, hh, 0:128], ckT[:, c, :], c == 0, c == 1),
                                     reads=[b_ckT, b_w], writes=[self.bPC[hh // 4]])
                        S.op("dve", I_copy(ktt[:, 0:8, :], PC[:].rearrange("p (h n) -> p h n", n=128)),
                             reads=self.bPC, writes=[b_ktt])
                        self.stop("p1d")
                        for half in range(2):
                            for c in range(2):
                                S.op("pe", I_mm(PB[:, half * 512:(half + 1) * 512], ckT[:, c, :],
                                                wukv4[:, c, half * 4:(half + 1) * 4, 128:256], c == 0, c == 1),
                                     reads=[b_ckT, b_w], writes=self.bPB)
                        S.op("act", I_acopy(vt[:], PB[:]), reads=self.bPB, writes=[b_vt])
                        self.stop("p1e")
                        S.dma("sp", I_dma(self.KT[par, 0:8, :, t * 128:(t + 1) * 128].rearrange("g d n -> d g n"), ktt[:, 0:8, :]),
                              reads=[b_ktt], cowrites=[self.bKT[par]])
                        S.dma("sp", I_dma(self.VS[par, :, :, t, :].rearrange("g p d -> p g d"),
                                          vt[:].rearrange("p (g d) -> p g d", d=128)),
                              reads=[b_vt], cowrites=[self.bVS[par]])
                self.stop("p1")
                blocks = ([] if last else [[0, 1]]) + [[2 + 4 * b + i for i in range(4)] for b in range(4)]
                for tiles in blocks:
                    isctx = tiles[0] < 2
                    j = NS if isctx else s
                    nq = len(tiles) * 128
                    nkc = 2 if isctx else NT
                    G1 = G1c if isctx else G1l
                    bG1 = b_G1c if isctx else b_G1l
                    for ti, t in enumerate(tiles):
                        r0 = s * NR + t * 128
                        S.dma("sp", I_dma(xblk[:, ti, :], self.XA[r0:r0 + 128, :]), reads=[self.bXA[s]], writes=[b_xblk[ti]])
                        self.transpose_mod(xblk[:, ti, :], 1, 0, j, hTt, b_xblk[ti], b_hTt)
                        cos_t = cosT[:, t, :]; sin_t = sinT[:, t, :]
                        if gqa:
                            for half in range(2):
                                for k in range(KC):
                                    S.op("pe", I_mm(PB[:, half * 512:(half + 1) * 512], hTt[:, k, :],
                                                    wqkv[:, k, half * 512:(half + 1) * 512], k == 0, k == KC - 1),
                                         reads=[b_hTt[k], b_w], writes=self.bPB)
                            S.op("act", I_acopy(pf[:], PB[:]), reads=self.bPB, writes=[b_pf])
                            self.rms_rstd(W, pf, 8, 128, b_pf)
                            pf3 = pf[:].rearrange("p (h d) -> p h d", d=128)
                            pn3 = pn[:].rearrange("p (h d) -> p h d", d=128)
                            S.op("dve", I_tt(pn3, pf3, W["rstd"][:, 0:8].unsqueeze(2).to_broadcast([128, 8, 128]), ALU.mult),
                                 reads=[b_pf, W["b_rstd"]], writes=[b_pn])
                            S.op("pool", I_tt(pn3, pn3, qg[:].unsqueeze(1).to_broadcast([128, 8, 128]), ALU.mult),
                                 reads=[b_pn, b_w], writes=[b_pn])
                            pr3 = pr[:].rearrange("p (h d) -> p h d", d=128)
                            self.rope(W, pn3, 8, 128, cos_t, sin_t, pr3, b_pn, b_pr, b_w)
                            for hh in range(8):
                                S.op("pe", I_tr(PAb[:, hh * 128:(hh + 1) * 128], pr[:, hh * 128:(hh + 1) * 128], self.identB[:]),
                                     reads=[b_pr, self.bconst], writes=self.bPA)
                            S.op("dve", I_copy(qTb[:, :, ti * 128:(ti + 1) * 128], PAb[:, 0:1024].rearrange("p (h n) -> p h n", n=128)),
                                 reads=self.bPA, writes=[b_qTb])
                        else:
                            for c0, c1 in ((0, 512), (512, 768)):
                                for k in range(KC):
                                    S.op("pe", I_mm(PB[:, c0:c1], hTt[:, k, :], wdq[:, k, c0:c1], k == 0, k == KC - 1),
                                         reads=[b_hTt[k], b_w], writes=self.bPB)
                            S.op("act", I_acopy(pf[:, 0:768], PB[:, 0:768]), reads=self.bPB, writes=[b_pf])
                            self.rms_rstd(W, pf, 1, 768, b_pf)
                            S.op("dve", I_ts(pn[:, 0:768], pf[:, 0:768], W["rstd"][:, 0:1], None, ALU.mult),
                                 reads=[b_pf, W["b_rstd"]], writes=[b_pn])
                            S.op("pool", I_tt(pr[:, 0:768], pn[:, 0:768], mqg[:], ALU.mult), reads=[b_pn, b_w], writes=[b_pr])
                            for c in range(6):
                                S.op("pe", I_tr(PAb[:, c * 128:(c + 1) * 128], pr[:, c * 128:(c + 1) * 128], self.identB[:]),
                                     reads=[b_pr, self.bconst], writes=self.bPA)
                            S.op("dve", I_copy(cqT[:, :, ti * 128:(ti + 1) * 128], PAb[:, 0:768].rearrange("p (c n) -> p c n", n=128)),
                                 reads=self.bPA, writes=[b_cqT])
                            wuq4 = wuq[:].rearrange("p c (h x) -> p c h x", x=192)
                            for c in range(6):
                                S.op("pe", I_mm(PB[:, 0:512], cqT[:, c, ti * 128:(ti + 1) * 128], wuq4[:, c, :, 128:192], c == 0, c == 5),
                                     reads=[b_cqT, b_w], writes=self.bPB)
                            S.op("act", I_acopy(pf[:, 0:512], PB[:, 0:512]), reads=self.bPB, writes=[b_pf])
                            src8 = pf[:, 0:512].rearrange("p (h d) -> p h d", d=64)
                            dst4 = qr2[:].rearrange("p (c x) -> p c x", x=256)
                            self.rope(W, src8[:, 0::2, :], 4, 64, cos_t, sin_t, dst4[:, :, 0:64], b_pf, b_qr2, b_w)
                            self.rope(W, src8[:, 1::2, :], 4, 64, cos_t, sin_t, dst4[:, :, 192:256], b_pf, b_qr2, b_w)
                            for c in range(8):
                                S.op("pe", I_tr(PAb[:, c * 128:(c + 1) * 128], qr2[:, c * 128:(c + 1) * 128], self.identB[:]),
                                     reads=[b_qr2, self.bconst], writes=self.bPA)
                            S.op("dve", I_copy(qrT[:, :, ti * 128:(ti + 1) * 128], PAb[:, 0:1024].rearrange("p (c n) -> p c n", n=128)),
                                 reads=self.bPA, writes=[b_qrT])
                    if not gqa:
                        wuq4 = wuq[:].rearrange("p c (h x) -> p c h x", x=192)
                        for hh in range(8):
                            pc = PC[:, (hh % 2) * 512:(hh % 2) * 512 + nq]
                            for c in range(6):
                                S.op("pe", I_mm(pc, wuq4[:, c, hh, 0:128], cqT[:, c, 0:nq], c == 0, c == 5),
                                     reads=[b_cqT, b_w], writes=[self.bPC[hh % 2]])
                            S.op("act", I_acopy(qTb[:, hh, 0:nq], pc), reads=[self.bPC[hh % 2]], writes=[b_qTb])
                    self.stop("q")
                    nk = nkc * 128
                    kvslot = {}

                    def load_kv(g):
                        nonlocal kvload
                        kb = kvload % 2; kvload += 1
                        kvslot[g] = kb
                        S.dma("sp", I_dma(kTg[kb][:, 0:nk], self.KT[par, g, :, 0:nk]), reads=[self.bKT[par]], writes=[b_kTg[kb]])
                        S.dma("sp", I_dma(Vg[kb][:, 0:nkc, :], self.VS[par, g, :, 0:nkc, :]), reads=[self.bVS[par]], writes=[b_Vg[kb]])

                    units = []
                    for g in range(NKH):
                        for hh in ([g * 4 + i for i in range(4)] if gqa else [g]):
                            for q0 in range(0, nq, QS):
                                units.append((g, hh, q0, min(QS, nq - q0)))

                    def emit_S(u, i, kc):
                        g, hh, q0, qn = u
                        kb = kvslot[g]
                        pc = PC[:, (kc % 2) * 512:(kc % 2) * 512 + qn]
                        if gqa:
                            S.op("pe", I_mm(pc, kTg[kb][:, kc * 128:(kc + 1) * 128], qTb[:, hh, q0:q0 + qn], True, True),
                                 reads=[b_kTg[kb], b_qTb], writes=[self.bPC[kc % 2]])
                        else:
                            S.op("pe", I_mm(pc, kTg[kb][:, kc * 128:(kc + 1) * 128], qTb[:, hh, q0:q0 + qn], True, False),
                                 reads=[b_kTg[kb], b_qTb], writes=[self.bPC[kc % 2]])
                            S.op("pe", I_mm(pc, krT[:, kc * 128:(kc + 1) * 128], qrT[:, hh, q0:q0 + qn], False, True),
                                 reads=[b_krT, b_qrT], writes=[self.bPC[kc % 2]])
                        S.op("act", I_act(PT[i % 2][:, kc, 0:qn], pc, AF.Exp, scale=SCALE), reads=[self.bPC[kc % 2]],
                             writes=[b_PT[i % 2][kc]])

                    def emit_PV(u, i, kc):
                        g, hh, q0, qn = u
                        kb = kvslot[g]
                        acc, bacc = (PD, self.bPD) if i % 2 == 0 else (PB, self.bPB)
                        S.op("pe", I_mm(acc[:, 0:qn], Vg[kb][:, kc, :], PT[i % 2][:, kc, 0:qn], kc == 0, kc == nkc - 1),
                             reads=[b_Vg[kb], b_PT[i % 2][kc]], writes=[bacc[0]])
                        S.op("pe", I_mm(acc[:, 512:512 + qn], self.onesB[:], PT[i % 2][:, kc, 0:qn], kc == 0, kc == nkc - 1),
                             reads=[self.bconst, b_PT[i % 2][kc]], writes=[bacc[1]])

                    def emit_norm(u, i):
                        g, hh, q0, qn = u
                        acc, bacc = (PD, self.bPD) if i % 2 == 0 else (PB, self.bPB)
                        S.op("dve", lambda h: h.reciprocal(out=rden[:, 0:qn], in_=acc[:, 512:512 + qn]),
                             reads=[bacc[1]], writes=[b_rden])
                        S.op("dve", I_tt(oTb[:, hh, q0:q0 + qn], acc[:, 0:qn], rden[:, 0:qn], ALU.mult),
                             reads=[bacc[0], b_rden], writes=[b_oTb])

                    load_kv(0)
                    for kc in range(nkc):
                        emit_S(units[0], 0, kc)
                    for i, u in enumerate(units):
                        nxt = units[i + 1] if i + 1 < len(units) else None
                        if (i == 0 or units[i - 1][0] != u[0]) and u[0] + 1 < NKH:
                            load_kv(u[0] + 1)
                        for kc in range(nkc):
                            if nxt is not None:
                                emit_S(nxt, i + 1, kc)
                            emit_PV(u, i, kc)
                        emit_norm(u, i)
                    self.stop("att")
                    for ti, t in enumerate(tiles):
                        r0 = s * NR + t * 128
                        for half in range(2):
                            for hh in range(8):
                                S.op("pe", I_mm(PB[:, half * 512:(half + 1) * 512], oTb[:, hh, ti * 128:(ti + 1) * 128],
                                                wo[:, hh, half * 512:(half + 1) * 512], hh == 0, hh == 7),
                                     reads=[b_oTb, b_w], writes=self.bPB)
                        xo = xblk[:, ti, :]; bxo = b_xblk[ti]
                        self.resid_ln(W, PB[:], xblk[:, ti, :], G1[:], lng[:], lnb[:], xo,
                                      self.bPB + [b_xblk[ti], bG1], bxo)
                        S.dma("sp", I_dma(self.XB[r0:r0 + 128, :], xo), reads=[bxo], cowrites=[self.bXB[s]])
                        S.dma("sp", I_dma(self.FA[r0:r0 + 128, :], self.zeros[:]), reads=[self.bconst],
                              cowrites=[self.bFAc if isctx else self.bFAl[s]])
                        self.transpose_mod(xo, 3, 2, j, h2T, bxo, b_h2T, split=False)
                        for k in range(KC):
                            S.op("pe", I_mm(PB[:, 0:E], h2T[:, k, :], wr[:, k, :], k == 0, k == KC - 1),
                                 reads=[b_h2T, b_w], writes=self.bPB)
                        S.op("dve", lambda h: h.tensor_reduce(out=sm["mx"][:], in_=PB[:, 0:E], axis=AX.X, op=ALU.max),
                             reads=self.bPB, writes=[b_sm["mx"]])
                        S.op("dve", I_ts(sm["nmx"][:], sm["mx"][:], -1.0, None, ALU.mult), reads=[b_sm["mx"]], writes=[b_sm["nmx"]])
                        S.op("act", I_act(sm["ex"][:], PB[:, 0:E], AF.Exp, scale=1.0, bias=sm["nmx"][:, 0:1], accum_out=sm["ssum"][:]),
                             reads=self.bPB + [b_sm["nmx"]], writes=[b_sm["ex"], b_sm["ssum"]])
                        S.op("dve", lambda h: h.reciprocal(out=sm["rs"][:], in_=sm["ssum"][:]), reads=[b_sm["ssum"]], writes=[b_sm["rs"]])
                        S.op("dve", I_ts(sm["aff"][:], sm["ex"][:], sm["rs"][:, 0:1], None, ALU.mult),
                             reads=[b_sm["ex"], b_sm["rs"]], writes=[b_sm["aff"]])
                        S.op("pe", I_tr(PA[0:E, 0:128], sm["aff"][:], self.identF[:]), reads=[b_sm["aff"], self.bconst], writes=self.bPA)
                        S.op("act", I_acopy(affT[:], PA[0:E, 0:128]), reads=self.bPA, writes=[b_affT])
                        if isctx:
                            dst = self.AFC[s * E:(s + 1) * E, t * 128:(t + 1) * 128]
                        else:
                            dst = self.AFL[s * E:(s + 1) * E, (t - 2) * 128:(t - 1) * 128]
                        S.dma("sp", I_dma(dst, affT[:]), reads=[b_affT], cowrites=[self.bAF])
            S.run_block()

    def phase_A1(self, l):
        from itertools import zip_longest
        nc, S, ext = self.nc, self.S, self.ext
        gqa = (l % 2 == 0)
        PA, PB, PC, PD = self.PA, self.PB, self.PC, self.PD
        PAb = PA[:].bitcast(BF16)
        with ExitStack() as es:
            T = lambda n, sh, dt: es.enter_context(nc.sbuf_tensor(f"{n}_L{l}", sh, dt))
            b_w = Buf()
            if gqa:
                wqkv = T("p_wqkv", [128, KC, 1536], BF16)
                for hf in range(2):
                    S.dma("pool", I_dma(wqkv[:, :, hf * 768:(hf + 1) * 768],
                                        ext[f"wqkv{l}"].rearrange("(k p) c -> p k c", p=128)[:, :, hf * 768:(hf + 1) * 768]), writes=[b_w])
                qkg = T("p_qkg", [128, 10, 128], F32)
                for hh in range(10):
                    src = ext[f"qg{l}"] if hh < 8 else ext[f"kg{l}"]
                    S.dma("sp", I_dma(qkg[:, hh, :], src[0, :].partition_broadcast(128)), writes=[b_w])
                cosT = T("p_cos", [128, NT, 64], F32); sinT = T("p_sin", [128, NT, 64], F32)
                S.dma("sp", I_dma(cosT[:], ext["cosg"]), writes=[b_w]); S.dma("sp", I_dma(sinT[:], ext["sing"]), writes=[b_w])
            else:
                wdq = T("p_wdq", [128, KC, 768], BF16); wuq = T("p_wuq", [128, 6, 1536], BF16)
                wdkv = T("p_wdkv", [128, KC, 320], BF16); wukv = T("p_wukv", [128, 2, 2048], BF16)
                S.dma("pool", I_dma(wdq[:], ext[f"wdq{l}"].rearrange("(k p) c -> p k c", p=128)), writes=[b_w])
                for hf in range(2):
                    S.dma("pool", I_dma(wuq[:, :, hf * 768:(hf + 1) * 768],
                                        ext[f"wuq{l}"].rearrange("(k p) c -> p k c", p=128)[:, :, hf * 768:(hf + 1) * 768]), writes=[b_w])
                S.dma("pool", I_dma(wdkv[:], ext[f"wdkv{l}"].rearrange("(k p) c -> p k c", p=128)), writes=[b_w])
                for hf in range(2):
                    S.dma("pool", I_dma(wukv[:, :, hf * 1024:(hf + 1) * 1024],
                                        ext[f"wukv{l}"].rearrange("(k p) c -> p k c", p=128)[:, :, hf * 1024:(hf + 1) * 1024]), writes=[b_w])
                mqg = T("p_mqg", [128, 768], F32); mkvg = T("p_mkvg", [128, 256], F32)
                S.dma("sp", I_dma(mqg[:], ext[f"mqg{l}"][0, :].partition_broadcast(128)), writes=[b_w])
                S.dma("sp", I_dma(mkvg[:], ext[f"mkvg{l}"][0, :].partition_broadcast(128)), writes=[b_w])
                cosT = T("p_cos", [128, NT, 32], F32); sinT = T("p_sin", [128, NT, 32], F32)
                S.dma("sp", I_dma(cosT[:], ext["cosm"]), writes=[b_w]); S.dma("sp", I_dma(sinT[:], ext["sinm"]), writes=[b_w])
                wukv4 = wukv[:].rearrange("p c (h x) -> p c h x", x=256)
                wuq4 = wuq[:].rearrange("p c (h x) -> p c h x", x=192)
            epsc = T("p_eps", [128, 1], F32); b_eps = Buf()
            S.op("pool", I_memset(epsc[:], EPS), writes=[b_eps])

            def mkW(tag, n):
                W = {"epsc": epsc, "b_eps": b_eps}
                for nm, sh in (("sq", [128, n]), ("ss", [128, 16]), ("lnv", [128, 16]), ("rstd", [128, 16]),
                               ("ra", [128, n // 2]), ("rb", [128, n // 2]), ("rc", [128, n // 2]), ("rd", [128, n // 2])):
                    W[nm] = T(f"p_{tag}_{nm}", sh, F32)
                    W["b_" + nm] = Buf()
                return W

            NSLOT = 4 if gqa else 3
            slots = []
            for i in range(NSLOT):
                d = {"xin": T(f"p_xin{i}", [128, D], F32), "b_xin": Buf(),
                     "hTt": T(f"p_hTt{i}", [128, KC, 128], BF16), "b_hTt": bufs(KC),
                     "vt": T(f"p_vt{i}", [128, 1024], BF16), "b_vt": Buf(),
                     "qtt": T(f"p_qtt{i}", [128, 8, 128], BF16), "b_qtt": Buf(),
                     "ktt": T(f"p_ktt{i}", [128, 9, 128], BF16), "b_ktt": Buf()}
                if gqa:
                    d["W"] = mkW(f"w{i}", 1280)
                    d["pf"] = T(f"p_pf{i}", [128, 1280], F32); d["b_pf"] = Buf()
                    d["pr"] = T(f"p_pr{i}", [128, 1280], BF16); d["b_pr"] = Buf()
                else:
                    d["Wk"] = mkW(f"wk{i}", 256); d["Wq"] = mkW(f"wq{i}", 768)
                    d["pfk"] = T(f"p_pfk{i}", [128, 320], F32); d["b_pfk"] = Buf()
                    d["prk"] = T(f"p_prk{i}", [128, 256], BF16); d["b_prk"] = Buf()
                    d["kr2"] = T(f"p_kr2{i}", [128, 128], BF16); d["b_kr2"] = Buf()
                    d["ckT"] = T(f"p_ckT{i}", [128, 2, 128], BF16); d["b_ckT"] = Buf()
                    d["pfq"] = T(f"p_pfq{i}", [128, 768], F32); d["b_pfq"] = Buf()
                    d["prq"] = T(f"p_prq{i}", [128, 768], BF16); d["b_prq"] = Buf()
                    d["cqT"] = T(f"p_cqT{i}", [128, 6, 128], BF16); d["b_cqT"] = Buf()
                    d["pfr"] = T(f"p_pfr{i}", [128, 512], F32); d["b_pfr"] = Buf()
                    d["qr2"] = T(f"p_qr2{i}", [128, 1024], BF16); d["b_qr2"] = Buf()
                    d["qrt"] = T(f"p_qrt{i}", [128, 8, 128], BF16); d["b_qrt"] = Buf()
                    S.op("pool", I_memset(d["qr2"][:], 0.0), writes=[d["b_qr2"]])
                slots.append(d)

            def chain_gqa(s, t, d):
                j = NS if t < 2 else s
                r0 = s * NR + t * 128
                W = d["W"]; pf = d["pf"]; pr = d["pr"]; sq = W["sq"]
                S.dma("sp", I_dma(d["xin"][:], self.XA[r0:r0 + 128, :]), reads=[self.bXA[s]], writes=[d["b_xin"]])
                self.transpose_mod(d["xin"], 1, 0, j, d["hTt"], d["b_xin"], d["b_hTt"])
                yield
                for half in range(2):
                    for k in range(KC):
                        S.op("pe", I_mm(PB[:, half * 512:(half + 1) * 512], d["hTt"][:, k, :], wqkv[:, k, half * 512:(half + 1) * 512],
                                        k == 0, k == KC - 1), reads=[d["b_hTt"][k], b_w], writes=[self.bPB[half]])
                for k in range(KC):
                    S.op("pe", I_mm(PC[:, 0:512], d["hTt"][:, k, :], wqkv[:, k, 1024:1536], k == 0, k == KC - 1),
                         reads=[d["b_hTt"][k], b_w], writes=[self.bPC[0]])
                S.op("act", I_acopy(pf[:, 0:1024], PB[:]), reads=self.bPB, writes=[d["b_pf"]])
                S.op("act", I_acopy(pf[:, 1024:1280], PC[:, 0:256]), reads=[self.bPC[0]], writes=[d["b_pf"]])
                S.op("act", I_acopy(d["vt"][:, 0:256], PC[:, 256:512]), reads=[self.bPC[0]], writes=[d["b_vt"]])
                yield
                self.rms_rstd(W, pf, 10, 128, d["b_pf"])
                yield
                pf3 = pf[:].rearrange("p (h d) -> p h d", d=128); pn3 = sq[:].rearrange("p (h d) -> p h d", d=128)
                S.op("dve", I_tt(pn3, pf3, W["rstd"][:, 0:10].unsqueeze(2).to_broadcast([128, 10, 128]), ALU.mult),
                     reads=[d["b_pf"], W["b_rstd"]], writes=[W["b_sq"]])
                S.op("dve", I_tt(pn3, pn3, qkg[:], ALU.mult), reads=[W["b_sq"], b_w], writes=[W["b_sq"]])
                yield
                pr3 = pr[:].rearrange("p (h d) -> p h d", d=128)
                self.rope(W, pn3, 10, 128, cosT[:, t, :], sinT[:, t, :], pr3, W["b_sq"], d["b_pr"], b_w)
                yield
                for hh in range(10):
                    S.op("pe", I_tr(PAb[:, hh * 128:(hh + 1) * 128], pr[:, hh * 128:(hh + 1) * 128], self.identB[:]),
                         reads=[d["b_pr"], self.bconst], writes=self.bPA)
                S.op("dve", I_copy(d["qtt"][:], PAb[:, 0:1024].rearrange("p (h n) -> p h n", n=128)), reads=self.bPA, writes=[d["b_qtt"]])
                S.op("dve", I_copy(d["ktt"][:, 0:2, :], PAb[:, 1024:1280].rearrange("p (h n) -> p h n", n=128)), reads=self.bPA, writes=[d["b_ktt"]])
                yield
                cs = slice(t * 128, (t + 1) * 128)
                S.dma("sp", I_dma(self.QT[s, :, :, cs].rearrange("h d n -> d h n"), d["qtt"][:]), reads=[d["b_qtt"]], cowrites=[self.bQT[s]])
                S.dma("sp", I_dma(self.KT[s, 0:2, :, cs].rearrange("g d n -> d g n"), d["ktt"][:, 0:2, :]), reads=[d["b_ktt"]], cowrites=[self.bKT[s]])
                S.dma("sp", I_dma(self.VS[s, 0:2, :, t, :].rearrange("g p d -> p g d"), d["vt"][:, 0:256].rearrange("p (g d) -> p g d", d=128)),
                      reads=[d["b_vt"]], cowrites=[self.bVS[s]])

            def chain_mla_k(s, t, d):
                W = d["Wk"]; pf = d["pfk"]; pr = d["prk"]; kr2 = d["kr2"]; ckT = d["ckT"]; ktt = d["ktt"]; vt = d["vt"]
                PBb = PB[:, 512:1024].bitcast(BF16)
                for k in range(KC):
                    S.op("pe", I_mm(PB[:, 0:320], d["hTt"][:, k, :], wdkv[:, k, :], k == 0, k == KC - 1),
                         reads=[d["b_hTt"][k], b_w], writes=[self.bPB[0]])
                S.op("act", I_acopy(pf[:], PB[:, 0:320]), reads=[self.bPB[0]], writes=[d["b_pfk"]])
                yield
                self.rms_rstd(W, pf, 1, 256, d["b_pfk"])
                yield
                S.op("dve", I_ts(W["sq"][:, 0:256], pf[:, 0:256], W["rstd"][:, 0:1], None, ALU.mult), reads=[d["b_pfk"], W["b_rstd"]], writes=[W["b_sq"]])
                S.op("dve", I_tt(pr[:], W["sq"][:, 0:256], mkvg[:], ALU.mult), reads=[W["b_sq"], b_w], writes=[d["b_prk"]])
                yield
                kr3 = kr2[:, 0:64].rearrange("p (h d) -> p h d", d=64)
                self.rope(W, pf[:, 256:320].rearrange("p (h d) -> p h d", d=64), 1, 64, cosT[:, t, :], sinT[:, t, :], kr3, d["b_pfk"], d["b_kr2"], b_w)
                S.op("pool", I_copy(kr2[:, 64:128], kr2[:, 0:64]), reads=[d["b_kr2"]], writes=[d["b_kr2"]])
                yield
                for c in range(2):
                    S.op("pe", I_tr(PBb[:, c * 128:(c + 1) * 128], pr[:, c * 128:(c + 1) * 128], self.identB[:]),
                         reads=[d["b_prk"], self.bconst], writes=[self.bPB[1]])
                S.op("pe", I_tr(PBb[:, 256:384], kr2[:], self.identB[:]), reads=[d["b_kr2"], self.bconst], writes=[self.bPB[1]])
                S.op("dve", I_copy(ckT[:], PBb[:, 0:256].rearrange("p (c n) -> p c n", n=128)), reads=[self.bPB[1]], writes=[d["b_ckT"]])
                S.op("dve", I_copy(ktt[:, 8, :], PBb[:, 256:384]), reads=[self.bPB[1]], writes=[d["b_ktt"]])
                yield
                for hh in range(8):
                    for c in range(2):
                        S.op("pe", I_mm(PD[:, hh * 128:(hh + 1) * 128], wukv4[:, c, hh, 0:128], ckT[:, c, :], c == 0, c == 1),
                             reads=[d["b_ckT"], b_w], writes=[self.bPD[hh // 4]])
                S.op("dve", I_copy(ktt[:, 0:8, :], PD[:].rearrange("p (h n) -> p h n", n=128)), reads=self.bPD, writes=[d["b_ktt"]])
                yield
                for half in range(2):
                    for c in range(2):
                        S.op("pe", I_mm(PB[:, half * 512:(half + 1) * 512], ckT[:, c, :], wukv4[:, c, half * 4:(half + 1) * 4, 128:256], c == 0, c == 1),
                             reads=[d["b_ckT"], b_w], writes=[self.bPB[half]])
                S.op("act", I_acopy(vt[:], PB[:]), reads=self.bPB, writes=[d["b_vt"]])
                yield
                cs = slice(t * 128, (t + 1) * 128)
                S.dma("sp", I_dma(self.KT[s, :, :, cs].rearrange("g d n -> d g n"), ktt[:]), reads=[d["b_ktt"]], cowrites=[self.bKT[s]])
                S.dma("sp", I_dma(self.VS[s, :, :, t, :].rearrange("g p d -> p g d"), vt[:].rearrange("p (g d) -> p g d", d=128)),
                      reads=[d["b_vt"]], cowrites=[self.bVS[s]])

            def chain_mla_q(s, t, d):
                W = d["Wq"]; pf = d["pfq"]; pr = d["prq"]; cqT = d["cqT"]; pfr = d["pfr"]; qr2 = d["qr2"]
                for c0, c1 in ((0, 512), (512, 768)):
                    for k in range(KC):
                        S.op("pe", I_mm(PC[:, c0:c1], d["hTt"][:, k, :], wdq[:, k, c0:c1], k == 0, k == KC - 1),
                             reads=[d["b_hTt"][k], b_w], writes=[self.bPC[c0 // 512]])
                S.op("act", I_acopy(pf[:], PC[:, 0:768]), reads=self.bPC, writes=[d["b_pfq"]])
                yield
                self.rms_rstd(W, pf, 1, 768, d["b_pfq"])
                yield
                S.op("dve", I_ts(W["sq"][:, 0:768], pf[:], W["rstd"][:, 0:1], None, ALU.mult), reads=[d["b_pfq"], W["b_rstd"]], writes=[W["b_sq"]])
                S.op("dve", I_tt(pr[:], W["sq"][:, 0:768], mqg[:], ALU.mult), reads=[W["b_sq"], b_w], writes=[d["b_prq"]])
                yield
                for c in range(6):
                    S.op("pe", I_tr(PAb[:, c * 128:(c + 1) * 128], pr[:, c * 128:(c + 1) * 128], self.identB[:]),
                         reads=[d["b_prq"], self.bconst], writes=[self.bPA[0]])
                S.op("dve", I_copy(cqT[:], PAb[:, 0:768].rearrange("p (c n) -> p c n", n=128)), reads=[self.bPA[0]], writes=[d["b_cqT"]])
                yield
                for hh in range(8):
                    for c in range(6):
                        S.op("pe", I_mm(PC[:, hh * 128:(hh + 1) * 128], wuq4[:, c, hh, 0:128], cqT[:, c, :], c == 0, c == 5),
                             reads=[d["b_cqT"], b_w], writes=[self.bPC[hh // 4]])
                S.op("act", I_acopy(d["qtt"][:], PC[:].rearrange("p (h n) -> p h n", n=128)), reads=self.bPC, writes=[d["b_qtt"]])
                for c in range(6):
                    S.op("pe", I_mm(PD[:, 0:512], cqT[:, c, :], wuq4[:, c, :, 128:192], c == 0, c == 5),
                         reads=[d["b_cqT"], b_w], writes=[self.bPD[0]])
                S.op("act", I_acopy(pfr[:], PD[:, 0:512]), reads=[self.bPD[0]], writes=[d["b_pfr"]])
                yield
                src8 = pfr[:].rearrange("p (h d) -> p h d", d=64)
                dst4 = qr2[:].rearrange("p (c x) -> p c x", x=256)
                self.rope(W, src8[:, 0::2, :], 4, 64, cosT[:, t, :], sinT[:, t, :], dst4[:, :, 0:64], d["b_pfr"], d["b_qr2"], b_w)
                yield
                self.rope(W, src8[:, 1::2, :], 4, 64, cosT[:, t, :], sinT[:, t, :], dst4[:, :, 192:256], d["b_pfr"], d["b_qr2"], b_w)
                yield
                for c in range(8):
                    S.op("pe", I_tr(PAb[:, 1024 + c * 128:1024 + (c + 1) * 128], qr2[:, c * 128:(c + 1) * 128], self.identB[:]),
                         reads=[d["b_qr2"], self.bconst], writes=[self.bPA[1]])
                S.op("dve", I_copy(d["qrt"][:], PAb[:, 1024:2048].rearrange("p (c n) -> p c n", n=128)), reads=[self.bPA[1]], writes=[d["b_qrt"]])
                yield
                cs = slice(t * 128, (t + 1) * 128)
                S.dma("sp", I_dma(self.QT[s, :, :, cs].rearrange("h d n -> d h n"), d["qtt"][:]), reads=[d["b_qtt"]], cowrites=[self.bQT[s]])
                S.dma("sp", I_dma(self.QR[s, :, :, cs].rearrange("h d n -> d h n"), d["qrt"][:]), reads=[d["b_qrt"]], cowrites=[self.bQT[s]])

            def chain_mla(s, t, d):
                j = NS if t < 2 else s
                r0 = s * NR + t * 128
                S.dma("sp", I_dma(d["xin"][:], self.XA[r0:r0 + 128, :]), reads=[self.bXA[s]], writes=[d["b_xin"]])
                self.transpose_mod(d["xin"], 1, 0, j, d["hTt"], d["b_xin"], d["b_hTt"])
                yield
                for _ in zip_longest(chain_mla_k(s, t, d), chain_mla_q(s, t, d)):
                    yield

            chain = chain_gqa if gqa else chain_mla
            work = [(s, t) for s in range(NS) for t in range(NT)]
            for w0 in range(0, len(work), NSLOT):
                gens = [chain(s, t, slots[i]) for i, (s, t) in enumerate(work[w0:w0 + NSLOT])]
                for _ in zip_longest(*gens):
                    pass
            S.run_block()

    def phase_A2(self, l, last):
        nc, S, ext = self.nc, self.S, self.ext
        gqa = (l % 2 == 0)
        PA, PB, PC, PD = self.PA, self.PB, self.PC, self.PD
        SCALE = 128 ** -0.5 if gqa else 192 ** -0.5
        NKH = 2 if gqa else 8
        QS = 512
        with ExitStack() as es:
            T = lambda n, sh, dt: es.enter_context(nc.sbuf_tensor(f"{n}_L{l}", sh, dt))
            wo = T("a_wo", [128, 8, D], BF16)
            b_w = Buf()
            S.dma("pool", I_dma(wo[:], ext[f"wo{l}"].rearrange("(h p) c -> p h c", p=128)), writes=[b_w])
            lng = T("a_lng", [128, D], F32); lnb = T("a_lnb", [128, D], F32)
            wr = T("a_wr", [128, KC, E], F32)
            S.dma("sp", I_dma(lng[:], ext[f"lnmg{l}"][0, :].partition_broadcast(128)), writes=[b_w])
            S.dma("sp", I_dma(lnb[:], ext[f"lnmb{l}"][0, :].partition_broadcast(128)), writes=[b_w])
            S.dma("sp", I_dma(wr[:], ext[f"rw{l}"].rearrange("(k p) e -> p k e", p=128)), writes=[b_w])
            G1c = T("a_G1c", [128, D], F32); G1l = T("a_G1l", [128, D], F32)
            b_G1c, b_G1l = bufs(2)
            S.dma("sp", I_dma(G1c[:], self.GS[0, NS]), reads=[self.bGS], writes=[b_G1c])
            W = {}
            for n, sh in (("t", [128, D]), ("st", [128, 2, 6]), ("mv", [128, 2]), ("lnv1", [128, 1]),
                          ("rstd1", [128, 1]), ("nmr", [128, 1]), ("epsc", [128, 1])):
                W[n] = T("a_" + n, sh, F32)
            for n in ("t", "st", "mv", "lnv1", "rstd1", "nmr", "eps"):
                W["b_" + n] = Buf()
            W["b_ln"] = b_w
            S.op("pool", I_memset(W["epsc"][:], EPS), writes=[W["b_eps"]])
            xblk = T("a_xblk", [128, 4, D], F32); b_xblk = bufs(4)
            qTb = [T(f"a_qTb{i}", [128, 8, 512], BF16) for i in range(2)]; b_qTb = bufs(2)
            oTb = T("a_oTb", [128, 8, 512], BF16); b_oTb = Buf()
            kTg = [T(f"a_kTg{i}", [128, NR], BF16) for i in range(2)]; b_kTg = bufs(2)
            Vg = [T(f"a_Vg{i}", [128, NT, 128], BF16) for i in range(2)]; b_Vg = bufs(2)
            PT = [T(f"a_PT{i}", [128, NT, QS], BF16) for i in range(2)]; b_PT = [bufs(NT) for _ in range(2)]
            rden = T("a_rden", [128, 512], F32); b_rden = Buf()
            h2T = T("a_h2T", [128, KC, 128], F32); b_h2T = Buf()
            sm = {n: T("a_sm_" + n, sh, F32) for n, sh in (("mx", [128, 1]), ("nmx", [128, 1]), ("ex", [128, E]),
                                                           ("ssum", [128, 1]), ("rs", [128, 1]), ("aff", [128, E]))}
            b_sm = {n: Buf() for n in sm}
            affT = T("a_affT", [16, 128], F32); b_affT = Buf()
            if not gqa:
                qrT = [T(f"a_qrT{i}", [128, 8, 512], BF16) for i in range(2)]; b_qrT = bufs(2)
                krT = T("a_krT", [128, NR], BF16); b_krT = Buf()
            CW = []
            for i in range(4):
                d = {n: T(f"a_c{i}_{n}", sh, F32) for n, sh in (("t", [128, D]), ("st", [128, 2, 6]), ("mv", [128, 2]), ("lnv", [128, 1]),
                                                               ("rstd", [128, 1]), ("nmr", [128, 1]), ("h2T", [128, KC, 128]),
                                                               ("mx", [128, 1]), ("nmx", [128, 1]), ("ex", [128, E]), ("ssum", [128, 1]),
                                                               ("rs", [128, 1]), ("aff", [128, E]), ("affT", [16, 128]))}
                for n in ("t", "st", "mv", "lnv", "rstd", "nmr", "mx", "nmx", "ex", "ssum", "rs", "aff", "affT"):
                    d["b_" + n] = Buf()
                d["b_h2T"] = bufs(KC)
                CW.append(d)

            def chain_block(s, tiles, isctx, j, G1, bG1):
                n = len(tiles)
                info = []
                for ti, t in enumerate(tiles):
                    P, bP = (PB, self.bPB) if ti % 2 == 0 else (PD, self.bPD)
                    PT_, bPT_ = (PA, self.bPA) if ti % 2 == 0 else (PC, self.bPC)
                    info.append((ti, t, s * NR + t * 128, CW[ti], P, bP, PT_, bPT_))
                for ti, t, r0, d, P, bP, PX, bPX in info:
                    for half in range(2):
                        for hh in range(8):
                            S.op("pe", I_mm(P[:, half * 512:(half + 1) * 512], oTb[:, hh, ti * 128:(ti + 1) * 128],
                                            wo[:, hh, half * 512:(half + 1) * 512], hh == 0, hh == 7),
                                 reads=[b_oTb, b_w], writes=[bP[half]])
                    S.op("dve", I_tt(d["t"][:], P[:], G1[:], ALU.mult), reads=bP + [bG1], writes=[d["b_t"]])
                yield
                for ti, t, r0, d, P, bP, PX, bPX in info:
                    S.op("dve", I_stt(d["t"][:], xblk[:, ti, :], ALPHA, d["t"][:], ALU.mult, ALU.add), reads=[b_xblk[ti], d["b_t"]], writes=[d["b_t"]])
                yield
                for ti, t, r0, d, P, bP, PX, bPX in info:
                    S.op("dve", (lambda d: lambda h: h.bn_stats(out=d["st"][:, 0, :], in_=d["t"][:, 0:512]))(d), reads=[d["b_t"]], writes=[d["b_st"]])
                    S.op("dve", (lambda d: lambda h: h.bn_stats(out=d["st"][:, 1, :], in_=d["t"][:, 512:1024]))(d), reads=[d["b_t"]], writes=[d["b_st"]])
                yield
                for ti, t, r0, d, P, bP, PX, bPX in info:
                    S.op("dve", (lambda d: lambda h: h.bn_aggr(out=d["mv"][:], in_=d["st"][:].rearrange("p a b -> p (a b)")))(d), reads=[d["b_st"]], writes=[d["b_mv"]])
                yield
                for ti, t, r0, d, P, bP, PX, bPX in info:
                    S.op("act", I_act(d["lnv"][:], d["mv"][:, 1:2], AF.Ln, scale=1.0, bias=W["epsc"][:, 0:1]), reads=[d["b_mv"], W["b_eps"]], writes=[d["b_lnv"]])
                yield
                for ti, t, r0, d, P, bP, PX, bPX in info:
                    S.op("act", I_act(d["rstd"][:], d["lnv"][:], AF.Exp, scale=-0.5), reads=[d["b_lnv"]], writes=[d["b_rstd"]])
                yield
                for ti, t, r0, d, P, bP, PX, bPX in info:
                    S.op("dve", I_ts(d["nmr"][:], d["mv"][:, 0:1], d["rstd"][:, 0:1], -1.0, ALU.mult, ALU.mult), reads=[d["b_mv"], d["b_rstd"]], writes=[d["b_nmr"]])
                yield
                for ti, t, r0, d, P, bP, PX, bPX in info:
                    S.op("act", I_act(d["t"][:], d["t"][:], AF.Identity, scale=d["rstd"][:, 0:1], bias=d["nmr"][:, 0:1]),
                         reads=[d["b_t"], d["b_rstd"], d["b_nmr"]], writes=[d["b_t"]])
                yield
                for ti, t, r0, d, P, bP, PX, bPX in info:
                    S.op("dve", I_tt(d["t"][:], d["t"][:], lng[:], ALU.mult), reads=[d["b_t"], b_w], writes=[d["b_t"]])
                yield
                for ti, t, r0, d, P, bP, PX, bPX in info:
                    S.op("pool", I_tt(d["t"][:], d["t"][:], lnb[:], ALU.add), reads=[d["b_t"], b_w], writes=[d["b_t"]])
                yield
                for ti, t, r0, d, P, bP, PX, bPX in info:
                    S.dma("sp", I_dma(self.XB[r0:r0 + 128, :], d["t"][:]), reads=[d["b_t"]], cowrites=[self.bXB[s]])
                    S.dma("sp", I_dma(self.FA[r0:r0 + 128, :], self.zeros[:]), reads=[self.bconst],
                          cowrites=[self.bFAc if isctx else self.bFAl[s]])
                yield
                for ti, t, r0, d, P, bP, PX, bPX in info:
                    for k in range(KC):
                        S.op("pe", I_tr(PX[:, k * 128:(k + 1) * 128], d["t"][:, k * 128:(k + 1) * 128], self.identF[:]),
                             reads=[d["b_t"], self.bconst], writes=[bPX[k // 4]])
                    for k in range(KC):
                        S.op("act", I_act(d["h2T"][:, k, :], PX[:, k * 128:(k + 1) * 128], AF.Identity,
                                          scale=self.modT[:, 3, k, j:j + 1], bias=self.modT[:, 2, k, j:j + 1]),
                             reads=[bPX[k // 4], self.bmodT], writes=[d["b_h2T"][k]])
                yield
                for ti, t, r0, d, P, bP, PX, bPX in info:
                    for k in range(KC):
                        S.op("pe", I_mm(P[:, 32 * (ti // 2):32 * (ti // 2) + E], d["h2T"][:, k, :], wr[:, k, :], k == 0, k == KC - 1),
                             reads=[d["b_h2T"][k], b_w], writes=[bP[0]])
                    S.op("dve", (lambda d, P, lc: lambda h: h.tensor_reduce(out=d["mx"][:], in_=P[:, lc:lc + E], axis=AX.X, op=ALU.max))(d, P, 32 * (ti // 2)),
                         reads=[bP[0]], writes=[d["b_mx"]])
                yield
                for ti, t, r0, d, P, bP, PX, bPX in info:
                    S.op("dve", I_ts(d["nmx"][:], d["mx"][:], -1.0, None, ALU.mult), reads=[d["b_mx"]], writes=[d["b_nmx"]])
                yield
                for ti, t, r0, d, P, bP, PX, bPX in info:
                    S.op("act", I_act(d["ex"][:], P[:, 32 * (ti // 2):32 * (ti // 2) + E], AF.Exp, scale=1.0, bias=d["nmx"][:, 0:1], accum_out=d["ssum"][:]),
                         reads=[bP[0], d["b_nmx"]], writes=[d["b_ex"], d["b_ssum"]])
                yield
                for ti, t, r0, d, P, bP, PX, bPX in info:
                    S.op("dve", (lambda d: lambda h: h.reciprocal(out=d["rs"][:], in_=d["ssum"][:]))(d), reads=[d["b_ssum"]], writes=[d["b_rs"]])
                yield
                for ti, t, r0, d, P, bP, PX, bPX in info:
                    S.op("dve", I_ts(d["aff"][:], d["ex"][:], d["rs"][:, 0:1], None, ALU.mult), reads=[d["b_ex"], d["b_rs"]], writes=[d["b_aff"]])
                yield
                for ti, t, r0, d, P, bP, PX, bPX in info:
                    S.op("pe", I_tr(P[0:E, 512:640], d["aff"][:], self.identF[:]), reads=[d["b_aff"], self.bconst], writes=[bP[1]])
                    S.op("act", I_acopy(d["affT"][:], P[0:E, 512:640]), reads=[bP[1]], writes=[d["b_affT"]])
                yield
                for ti, t, r0, d, P, bP, PX, bPX in info:
                    if isctx:
                        dst = self.AFC[s * E:(s + 1) * E, t * 128:(t + 1) * 128]
                    else:
                        dst = self.AFL[s * E:(s + 1) * E, (t - 2) * 128:(t - 1) * 128]
                    S.dma("sp", I_dma(dst, d["affT"][:]), reads=[d["b_affT"]], cowrites=[self.bAF])

            kvload = 0
            blkcnt = 0
            blocks_of = lambda: ([] if last else [[0, 1]]) + [[2 + 4 * b_ + i for i in range(4)] for b_ in range(4)]
            allblocks = [(s_, tl) for s_ in range(NS) for tl in blocks_of()]
            pre_kb = {}

            def prefetch(bi_):
                nonlocal kvload
                s_, tl = allblocks[bi_]
                nq_ = len(tl) * 128
                nkc_ = 2 if tl[0] < 2 else NT
                c0_ = tl[0] * 128
                qb_ = bi_ % 2
                S.dma("sp", I_dma(qTb[qb_][:, :, 0:nq_], self.QT[s_, :, :, c0_:c0_ + nq_].rearrange("h d n -> d h n")),
                      reads=[self.bQT[s_]], writes=[b_qTb[qb_]])
                if not gqa:
                    S.dma("sp", I_dma(qrT[qb_][:, :, 0:nq_], self.QR[s_, :, :, c0_:c0_ + nq_].rearrange("h d n -> d h n")),
                          reads=[self.bQT[s_]], writes=[b_qrT[qb_]])
                kb = kvload % 2; kvload += 1
                pre_kb[bi_] = kb
                S.dma("sp", I_dma(kTg[kb][:, 0:nkc_ * 128], self.KT[s_, 0, :, 0:nkc_ * 128]), reads=[self.bKT[s_]], writes=[b_kTg[kb]])
                S.dma("sp", I_dma(Vg[kb][:, 0:nkc_, :], self.VS[s_, 0, :, 0:nkc_, :]), reads=[self.bVS[s_]], writes=[b_Vg[kb]])

            bi = 0
            prefetch(0)
            for s in range(NS):
                S.dma("sp", I_dma(G1l[:], self.GS[0, s]), reads=[self.bGS], writes=[b_G1l])
                if not gqa:
                    S.dma("sp", I_dma(krT[:], self.KT[s, 8, :, :]), reads=[self.bKT[s]], writes=[b_krT])
                blocks = ([] if last else [[0, 1]]) + [[2 + 4 * b + i for i in range(4)] for b in range(4)]
                for tiles in blocks:
                    isctx = tiles[0] < 2
                    j = NS if isctx else s
                    nq = len(tiles) * 128
                    nkc = 2 if isctx else NT
                    G1 = G1c if isctx else G1l
                    bG1 = b_G1c if isctx else b_G1l
                    assert allblocks[bi] == (s, tiles)
                    qb = bi % 2
                    for ti, t in enumerate(tiles):
                        r0 = s * NR + t * 128
                        S.dma("sp", I_dma(xblk[:, ti, :], self.XA[r0:r0 + 128, :]), reads=[self.bXA[s]], writes=[b_xblk[ti]])
                    nk = nkc * 128
                    kvslot = {0: pre_kb[bi]}

                    def load_kv(g):
                        nonlocal kvload
                        kb = kvload % 2; kvload += 1
                        kvslot[g] = kb
                        S.dma("sp", I_dma(kTg[kb][:, 0:nk], self.KT[s, g, :, 0:nk]), reads=[self.bKT[s]], writes=[b_kTg[kb]])
                        S.dma("sp", I_dma(Vg[kb][:, 0:nkc, :], self.VS[s, g, :, 0:nkc, :]), reads=[self.bVS[s]], writes=[b_Vg[kb]])

                    units = []
                    for g in range(NKH):
                        for hh in ([g * 4 + i for i in range(4)] if gqa else [g]):
                            for q0 in range(0, nq, QS):
                                units.append((g, hh, q0, min(QS, nq - q0)))

                    def emit_S(u, i, kc):
                        g, hh, q0, qn = u
                        kb = kvslot[g]
                        pc = PC[:, (kc % 2) * 512:(kc % 2) * 512 + qn]
                        if gqa:
                            S.op("pe", I_mm(pc, kTg[kb][:, kc * 128:(kc + 1) * 128], qTb[qb][:, hh, q0:q0 + qn], True, True),
                                 reads=[b_kTg[kb], b_qTb[qb]], writes=[self.bPC[kc % 2]])
                        else:
                            S.op("pe", I_mm(pc, kTg[kb][:, kc * 128:(kc + 1) * 128], qTb[qb][:, hh, q0:q0 + qn], True, False),
                                 reads=[b_kTg[kb], b_qTb[qb]], writes=[self.bPC[kc % 2]])
                            S.op("pe", I_mm(pc, krT[:, kc * 128:(kc + 1) * 128], qrT[qb][:, hh, q0:q0 + qn], False, True),
                                 reads=[b_krT, b_qrT[qb]], writes=[self.bPC[kc % 2]])
                        S.op("act", I_act(PT[i % 2][:, kc, 0:qn], pc, AF.Exp, scale=SCALE), reads=[self.bPC[kc % 2]],
                             writes=[b_PT[i % 2][kc]])

                    def emit_PV(u, i, kc):
                        g, hh, q0, qn = u
                        kb = kvslot[g]
                        acc, bacc = (PD, self.bPD) if i % 2 == 0 else (PB, self.bPB)
                        S.op("pe", I_mm(acc[:, 0:qn], Vg[kb][:, kc, :], PT[i % 2][:, kc, 0:qn], kc == 0, kc == nkc - 1),
                             reads=[b_Vg[kb], b_PT[i % 2][kc]], writes=[bacc[0]])
                        S.op("pe", I_mm(acc[:, 512:512 + qn], self.onesB[:], PT[i % 2][:, kc, 0:qn], kc == 0, kc == nkc - 1),
                             reads=[self.bconst, b_PT[i % 2][kc]], writes=[bacc[1]])

                    def emit_norm(u, i):
                        g, hh, q0, qn = u
                        acc, bacc = (PD, self.bPD) if i % 2 == 0 else (PB, self.bPB)
                        S.op("dve", lambda h: h.reciprocal(out=rden[:, 0:qn], in_=acc[:, 512:512 + qn]),
                             reads=[bacc[1]], writes=[b_rden])
                        S.op("dve", I_tt(oTb[:, hh, q0:q0 + qn], acc[:, 0:qn], rden[:, 0:qn], ALU.mult),
                             reads=[bacc[0], b_rden], writes=[b_oTb])

                    for kc in range(nkc):
                        emit_S(units[0], 0, kc)
                    for i, u in enumerate(units):
                        nxt = units[i + 1] if i + 1 < len(units) else None
                        if (i == 0 or units[i - 1][0] != u[0]) and u[0] + 1 < NKH:
                            load_kv(u[0] + 1)
                        for kc in range(nkc):
                            if nxt is not None:
                                emit_S(nxt, i + 1, kc)
                            emit_PV(u, i, kc)
                        emit_norm(u, i)
                    if bi + 1 < len(allblocks):
                        prefetch(bi + 1)
                    bi += 1
                    for _ in chain_block(s, tiles, isctx, j, G1, bG1):
                        pass
            S.run_block()

    def phase_R(self, l, last):
        nc, S, ext = self.nc, self.S, self.ext
        PA = self.PA
        es = self.es_R = ExitStack()
        T = lambda n, sh, dt: es.enter_context(nc.sbuf_tensor(f"{n}_L{l}", sh, dt))
        self.idxT = T("r_idxT", [128, 2, 64], I32)
        self.gT = T("r_gT", [128, 2, 64], F32)
        self.idxC = T("r_idxC", [128, E], I32)
        self.gC = T("r_gC", [128, E], F32)
        self.b_idx = Buf()
        with ExitStack() as es2:
            T2 = lambda n, sh, dt: es2.enter_context(nc.sbuf_tensor(f"{n}_L{l}", sh, dt))
            aw = T2("r_aw", [64, SEQ], F32); tv = T2("r_tv", [64, CAPL], F32); ti = T2("r_ti", [64, CAPL], U32)
            tf = T2("r_tf", [64, CAPL], F32); offs = T2("r_offs", [64, 2], F32)
            b_aw, b_tv, b_ti, b_tf, b_offs = bufs(5)
            S.dma("sp", I_dma(offs[:], ext["offs"]), writes=[b_offs])
            S.dma("sp", I_dma(aw[:], self.AFL), reads=[self.bAF], writes=[b_aw])
            for it in range(CAPL // 8):
                sl = slice(it * 8, it * 8 + 8)
                S.op("dve", lambda h, sl=sl: h.max(out=tv[:, sl], in_=aw[:]), reads=[b_aw], writes=[b_tv])
                S.op("dve", lambda h, sl=sl: h.max_index(out=ti[:, sl], in_max=tv[:, sl], in_values=aw[:]), reads=[b_aw, b_tv], writes=[b_ti])
                S.op("dve", lambda h, sl=sl: h.match_replace(out=aw[:], in_to_replace=tv[:, sl], in_values=aw[:], imm_value=-1.0),
                     reads=[b_tv, b_ti, b_aw], writes=[b_aw])
            S.op("dve", I_copy(tf[:], ti[:]), reads=[b_ti], writes=[b_tf])
            S.op("dve", I_ts(tf[:], tf[:], offs[:, 0:1], None, ALU.add), reads=[b_tf, b_offs], writes=[b_tf])
            for rb in range(2):
                S.op("pe", I_tr(PA[:, rb * 64:(rb + 1) * 64], tf[:, rb * 128:(rb + 1) * 128], self.identF[0:64, 0:64]),
                     reads=[b_tf, self.bconst], writes=self.bPA)
                S.op("pe", I_tr(PA[:, 128 + rb * 64:128 + (rb + 1) * 64], tv[:, rb * 128:(rb + 1) * 128], self.identF[0:64, 0:64]),
                     reads=[b_tv, self.bconst], writes=self.bPA)
            S.op("dve", I_copy(self.idxT[:], PA[:, 0:128].rearrange("p (r c) -> p r c", c=64)), reads=self.bPA, writes=[self.b_idx])
            S.op("act", I_acopy(self.gT[:], PA[:, 128:256].rearrange("p (r c) -> p r c", c=64)), reads=self.bPA, writes=[self.b_idx])
            if not last:
                ac = T2("r_ac", [64, CTX], F32); cv = T2("r_cv", [64, CAPC], F32); ci = T2("r_ci", [64, CAPC], U32)
                cf = T2("r_cf", [64, CAPC], F32); ciT = T2("r_ciT", [32, 64], I32); cgT = T2("r_cgT", [32, 64], F32)
                b_ac, b_cv, b_ci, b_cf, b_ciT = bufs(5)
                S.dma("sp", I_dma(ac[:], self.AFC), reads=[self.bAF], writes=[b_ac])
                for it in range(CAPC // 8):
                    sl = slice(it * 8, it * 8 + 8)
                    S.op("dve", lambda h, sl=sl: h.max(out=cv[:, sl], in_=ac[:]), reads=[b_ac], writes=[b_cv])
                    S.op("dve", lambda h, sl=sl: h.max_index(out=ci[:, sl], in_max=cv[:, sl], in_values=ac[:]), reads=[b_ac, b_cv], writes=[b_ci])
                    S.op("dve", lambda h, sl=sl: h.match_replace(out=ac[:], in_to_replace=cv[:, sl], in_values=ac[:], imm_value=-1.0),
                         reads=[b_cv, b_ci, b_ac], writes=[b_ac])
                S.op("dve", I_copy(cf[:], ci[:]), reads=[b_ci], writes=[b_cf])
                S.op("dve", I_ts(cf[:], cf[:], offs[:, 1:2], None, ALU.add), reads=[b_cf, b_offs], writes=[b_cf])
                S.op("pe", I_tr(PA[0:32, 256:320], cf[:], self.identF[0:64, 0:64]), reads=[b_cf, self.bconst], writes=self.bPA)
                S.op("pe", I_tr(PA[0:32, 320:384], cv[:], self.identF[0:64, 0:64]), reads=[b_cv, self.bconst], writes=self.bPA)
                S.op("dve", I_copy(ciT[:], PA[0:32, 256:320]), reads=self.bPA, writes=[b_ciT])
                S.op("act", I_acopy(cgT[:], PA[0:32, 320:384]), reads=self.bPA, writes=[b_ciT])
                for s in range(NS):
                    S.dma("sp", I_dma(self.idxC[s * 32:(s + 1) * 32, :], ciT[:, s * E:(s + 1) * E]), reads=[b_ciT], writes=[self.b_idx])
                    S.dma("sp", I_dma(self.gC[s * 32:(s + 1) * 32, :], cgT[:, s * E:(s + 1) * E]), reads=[b_ciT], writes=[self.b_idx])
            S.run_block()

    def phase_C(self, l, last):
        nc, S, ext = self.nc, self.S, self.ext
        PA, PB, PC, PD = self.PA, self.PB, self.PC, self.PD
        ntile = 2 * NS + (0 if last else 1)
        ncols = ntile * 128
        cblocks = [(c0, min(c0 + 512, ncols)) for c0 in range(0, ncols, 512)]
        with ExitStack() as es:
            T = lambda n, sh, dt: es.enter_context(nc.sbuf_tensor(f"{n}_L{l}", sh, dt))
            wg = [T(f"c_wg{i}", [128, KC, D], BF16) for i in range(2)]
            wu = [T(f"c_wu{i}", [128, KC, D], BF16) for i in range(2)]
            wd = [T(f"c_wd{i}", [128, KC, D], BF16) for i in range(2)]
            b_wt = bufs(2)
            xs = [T(f"c_xs{i}", [128, D], F32) for i in range(9)]; b_xs = bufs(9)
            xsT2 = [T(f"c_xsT{i}", [128, KC, 9 * 128], BF16) for i in range(2)]; b_xsT2 = [bufs(KC) for _ in range(2)]
            hT = T("c_hT", [128, KC, 9 * 128], BF16); b_hT = Buf()
            sil = [T(f"c_sil{i}", [128, 512], F32) for i in range(2)]; b_sil = bufs(2)
            yg = [T(f"c_yg{i}", [128, D], F32) for i in range(2)]; b_yg = bufs(2)

            def load_w(e):
                i = e % 2
                for wt, nm in ((wg[i], "wg"), (wu[i], "wu"), (wd[i], "wd")):
                    S.dma("pool", I_dma(wt[:], ext[f"{nm}{l}"][e].rearrange("(k p) f -> p k f", p=128)), writes=[b_wt[i]])

            def tile_info(jt):
                if jt < 2 * NS:
                    s, rb = jt // 2, jt % 2
                    return s, self.idxT[:, rb, s * E:(s + 1) * E], self.gT[:, rb, s * E:(s + 1) * E], self.bFAl[s], self.bXB[s:s + 1]
                return NS, self.idxC[:, :], self.gC[:, :], self.bFAc, self.bXB

            def gathers(e):
                for jt in range(ntile):
                    jmod, idx, gate, bfa, bxb = tile_info(jt)
                    S.dma("pool", (lambda x, idx, e: (lambda h: h.indirect_dma_start(
                        out=x[:], out_offset=None, in_=self.XB,
                        in_offset=bass.IndirectOffsetOnAxis(ap=idx[:, e:e + 1], axis=0))))(xs[jt], idx, e),
                          reads=[self.b_idx] + list(bxb), writes=[b_xs[jt]])

            def prep(e):
                xsT = xsT2[e % 2]; b_xsT = b_xsT2[e % 2]
                for jt in range(ntile):
                    jmod, idx, gate, bfa, bxb = tile_info(jt)
                    x = xs[jt]; bx = b_xs[jt]
                    for k in range(KC):
                        S.op("pe", I_tr(PA[:, k * 128:(k + 1) * 128], x[:, k * 128:(k + 1) * 128], self.identF[:]),
                             reads=[bx, self.bconst], writes=[self.bPA[k // 4]])
                    for k in range(KC):
                        sc = self.modT[:, 3, k, jmod:jmod + 1]; sh = self.modT[:, 2, k, jmod:jmod + 1]
                        S.op("act", I_act(xsT[:, k, jt * 128:(jt + 1) * 128], PA[:, k * 128:(k + 1) * 128], AF.Identity, scale=sc, bias=sh),
                             reads=[self.bPA[k // 4], self.bmodT], writes=[b_xsT[k]])
                    yield

            load_w(0)
            gathers(0)
            for _ in prep(0):
                pass
            cnt = 0
            for e in range(E):
                i = e % 2
                xsT = xsT2[e % 2]; b_xsT = b_xsT2[e % 2]
                if e + 1 < E:
                    gathers(e + 1)
                    load_w(e + 1)
                pg = prep(e + 1) if e + 1 < E else iter(())
                gcnt = 0
                for f in range(KC):
                    for (c0, c1) in cblocks:
                        w_ = c1 - c0
                        pb = gcnt % 2; gcnt += 1
                        pa_ = PB[:, pb * 512:pb * 512 + w_]; pu_ = PC[:, pb * 512:pb * 512 + w_]
                        for k in range(KC):
                            S.op("pe", I_mm(pa_, wg[i][:, k, f * 128:(f + 1) * 128], xsT[:, k, c0:c1], k == 0, k == KC - 1),
                                 reads=[b_wt[i], b_xsT[k]], writes=[self.bPBh[pb]])
                        for k in range(KC):
                            S.op("pe", I_mm(pu_, wu[i][:, k, f * 128:(f + 1) * 128], xsT[:, k, c0:c1], k == 0, k == KC - 1),
                                 reads=[b_wt[i], b_xsT[k]], writes=[self.bPC[pb]])
                        S.op("act", I_act(sil[pb][:, 0:w_], pa_, AF.Silu), reads=[self.bPBh[pb]], writes=[b_sil[pb]])
                        S.op("dve", I_tt(hT[:, f, c0:c1], pu_, sil[pb][:, 0:w_], ALU.mult),
                             reads=[self.bPC[pb], b_sil[pb]], writes=[b_hT])
                        if gcnt % 2 == 0:
                            next(pg, None)
                for _ in pg:
                    pass
                for jt in range(ntile):
                    jmod, idx, gate, bfa, bxb = tile_info(jt)
                    for half in range(2):
                        for f in range(KC):
                            S.op("pe", I_mm(PD[:, half * 512:(half + 1) * 512], hT[:, f, jt * 128:(jt + 1) * 128],
                                            wd[i][:, f, half * 512:(half + 1) * 512], f == 0, f == KC - 1),
                                 reads=[b_hT, b_wt[i]], writes=[self.bPD[half]])
                    y = yg[jt % 2]; by = b_yg[jt % 2]
                    S.op("act", I_act(y[:, 0:512], PD[:, 0:512], AF.Identity, scale=gate[:, e:e + 1]),
                         reads=[self.bPD[0], self.b_idx], writes=[by])
                    S.op("dve", I_ts(y[:, 512:1024], PD[:, 512:1024], gate[:, e:e + 1], None, ALU.mult),
                         reads=[self.bPD[1], self.b_idx], writes=[by])
                    S.dma("pool", (lambda y, idx, e: (lambda h: h.indirect_dma_start(
                        out=self.FA, out_offset=bass.IndirectOffsetOnAxis(ap=idx[:, e:e + 1], axis=0),
                        in_=y[:], in_offset=None, compute_op=ALU.add)))(y, idx, e),
                          reads=[by, self.b_idx], writes=[bfa])
            S.run_block()
        self.es_R.close()

    def phase_D(self, l, last, is_out):
        nc, S, ext = self.nc, self.S, self.ext
        GD = 6
        with ExitStack() as es:
            T = lambda n, sh, dt: es.enter_context(nc.sbuf_tensor(f"{n}_L{l}", sh, dt))
            lng = T("d_lng", [128, D], F32); lnb = T("d_lnb", [128, D], F32)
            G2c = T("d_G2c", [128, D], F32); G2l = T("d_G2l", [128, D], F32)
            epsc = T("d_eps", [128, 1], F32)
            b_w, b_G2c, b_G2l, b_eps = bufs(4)
            S.dma("sp", I_dma(lng[:], ext[f"lnfg{l}"][0, :].partition_broadcast(128)), writes=[b_w])
            S.dma("sp", I_dma(lnb[:], ext[f"lnfb{l}"][0, :].partition_broadcast(128)), writes=[b_w])
            S.dma("sp", I_dma(G2c[:], self.GS[1, NS]), reads=[self.bGS], writes=[b_G2c])
            S.op("pool", I_memset(epsc[:], EPS), writes=[b_eps])
            slots = []
            for i in range(GD):
                d = {n: T(f"d_{n}{i}", sh, F32) for n, sh in (("x1", [128, D]), ("f", [128, D]), ("t", [128, D]), ("st", [128, 2, 6]),
                                                              ("mv", [128, 2]), ("lnv", [128, 1]), ("rstd", [128, 1]), ("nmr", [128, 1]))}
                for n in ("x1", "f", "t", "st", "mv", "lnv", "rstd", "nmr"):
                    d["b_" + n] = Buf()
                slots.append(d)
            for s in range(NS):
                S.dma("sp", I_dma(G2l[:], self.GS[1, s]), reads=[self.bGS], writes=[b_G2l])
                tl = list(range(2 if last else 0, NT))
                for g0 in range(0, len(tl), GD):
                    grp = [(tl[g0 + i], slots[i]) for i in range(min(GD, len(tl) - g0))]
                    info = []
                    for t, d in grp:
                        isctx = t < 2
                        r0 = s * NR + t * 128
                        info.append((t, d, isctx, r0, (G2c if isctx else G2l), (b_G2c if isctx else b_G2l)))
                    for t, d, isctx, r0, G2, bG2 in info:
                        S.dma("sp", I_dma(d["x1"][:], self.XB[r0:r0 + 128, :]), reads=[self.bXB[s]], writes=[d["b_x1"]])
                        S.dma("sp", I_dma(d["f"][:], self.FA[r0:r0 + 128, :]), reads=[self.bFAc if isctx else self.bFAl[s]], writes=[d["b_f"]])
                    for t, d, isctx, r0, G2, bG2 in info:
                        S.op("pool", I_tt(d["t"][:], d["f"][:], G2[:], ALU.mult), reads=[d["b_f"], bG2], writes=[d["b_t"]])
                    for t, d, isctx, r0, G2, bG2 in info:
                        S.op("dve", I_stt(d["t"][:], d["x1"][:], ALPHA, d["t"][:], ALU.mult, ALU.add), reads=[d["b_x1"], d["b_t"]], writes=[d["b_t"]])
                    for t, d, isctx, r0, G2, bG2 in info:
                        S.op("dve", (lambda d: lambda h: h.bn_stats(out=d["st"][:, 0, :], in_=d["t"][:, 0:512]))(d), reads=[d["b_t"]], writes=[d["b_st"]])
                        S.op("dve", (lambda d: lambda h: h.bn_stats(out=d["st"][:, 1, :], in_=d["t"][:, 512:1024]))(d), reads=[d["b_t"]], writes=[d["b_st"]])
                    for t, d, isctx, r0, G2, bG2 in info:
                        S.op("dve", (lambda d: lambda h: h.bn_aggr(out=d["mv"][:], in_=d["st"][:].rearrange("p a b -> p (a b)")))(d), reads=[d["b_st"]], writes=[d["b_mv"]])
                    for t, d, isctx, r0, G2, bG2 in info:
                        S.op("act", I_act(d["lnv"][:], d["mv"][:, 1:2], AF.Ln, scale=1.0, bias=epsc[:, 0:1]), reads=[d["b_mv"], b_eps], writes=[d["b_lnv"]])
                    for t, d, isctx, r0, G2, bG2 in info:
                        S.op("act", I_act(d["rstd"][:], d["lnv"][:], AF.Exp, scale=-0.5), reads=[d["b_lnv"]], writes=[d["b_rstd"]])
                    for t, d, isctx, r0, G2, bG2 in info:
                        S.op("dve", I_ts(d["nmr"][:], d["mv"][:, 0:1], d["rstd"][:, 0:1], -1.0, ALU.mult, ALU.mult), reads=[d["b_mv"], d["b_rstd"]], writes=[d["b_nmr"]])
                    for t, d, isctx, r0, G2, bG2 in info:
                        S.op("act", I_act(d["t"][:], d["t"][:], AF.Identity, scale=d["rstd"][:, 0:1], bias=d["nmr"][:, 0:1]),
                             reads=[d["b_t"], d["b_rstd"], d["b_nmr"]], writes=[d["b_t"]])
                    for t, d, isctx, r0, G2, bG2 in info:
                        S.op("dve", I_tt(d["t"][:], d["t"][:], lng[:], ALU.mult), reads=[d["b_t"], b_w], writes=[d["b_t"]])
                    for t, d, isctx, r0, G2, bG2 in info:
                        S.op("pool", I_tt(d["t"][:], d["t"][:], lnb[:], ALU.add), reads=[d["b_t"], b_w], writes=[d["b_t"]])
                    for t, d, isctx, r0, G2, bG2 in info:
                        if is_out:
                            if self.final:
                                if isctx:
                                    continue
                                dst = self.y[s * SEQ + (t - 2) * 128: s * SEQ + (t - 1) * 128, :]
                            else:
                                dst = self.y[r0:r0 + 128, :]
                            self.out_toks.append(S.dma("sp", I_dma(dst, d["t"][:]), reads=[d["b_t"]], cowrites=[self.bY]))
                        else:
                            S.dma("sp", I_dma(self.XA[r0:r0 + 128, :], d["t"][:]), reads=[d["b_t"]], cowrites=[self.bXA[s]])
            S.run_block(final_waits=self.out_toks if is_out else ())


_PROG_CACHE = {}


def get_prog(layers, first, final):
    key = (tuple(layers), first, final)
    if key not in _PROG_CACHE:
        _PROG_CACHE[key] = Prog(list(layers), first, final)
    return _PROG_CACHE[key]


def const_inputs():
    cg, sg = rope_tables(128)
    cm, sm = rope_tables(64)
    offs = np.zeros((64, 2), np.float32)
    for s in range(NS):
        offs[s * E:(s + 1) * E, 0] = s * NR + CTX
        offs[s * E:(s + 1) * E, 1] = s * NR
    return dict(identf=np.eye(128, dtype=np.float32), cosg=cg, sing=sg, cosm=cm, sinm=sm, offs=offs)


def layer_inputs(l, inp):
    j = l // 2
    m = {}
    m[f"ada_w{l}"] = inp["ada_w"][l]
    m[f"ada_b{l}"] = inp["ada_b"][l][None, :]
    m[f"ada_bT{l}"] = np.ascontiguousarray(inp["ada_b"][l].reshape(48, 128).T)
    m[f"lnmg{l}"] = inp["ln_mix_g"][l][None, :]; m[f"lnmb{l}"] = inp["ln_mix_b"][l][None, :]
    m[f"lnfg{l}"] = inp["ln_ffn_g"][l][None, :]; m[f"lnfb{l}"] = inp["ln_ffn_b"][l][None, :]
    m[f"rw{l}"] = inp["router_w"][l]
    m[f"wg{l}"] = inp["expert_w_gate"][l]; m[f"wu{l}"] = inp["expert_w_up"][l]; m[f"wd{l}"] = inp["expert_w_down"][l]
    if l % 2 == 0:
        m[f"wqkv{l}"] = inp["gqa_w_qkv"][j]; m[f"qg{l}"] = inp["gqa_q_g"][j][None, :]
        m[f"kg{l}"] = inp["gqa_k_g"][j][None, :]; m[f"wo{l}"] = inp["gqa_w_o"][j]
    else:
        m[f"wdq{l}"] = inp["mla_w_dq"][j]; m[f"mqg{l}"] = inp["mla_q_g"][j][None, :]; m[f"wuq{l}"] = inp["mla_w_uq"][j]
        m[f"wdkv{l}"] = inp["mla_w_dkv"][j]; m[f"mkvg{l}"] = inp["mla_kv_g"][j][None, :]
        m[f"wukv{l}"] = inp["mla_w_ukv"][j]; m[f"wo{l}"] = inp["mla_w_o"][j]
    return {k: np.ascontiguousarray(v, dtype=np.float32) for k, v in m.items()}


LAUNCH_GROUPS = [[0, 1, 2, 3]]


def kernel(**inp):
    inp = {k: np.asarray(v) for k, v in inp.items()}
    consts = const_inputs()
    xa = []
    ccs = []
    for c in range(NCORES):
        rows = []
        for s in range(NS):
            b = c * NS + s
            rows.append(inp["ctx"][b]); rows.append(inp["x"][b])
        xa.append(np.ascontiguousarray(np.concatenate(rows, 0), dtype=np.float32))
        cc = np.concatenate([inp["c"][c * NS:(c + 1) * NS], inp["c_ctx"][None, :]], 0)
        ccs.append(np.ascontiguousarray(cc.reshape(NS + 1, KC, 128).transpose(2, 1, 0), dtype=np.float32))
    out = None
    for gi, layers in enumerate(LAUNCH_GROUPS):
        final = layers[-1] == DEPTH - 1
        prog = get_prog(layers, gi == 0, final)
        wl = {}
        for l in layers:
            wl.update(layer_inputs(l, inp))
        in_maps = []
        for c in range(NCORES):
            m = dict(consts); m.update(wl)
            m["xa_in"] = xa[c]; m["ccT"] = ccs[c]
            in_maps.append(m)
        res = run_bass_kernel_spmd(prog.nc, in_maps, core_ids=list(range(NCORES)))
        if final:
            out = np.concatenate([r["y"].reshape(NS, SEQ, D) for r in res.results], 0)
        else:
            xa = [np.ascontiguousarray(r["xa_out"]) for r in res.results]
    return out.astype(np.float32)
```

```python
import numpy as np
from contextlib import ExitStack
import concourse.bass as bass
import concourse.mybir as mybir
from concourse.bass_utils import run_bass_kernel_spmd

F32 = mybir.dt.float32
BF16 = mybir.dt.bfloat16
I32 = mybir.dt.int32
U32 = mybir.dt.uint32
AF = mybir.ActivationFunctionType
ALU = mybir.AluOpType
AX = mybir.AxisListType

ENGS = ("pe", "act", "dve", "pool", "sp")

D = 1024
KC = 8
NCORES = 8
NS = 4
SEQ = 2048
CTX = 256
NR = SEQ + CTX
NT = NR // 128
E = 16
CAPL = 256
CAPC = 32
DEPTH = 4
ALPHA = float((2 * DEPTH) ** 0.25)
EPS = 1e-6
THETA = 10000.0
GRID_W = 64


class Buf:
    __slots__ = ("w", "r")

    def __init__(self):
        self.w = []
        self.r = []


def bufs(n):
    return [Buf() for _ in range(n)]


class DmaGroup:
    __slots__ = ("ring", "sem_i", "total", "prev_total", "first")


class Sched:
    def __init__(self, nc, esems, rings):
        self.nc = nc
        self.esems = esems
        self.rings = rings
        self.cnt = {e: 0 for e in ENGS}
        self.dtot = {q: [0] * len(v) for q, v in rings.items()}
        self.dnext = {q: 0 for q in rings}
        self.seen = {e: {} for e in ENGS}
        self.q = {e: [] for e in ENGS}
        self.ninst = 0
        self.muted = False

    def _deps(self, reads, writes, cowrites=()):
        deps = []
        for b in reads:
            for t in b.w:
                deps.append((t, True))
        for b in writes:
            for t in b.w:
                deps.append((t, False))
            for t in b.r:
                deps.append((t, False))
        for b in cowrites:
            for t in b.r:
                deps.append((t, False))
        return deps

    def op(self, eng, fn, reads=(), writes=()):
        if self.muted:
            return ("dv", "sp", 0, 0)
        deps = self._deps(reads, writes)
        self.cnt[eng] += 1
        tok = ("e", eng, self.cnt[eng])
        self.q[eng].append((deps, fn, ("e", eng)))
        for b in reads:
            b.r.append(tok)
        for b in writes:
            b.w = [tok]
            b.r = []
        self.ninst += 1
        return tok

    def dma_group(self, eng):
        i = self.dnext[eng]
        self.dnext[eng] = (i + 1) % len(self.rings[eng])
        g = DmaGroup()
        g.ring = eng
        g.sem_i = i
        g.total = self.dtot[eng][i]
        g.prev_total = self.dtot[eng][i]
        g.first = True
        return g

    def dma(self, eng, fn, reads=(), writes=(), group=None, cowrites=()):
        if self.muted:
            return ("dv", "sp", 0, 0)
        g = group if group is not None else self.dma_group(eng)
        assert g.ring == eng
        deps = self._deps(reads, writes, cowrites)
        if g.first:
            if g.prev_total > 0:
                deps.append((("dv", eng, g.sem_i, g.prev_total), True))
            g.first = False
        self.dtot[eng][g.sem_i] += 16
        g.total = self.dtot[eng][g.sem_i]
        tok = ("d", g)
        self.q[eng].append((deps, fn, ("d", eng, g.sem_i)))
        for b in reads:
            b.r.append(tok)
        for b in writes:
            b.w = [tok]
            b.r = []
        for b in cowrites:
            b.w.append(tok)
        self.ninst += 1
        return tok

    def _resolve(self, dep):
        if dep[0] == "e":
            return ("e", dep[1]), dep[2]
        if dep[0] == "d":
            g = dep[1]
            return ("d", g.ring, g.sem_i), g.total
        return ("d", dep[1], dep[2]), dep[3]

    def _sem(self, key):
        return self.esems[key[1]] if key[0] == "e" else self.rings[key[1]][key[2]]

    def replay_engine(self, eng, h):
        seen = self.seen[eng]
        for deps, fn, inc in self.q[eng]:
            need = {}
            for d, raw in deps:
                if d[0] == "e" and d[1] == eng and eng == "pe":
                    continue
                key, val = self._resolve(d)
                if val > seen.get(key, 0) and val > need.get(key, 0):
                    need[key] = val
            for key, val in need.items():
                h.wait_ge(self._sem(key), val)
                seen[key] = val
            ins = fn(h)
            if inc[0] == "e":
                ins.then_inc(self.esems[inc[1]], 1)
            else:
                ins.then_inc(self.rings[inc[1]][inc[2]], 16)
        self.q[eng] = []

    def run_block(self, final_waits=()):
        fence = getattr(self, "fence", [])
        for eng in ENGS:
            if fence and self.q[eng]:
                deps, fn, inc = self.q[eng][0]
                self.q[eng][0] = (list(deps) + [(d, True) for d in fence if not (d[0] == "e" and d[1] == eng)], fn, inc)
        with self.nc.Block() as block:
            @block.tensor
            def _(h):
                self.replay_engine("pe", h)

            @block.scalar
            def _(h):
                self.replay_engine("act", h)

            @block.vector
            def _(h):
                self.replay_engine("dve", h)

            @block.gpsimd
            def _(h):
                self.replay_engine("pool", h)

            @block.sync
            def _(h):
                self.replay_engine("sp", h)
                for tok in final_waits:
                    key, val = self._resolve(tok)
                    h.wait_ge(self._sem(key), val)
        self.fence = [("e", e, self.cnt[e]) for e in ENGS if self.cnt[e] > 0]
        for q, tots in self.dtot.items():
            for i, v in enumerate(tots):
                if v > 0:
                    self.fence.append(("dv", q, i, v))


def I_mm(out, lhsT, rhs, start, stop):
    return lambda h: h.matmul(out, lhsT=lhsT, rhs=rhs, start=start, stop=stop)


def I_tr(out, in_, ident):
    return lambda h: h.transpose(out=out, in_=in_, identity=ident)


def I_act(out, in_, func, scale=1.0, bias=0.0, accum_out=None):
    if accum_out is None:
        return lambda h: h.activation(out=out, in_=in_, func=func, bias=bias, scale=scale)
    return lambda h: h.activation(out=out, in_=in_, func=func, bias=bias, scale=scale, accum_out=accum_out)


def I_tt(out, in0, in1, op):
    return lambda h: h.tensor_tensor(out=out, in0=in0, in1=in1, op=op)


def I_ts(out, in0, s1, s2, op0, op1=None):
    if op1 is None:
        return lambda h: h.tensor_scalar(out=out, in0=in0, scalar1=s1, scalar2=None, op0=op0)
    return lambda h: h.tensor_scalar(out=out, in0=in0, scalar1=s1, scalar2=s2, op0=op0, op1=op1)


def I_stt(out, in0, scalar, in1, op0, op1):
    return lambda h: h.scalar_tensor_tensor(out=out, in0=in0, scalar=scalar, in1=in1, op0=op0, op1=op1)


def I_acopy(out, in_):
    return lambda h: h.activation(out=out, in_=in_, func=AF.Copy)


def I_copy(out, in_):
    return lambda h: h.tensor_copy(out=out, in_=in_)


def I_dma(out, in_):
    return lambda h: h.dma_start(out=out, in_=in_)


def I_memset(ap, v):
    return lambda h: h.memset(ap, v)


def rope_tables(rot_dim):
    t = np.arange(SEQ, dtype=np.int32)
    row = (t // GRID_W).astype(np.float32)
    col = (t % GRID_W).astype(np.float32)
    axis_dim = rot_dim // 2
    freqs = (np.float32(THETA) ** (-np.arange(0, axis_dim, 2, dtype=np.float32) / np.float32(axis_dim))).astype(np.float32)
    ang = np.concatenate([row[:, None] * freqs, col[:, None] * freqs], axis=-1).astype(np.float32)
    cos = np.ones((NR, rot_dim // 2), np.float32)
    sin = np.zeros((NR, rot_dim // 2), np.float32)
    cos[CTX:] = np.cos(ang)
    sin[CTX:] = np.sin(ang)
    cos = np.ascontiguousarray(cos.reshape(NT, 128, -1).transpose(1, 0, 2))
    sin = np.ascontiguousarray(sin.reshape(NT, 128, -1).transpose(1, 0, 2))
    return cos, sin


class _Stop(Exception):
    pass


import os
KSTOP = os.environ.get("KSTOP", "")


class Prog:
    def __init__(self, layers, first, final):
        self.layers = layers
        self.first = first
        self.final = final
        nc = self.nc = bass.Bass("TRN2", target_bir_lowering=False)
        ext = self.ext = {}

        def inp(name, shape, dt=F32):
            ext[name] = nc.dram_tensor(name, shape, dt, kind="ExternalInput").ap()
            return ext[name]

        inp("xa_in", [NS * NR, D])
        inp("ccT", [128, KC, NS + 1])
        inp("identf", [128, 128])
        inp("cosg", [128, NT, 64]); inp("sing", [128, NT, 64])
        inp("cosm", [128, NT, 32]); inp("sinm", [128, NT, 32])
        inp("offs", [64, 2])
        for l in layers:
            inp(f"ada_w{l}", [D, 6 * D]); inp(f"ada_b{l}", [1, 6 * D]); inp(f"ada_bT{l}", [128, 48])
            inp(f"lnmg{l}", [1, D]); inp(f"lnmb{l}", [1, D]); inp(f"lnfg{l}", [1, D]); inp(f"lnfb{l}", [1, D])
            inp(f"rw{l}", [D, E])
            inp(f"wg{l}", [E, D, D]); inp(f"wu{l}", [E, D, D]); inp(f"wd{l}", [E, D, D])
            if l % 2 == 0:
                inp(f"wqkv{l}", [D, 1536]); inp(f"qg{l}", [1, 128]); inp(f"kg{l}", [1, 128]); inp(f"wo{l}", [D, D])
            else:
                inp(f"wdq{l}", [D, 768]); inp(f"mqg{l}", [1, 768]); inp(f"wuq{l}", [768, 1536])
                inp(f"wdkv{l}", [D, 320]); inp(f"mkvg{l}", [1, 256]); inp(f"wukv{l}", [256, 2048]); inp(f"wo{l}", [D, D])
        if final:
            self.y = nc.dram_tensor("y", [NS * SEQ, D], F32, kind="ExternalOutput").ap()
        else:
            self.y = nc.dram_tensor("xa_out", [NS * NR, D], F32, kind="ExternalOutput").ap()
        self.XA = nc.dram_tensor("XA", [NS * NR, D], F32).ap()
        self.XB = nc.dram_tensor("XB", [NS * NR, D], F32).ap()
        self.FA = nc.dram_tensor("FA", [NS * NR, D], F32).ap()
        self.KT = nc.dram_tensor("KT", [NS, 9, 128, NR], BF16).ap()
        self.VS = nc.dram_tensor("VS", [NS, 8, 128, NT, 128], BF16).ap()
        self.QT = nc.dram_tensor("QT", [NS, 8, 128, NR], BF16).ap()
        self.QR = nc.dram_tensor("QR", [NS, 8, 128, NR], BF16).ap()
        self.AFL = nc.dram_tensor("AFL", [NS * E, SEQ], F32).ap()
        self.AFC = nc.dram_tensor("AFC", [NS * E, CTX], F32).ap()
        self.GS = nc.dram_tensor("GS", [2, NS + 1, 128, D], F32).ap()
        self.build()

    def build(self):
        nc = self.nc
        with ExitStack() as es:
            esems = {e: es.enter_context(nc.semaphore("s_" + e)) for e in ENGS}
            rings = {q: [es.enter_context(nc.semaphore(f"d_{q}{i}")) for i in range(8)] for q in ("sp", "pool")}
            S = self.S = Sched(nc, esems, rings)
            self.PA = es.enter_context(nc.psum_tensor("PA", [128, 1024], F32))
            self.PB = es.enter_context(nc.psum_tensor("PB", [128, 1024], F32))
            self.PC = es.enter_context(nc.psum_tensor("PC", [128, 1024], F32))
            self.PD = es.enter_context(nc.psum_tensor("PD", [128, 1024], F32))
            self.bPA = bufs(2)
            self.bPB = bufs(2)
            self.bPBh = self.bPB
            self.bPC = bufs(2)
            self.bPD = bufs(2)
            self.identF = es.enter_context(nc.sbuf_tensor("identF", [128, 128], F32))
            self.identB = es.enter_context(nc.sbuf_tensor("identB", [128, 128], BF16))
            self.onesB = es.enter_context(nc.sbuf_tensor("onesB", [128, 128], BF16))
            self.zeros = es.enter_context(nc.sbuf_tensor("zeros", [128, D], F32))
            self.modT = es.enter_context(nc.sbuf_tensor("modT", [128, 4, KC, 8], F32))
            self.bconst, self.bmodT = Buf(), Buf()
            self.bXA = bufs(NS); self.bXB = bufs(NS); self.bFAl = bufs(NS); self.bFAc = Buf()
            self.bKT = bufs(NS); self.bVS = bufs(NS); self.bQT = bufs(NS)
            self.bAF = Buf(); self.bGS = Buf(); self.bY = Buf()
            self.out_toks = []

            S.dma("sp", I_dma(self.identF[:], self.ext["identf"]), writes=[self.bconst])
            S.op("dve", I_copy(self.identB[:], self.identF[:]), reads=[self.bconst], writes=[self.bconst])
            S.op("pool", I_memset(self.onesB[:], 1.0), writes=[self.bconst])
            S.op("pool", I_memset(self.zeros[:], 0.0), writes=[self.bconst])
            for s in range(NS):
                S.dma("sp", I_dma(self.XA[s * NR:(s + 1) * NR, :], self.ext["xa_in"][s * NR:(s + 1) * NR, :]),
                      writes=[self.bXA[s]])
            S.run_block()

            nl = len(self.layers)
            try:
                for li, l in enumerate(self.layers):
                    last = self.final and li == nl - 1
                    self.phase_M(l)
                    self.phase_A1(l)
                    self.phase_A2(l, last)
                    self.stop("A")
                    self.phase_R(l, last)
                    self.stop("R")
                    self.phase_C(l, last)
                    self.stop("C")
                    self.phase_D(l, last, is_out=(li == nl - 1))
            except _Stop:
                pass

    def stop(self, tag):
        if KSTOP == tag and not self.S.muted:
            self.S.muted = True

    def phase_M(self, l):
        nc, S, ext = self.nc, self.S, self.ext
        NJ = NS + 1
        PD, PB = self.PD, self.PB
        with ExitStack() as es:
            T = lambda n, sh, dt: es.enter_context(nc.sbuf_tensor(f"{n}_L{l}", sh, dt))
            cc = T("m_cc", [128, KC, NJ], F32)
            sT = T("m_sT", [128, KC, NJ], F32)
            sTb = T("m_sTb", [128, KC, NJ], BF16)
            onesf = T("m_ones", [128, 128], F32)
            sBC = T("m_sBC", [128, KC, NJ, 128], BF16)
            W = [T(f"m_W{i}", [128, KC, 1024], BF16) for i in range(2)]
            bT = T("m_bT", [128, 48], F32)
            bBC = T("m_bBC", [128, 2, 1024], F32)
            Gst = [T(f"m_G{i}", [128, 1024], F32) for i in range(2)]
            b_cc, b_sT, b_sTb, b_ones, b_sBC, b_bT, b_bBC = bufs(7)
            b_W = bufs(2); b_G = bufs(2)
            S.dma("sp", I_dma(cc[:], ext["ccT"]), writes=[b_cc])
            S.dma("sp", I_dma(bT[:], ext[f"ada_bT{l}"]), writes=[b_bT])
            for gi, c0 in enumerate((2 * D, 5 * D)):
                S.dma("sp", I_dma(bBC[:, gi, :], ext[f"ada_b{l}"][0, c0:c0 + D].partition_broadcast(128)), writes=[b_bBC])
            S.op("act", I_act(sT[:], cc[:], AF.Silu), reads=[b_cc], writes=[b_sT])
            S.op("dve", I_copy(sTb[:], sT[:]), reads=[b_sT], writes=[b_sTb])
            S.op("pool", I_memset(onesf[:], 1.0), writes=[b_ones])
            for k in range(KC):
                for j in range(NJ):
                    S.op("dve", I_ts(sBC[:, k, j, :], onesf[:], sT[:, k, j:j + 1], None, ALU.mult),
                         reads=[b_ones, b_sT], writes=[b_sBC])
            wv = ext[f"ada_w{l}"].rearrange("(k p) c -> p k c", p=128)
            kinds = {0: 0, 1: 1, 3: 2, 4: 3}
            gcount = 0
            for cb in range(6):
                wb = W[cb % 2]; bw = b_W[cb % 2]
                S.dma("pool", I_dma(wb[:], wv[:, :, cb * 1024:(cb + 1) * 1024]), writes=[bw])
                if cb in kinds:
                    kind = kinds[cb]
                    for oc in range(KC):
                        for k in range(KC):
                            S.op("pe", I_mm(PD[:, oc * 8:oc * 8 + NJ], wb[:, k, oc * 128:(oc + 1) * 128], sTb[:, k, :],
                                            k == 0, k == KC - 1), reads=[bw, b_sTb], writes=[self.bPD[0]])
                    pv = PD[:, 0:64].rearrange("p (o j) -> p o j", j=8)[:, :, 0:NJ]
                    bb = bT[:, cb * 8:(cb + 1) * 8].unsqueeze(2).to_broadcast([128, KC, NJ])
                    S.op("dve", I_tt(self.modT[:, kind, :, 0:NJ], pv, bb, ALU.add),
                         reads=[self.bPD[0], b_bT], writes=[self.bmodT])
                    if kind in (1, 3):
                        S.op("dve", I_ts(self.modT[:, kind, :, 0:NJ], self.modT[:, kind, :, 0:NJ], 1.0, None, ALU.add),
                             reads=[self.bmodT], writes=[self.bmodT])
                else:
                    gi = 0 if cb == 2 else 1
                    for j in range(NJ):
                        gs = Gst[gcount % 2]; bg = b_G[gcount % 2]; gcount += 1
                        for half in range(2):
                            for k in range(KC):
                                S.op("pe", I_mm(PB[:, half * 512:(half + 1) * 512], sBC[:, k, j, :],
                                                wb[:, k, half * 512:(half + 1) * 512], k == 0, k == KC - 1),
                                     reads=[bw, b_sBC], writes=self.bPB)
                        S.op("dve", I_tt(gs[:], PB[:], bBC[:, gi, :], ALU.add), reads=self.bPB + [b_bBC], writes=[bg])
                        S.dma("sp", I_dma(self.GS[gi, j], gs[:]), reads=[bg], cowrites=[self.bGS])
            S.run_block()

    def transpose_mod(self, src, kind_sc, kind_sh, j, dst, b_src, b_dst, split=False):
        S, PA = self.S, self.PA
        bd = b_dst if isinstance(b_dst, list) else [b_dst] * KC
        for k in range(KC):
            S.op("pe", I_tr(PA[:, k * 128:(k + 1) * 128], src[:, k * 128:(k + 1) * 128], self.identF[:]),
                 reads=[b_src, self.bconst], writes=[self.bPA[k // 4]])
        for k in range(KC):
            sc = self.modT[:, kind_sc, k, j:j + 1]; sh = self.modT[:, kind_sh, k, j:j + 1]
            if split and k % 2 == 1:
                S.op("dve", I_ts(dst[:, k, :], PA[:, k * 128:(k + 1) * 128], sc, sh, ALU.mult, ALU.add),
                     reads=[self.bPA[k // 4], self.bmodT], writes=[bd[k]])
            else:
                S.op("act", I_act(dst[:, k, :], PA[:, k * 128:(k + 1) * 128], AF.Identity, scale=sc, bias=sh),
                     reads=[self.bPA[k // 4], self.bmodT], writes=[bd[k]])

    def rms_rstd(self, W, src_f, nh, hd, b_src):
        S = self.S
        sq, ss, lnv, rstd = W["sq"], W["ss"], W["lnv"], W["rstd"]
        n = nh * hd
        S.op("dve", I_tt(sq[:, 0:n], src_f[:, 0:n], src_f[:, 0:n], ALU.mult), reads=[b_src], writes=[W["b_sq"]])
        S.op("dve", lambda h: h.tensor_reduce(out=ss[:, 0:nh], in_=sq[:, 0:n].rearrange("p (h d) -> p h d", d=hd),
                                              axis=AX.X, op=ALU.add), reads=[W["b_sq"]], writes=[W["b_ss"]])
        S.op("act", I_act(lnv[:, 0:nh], ss[:, 0:nh], AF.Ln, scale=1.0 / hd, bias=W["epsc"][:, 0:1]),
             reads=[W["b_ss"], W["b_eps"]], writes=[W["b_lnv"]])
        S.op("act", I_act(rstd[:, 0:nh], lnv[:, 0:nh], AF.Exp, scale=-0.5), reads=[W["b_lnv"]], writes=[W["b_rstd"]])

    def rope(self, W, src, nh, hd, cos, sin, dst, b_src, b_dst, b_tab):
        S = self.S
        hp = hd // 2
        x0 = src[:, :, 0::2]; x1 = src[:, :, 1::2]
        cb = cos.unsqueeze(1).to_broadcast([128, nh, hp]); sb = sin.unsqueeze(1).to_broadcast([128, nh, hp])
        ra = W["ra"][:, 0:nh * hp].rearrange("p (h d) -> p h d", d=hp)
        rb = W["rb"][:, 0:nh * hp].rearrange("p (h d) -> p h d", d=hp)
        rc = W["rc"][:, 0:nh * hp].rearrange("p (h d) -> p h d", d=hp)
        rd = W["rd"][:, 0:nh * hp].rearrange("p (h d) -> p h d", d=hp)
        S.op("dve", I_tt(ra, x0, cb, ALU.mult), reads=[b_src, b_tab], writes=[W["b_ra"]])
        S.op("dve", I_tt(rb, x1, sb, ALU.mult), reads=[b_src, b_tab], writes=[W["b_rb"]])
        S.op("dve", I_tt(rc, x0, sb, ALU.mult), reads=[b_src, b_tab], writes=[W["b_rc"]])
        S.op("dve", I_tt(rd, x1, cb, ALU.mult), reads=[b_src, b_tab], writes=[W["b_rd"]])
        S.op("dve", I_tt(dst[:, :, 0::2], ra, rb, ALU.subtract), reads=[W["b_ra"], W["b_rb"]], writes=[b_dst])
        S.op("dve", I_tt(dst[:, :, 1::2], rc, rd, ALU.add), reads=[W["b_rc"], W["b_rd"]], writes=[b_dst])

    def resid_ln(self, W, P, x, G, lng, lnb, out, reads, b_out):
        S = self.S
        t = W["t"]; st = W["st"]; mv = W["mv"]; lnv = W["lnv1"]; rstd = W["rstd1"]; nmr = W["nmr"]
        S.op("dve", I_tt(t[:], P, G, ALU.mult), reads=reads, writes=[W["b_t"]])
        S.op("dve", I_stt(t[:], x, ALPHA, t[:], ALU.mult, ALU.add), reads=reads + [W["b_t"]], writes=[W["b_t"]])
        S.op("dve", lambda h: h.bn_stats(out=st[:, 0, :], in_=t[:, 0:512]), reads=[W["b_t"]], writes=[W["b_st"]])
        S.op("dve", lambda h: h.bn_stats(out=st[:, 1, :], in_=t[:, 512:1024]), reads=[W["b_t"]], writes=[W["b_st"]])
        S.op("dve", lambda h: h.bn_aggr(out=mv[:], in_=st[:].rearrange("p a b -> p (a b)")), reads=[W["b_st"]], writes=[W["b_mv"]])
        S.op("act", I_act(lnv[:], mv[:, 1:2], AF.Ln, scale=1.0, bias=W["epsc"][:, 0:1]), reads=[W["b_mv"], W["b_eps"]], writes=[W["b_lnv1"]])
        S.op("act", I_act(rstd[:], lnv[:], AF.Exp, scale=-0.5), reads=[W["b_lnv1"]], writes=[W["b_rstd1"]])
        S.op("dve", I_ts(nmr[:], mv[:, 0:1], rstd[:, 0:1], -1.0, ALU.mult, ALU.mult), reads=[W["b_mv"], W["b_rstd1"]], writes=[W["b_nmr"]])
        S.op("act", I_act(t[:], t[:], AF.Identity, scale=rstd[:, 0:1], bias=nmr[:, 0:1]),
             reads=[W["b_t"], W["b_rstd1"], W["b_nmr"]], writes=[W["b_t"]])
        S.op("dve", I_tt(t[:], t[:], lng, ALU.mult), reads=[W["b_t"], W["b_ln"]], writes=[W["b_t"]])
        S.op("pool", I_tt(out, t[:], lnb, ALU.add), reads=[W["b_t"], W["b_ln"]], writes=[b_out])

    def phase_A(self, l, last):
        nc, S, ext = self.nc, self.S, self.ext
        gqa = (l % 2 == 0)
        PA, PB, PC, PD = self.PA, self.PB, self.PC, self.PD
        PAb = PA[:].bitcast(BF16)
        SCALE = 128 ** -0.5 if gqa else 192 ** -0.5
        NKH = 2 if gqa else 8
        with ExitStack() as es:
            T = lambda n, sh, dt: es.enter_context(nc.sbuf_tensor(f"{n}_L{l}", sh, dt))
            wo = T("a_wo", [128, 8, D], BF16)
            b_w = Buf()
            S.dma("pool", I_dma(wo[:], ext[f"wo{l}"].rearrange("(h p) c -> p h c", p=128)), writes=[b_w])
            if gqa:
                wqkv = T("a_wqkv", [128, KC, 1536], BF16)
                for hf in range(2):
                    S.dma("pool", I_dma(wqkv[:, :, hf * 768:(hf + 1) * 768],
                                        ext[f"wqkv{l}"].rearrange("(k p) c -> p k c", p=128)[:, :, hf * 768:(hf + 1) * 768]), writes=[b_w])
                qg = T("a_qg", [128, 128], F32); kg = T("a_kg", [128, 128], F32)
                S.dma("sp", I_dma(qg[:], ext[f"qg{l}"][0, :].partition_broadcast(128)), writes=[b_w])
                S.dma("sp", I_dma(kg[:], ext[f"kg{l}"][0, :].partition_broadcast(128)), writes=[b_w])
                cosT = T("a_cos", [128, NT, 64], F32); sinT = T("a_sin", [128, NT, 64], F32)
                S.dma("sp", I_dma(cosT[:], ext["cosg"]), writes=[b_w])
                S.dma("sp", I_dma(sinT[:], ext["sing"]), writes=[b_w])
            else:
                wdq = T("a_wdq", [128, KC, 768], BF16)
                wuq = T("a_wuq", [128, 6, 1536], BF16)
                wdkv = T("a_wdkv", [128, KC, 320], BF16)
                wukv = T("a_wukv", [128, 2, 2048], BF16)
                S.dma("pool", I_dma(wdq[:], ext[f"wdq{l}"].rearrange("(k p) c -> p k c", p=128)), writes=[b_w])
                for hf in range(2):
                    S.dma("pool", I_dma(wuq[:, :, hf * 768:(hf + 1) * 768],
                                        ext[f"wuq{l}"].rearrange("(k p) c -> p k c", p=128)[:, :, hf * 768:(hf + 1) * 768]), writes=[b_w])
                S.dma("pool", I_dma(wdkv[:], ext[f"wdkv{l}"].rearrange("(k p) c -> p k c", p=128)), writes=[b_w])
                for hf in range(2):
                    S.dma("pool", I_dma(wukv[:, :, hf * 1024:(hf + 1) * 1024],
                                        ext[f"wukv{l}"].rearrange("(k p) c -> p k c", p=128)[:, :, hf * 1024:(hf + 1) * 1024]), writes=[b_w])
                mqg = T("a_mqg", [128, 768], F32); mkvg = T("a_mkvg", [128, 256], F32)
                S.dma("sp", I_dma(mqg[:], ext[f"mqg{l}"][0, :].partition_broadcast(128)), writes=[b_w])
                S.dma("sp", I_dma(mkvg[:], ext[f"mkvg{l}"][0, :].partition_broadcast(128)), writes=[b_w])
                cosT = T("a_cos", [128, NT, 32], F32); sinT = T("a_sin", [128, NT, 32], F32)
                S.dma("sp", I_dma(cosT[:], ext["cosm"]), writes=[b_w])
                S.dma("sp", I_dma(sinT[:], ext["sinm"]), writes=[b_w])
            lng = T("a_lng", [128, D], F32); lnb = T("a_lnb", [128, D], F32)
            wr = T("a_wr", [128, KC, E], F32)
            S.dma("sp", I_dma(lng[:], ext[f"lnmg{l}"][0, :].partition_broadcast(128)), writes=[b_w])
            S.dma("sp", I_dma(lnb[:], ext[f"lnmb{l}"][0, :].partition_broadcast(128)), writes=[b_w])
            S.dma("sp", I_dma(wr[:], ext[f"rw{l}"].rearrange("(k p) e -> p k e", p=128)), writes=[b_w])
            G1c = T("a_G1c", [128, D], F32); G1l = T("a_G1l", [128, D], F32)
            b_G1c, b_G1l = bufs(2)
            S.dma("sp", I_dma(G1c[:], self.GS[0, NS]), reads=[self.bGS], writes=[b_G1c])
            W = {}
            for n, sh in (("sq", [128, 1024]), ("ss", [128, 8]), ("lnv", [128, 8]), ("rstd", [128, 8]),
                          ("t", [128, D]), ("st", [128, 2, 6]), ("mv", [128, 2]), ("lnv1", [128, 1]),
                          ("rstd1", [128, 1]), ("nmr", [128, 1]), ("epsc", [128, 1])):
                W[n] = T("a_" + n, sh, F32)
            for n in ("sq", "ss", "lnv", "rstd", "t", "st", "mv", "lnv1", "rstd1", "nmr"):
                W["b_" + n] = Buf()
            W["b_ln"] = b_w
            W["b_eps"] = Buf()
            S.op("pool", I_memset(W["epsc"][:], EPS), writes=[W["b_eps"]])
            hTt = T("a_hTt", [128, KC, 128], BF16); b_hTt = bufs(KC)
            pf = T("a_pf", [128, 1024], F32); b_pf = Buf()
            pn = W["sq"]; b_pn = W["b_sq"]
            pr = T("a_pr", [128, 1024], BF16); b_pr = Buf()
            vt = T("a_vt", [128, 1024], BF16); b_vt = Buf()
            ktt = T("a_ktt", [128, 9, 128], BF16); b_ktt = Buf()
            xblk = T("a_xblk", [128, 4, D], F32); b_xblk = bufs(4)
            xin = [xblk[:, 0, :], xblk[:, 1, :]]; b_xin = b_xblk[0:2]
            qTb = T("a_qTb", [128, 8, 512], BF16); b_qTb = Buf()
            oTraw = T("a_oTb", [128, 8 * 512], BF16); b_oTb = Buf()
            oTb = oTraw[:].rearrange("p (h n) -> p h n", n=512)
            rcrd = oTraw[:, 0:2048].bitcast(F32)
            kTg = [T(f"a_kTg{i}", [128, NR], BF16) for i in range(2)]; b_kTg = bufs(2)
            Vg = [T(f"a_Vg{i}", [128, NT, 128], BF16) for i in range(2)]; b_Vg = bufs(2)
            QS = 512 if gqa else 256
            PT = [T(f"a_PT{i}", [128, NT, QS], BF16) for i in range(2)]; b_PT = [bufs(NT) for _ in range(2)]
            rden = T("a_rden", [128, 512], F32); b_rden = Buf()
            h2T = pf[:].rearrange("p (k n) -> p k n", n=128); b_h2T = b_pf
            W["ra"] = W["t"][:, 0:512]; W["rb"] = W["t"][:, 512:1024]; W["b_ra"] = W["b_rb"] = W["b_t"]
            W["rc"] = rcrd[:, 0:512]; W["rd"] = rcrd[:, 512:1024]; W["b_rc"] = W["b_rd"] = b_oTb
            sm = {n: T("a_sm_" + n, sh, F32) for n, sh in (("mx", [128, 1]), ("nmx", [128, 1]), ("ex", [128, E]),
                                                           ("ssum", [128, 1]), ("rs", [128, 1]), ("aff", [128, E]))}
            b_sm = {n: Buf() for n in sm}
            affT = T("a_affT", [16, 128], F32); b_affT = Buf()
            if not gqa:
                cqT = T("a_cqT", [128, 6, 512], BF16); b_cqT = Buf()
                ckT = T("a_ckT", [128, 2, 128], BF16); b_ckT = Buf()
                qrT = T("a_qrT", [128, 8, 512], BF16); b_qrT = Buf()
                krT = T("a_krT", [128, NR], BF16); b_krT = Buf()
                kr2 = T("a_kr2", [128, 128], BF16); b_kr2 = Buf()
                qr2 = T("a_qr2", [128, 1024], BF16); b_qr2 = Buf()
                S.op("pool", I_memset(qr2[:], 0.0), writes=[b_qr2])
            kvload = 0
            x1cnt = 0
            self.stop("w")
            for s in range(NS):
                par = s % 2
                S.dma("sp", I_dma(G1l[:], self.GS[0, s]), reads=[self.bGS], writes=[b_G1l])
                for t in range(NT):
                    j = NS if t < 2 else s
                    xi = xin[t % 2]; bx = b_xin[t % 2]
                    r0 = s * NR + t * 128
                    S.dma("sp", I_dma(xi, self.XA[r0:r0 + 128, :]), reads=[self.bXA[s]], writes=[bx])
                    self.transpose_mod(xi, 1, 0, j, hTt, bx, b_hTt)
                    cos_t = cosT[:, t, :]; sin_t = sinT[:, t, :]
                    if gqa:
                        for k in range(KC):
                            S.op("pe", I_mm(PB[:, 0:512], hTt[:, k, :], wqkv[:, k, 1024:1536], k == 0, k == KC - 1),
                                 reads=[b_hTt[k], b_w], writes=self.bPB)
                        S.op("act", I_acopy(pf[:, 0:256], PB[:, 0:256]), reads=self.bPB, writes=[b_pf])
                        S.op("act", I_acopy(vt[:, 0:256], PB[:, 256:512]), reads=self.bPB, writes=[b_vt])
                        self.rms_rstd(W, pf, 2, 128, b_pf)
                        pf3 = pf[:, 0:256].rearrange("p (h d) -> p h d", d=128)
                        pn3 = pn[:, 0:256].rearrange("p (h d) -> p h d", d=128)
                        S.op("dve", I_tt(pn3, pf3, W["rstd"][:, 0:2].unsqueeze(2).to_broadcast([128, 2, 128]), ALU.mult),
                             reads=[b_pf, W["b_rstd"]], writes=[b_pn])
                        S.op("pool", I_tt(pn3, pn3, kg[:].unsqueeze(1).to_broadcast([128, 2, 128]), ALU.mult),
                             reads=[b_pn, b_w], writes=[b_pn])
                        pr3 = pr[:, 0:256].rearrange("p (h d) -> p h d", d=128)
                        self.rope(W, pn3, 2, 128, cos_t, sin_t, pr3, b_pn, b_pr, b_w)
                        for g in range(2):
                            S.op("pe", I_tr(PAb[:, g * 128:(g + 1) * 128], pr[:, g * 128:(g + 1) * 128], self.identB[:]),
                                 reads=[b_pr, self.bconst], writes=self.bPA)
                        S.op("dve", I_copy(ktt[:, 0:2, :], PAb[:, 0:256].rearrange("p (g n) -> p g n", n=128)),
                             reads=self.bPA, writes=[b_ktt])
                        S.dma("sp", I_dma(self.KT[par, 0:2, :, t * 128:(t + 1) * 128].rearrange("g d n -> d g n"), ktt[:, 0:2, :]),
                              reads=[b_ktt], cowrites=[self.bKT[par]])
                        S.dma("sp", I_dma(self.VS[par, 0:2, :, t, :].rearrange("g p d -> p g d"),
                                          vt[:, 0:256].rearrange("p (g d) -> p g d", d=128)),
                              reads=[b_vt], cowrites=[self.bVS[par]])
                    else:
                        for k in range(KC):
                            S.op("pe", I_mm(PB[:, 0:320], hTt[:, k, :], wdkv[:, k, :], k == 0, k == KC - 1),
                                 reads=[b_hTt[k], b_w], writes=self.bPB)
                        S.op("act", I_acopy(pf[:, 0:320], PB[:, 0:320]), reads=self.bPB, writes=[b_pf])
                        self.stop("p1a")
                        self.rms_rstd(W, pf, 1, 256, b_pf)
                        S.op("dve", I_ts(pn[:, 0:256], pf[:, 0:256], W["rstd"][:, 0:1], None, ALU.mult),
                             reads=[b_pf, W["b_rstd"]], writes=[b_pn])
                        S.op("pool", I_tt(pr[:, 0:256], pn[:, 0:256], mkvg[:], ALU.mult), reads=[b_pn, b_w], writes=[b_pr])
                        kr3 = kr2[:, 0:64].rearrange("p (h d) -> p h d", d=64)
                        self.rope(W, pf[:, 256:320].rearrange("p (h d) -> p h d", d=64), 1, 64, cos_t, sin_t, kr3, b_pf, b_kr2, b_w)
                        S.op("pool", I_copy(kr2[:, 64:128], kr2[:, 0:64]), reads=[b_kr2], writes=[b_kr2])
                        self.stop("p1b")
                        for c in range(2):
                            S.op("pe", I_tr(PAb[:, c * 128:(c + 1) * 128], pr[:, c * 128:(c + 1) * 128], self.identB[:]),
                                 reads=[b_pr, self.bconst], writes=self.bPA)
                        S.op("pe", I_tr(PAb[:, 256:384], kr2[:], self.identB[:]), reads=[b_kr2, self.bconst], writes=self.bPA)
                        S.op("dve", I_copy(ckT[:], PAb[:, 0:256].rearrange("p (c n) -> p c n", n=128)), reads=self.bPA, writes=[b_ckT])
                        S.op("dve", I_copy(krT[:, t * 128:(t + 1) * 128], PAb[:, 256:384]), reads=self.bPA, writes=[b_krT])
                        self.stop("p1c")
                        wukv4 = wukv[:].rearrange("p c (h x) -> p c h x", x=256)
                        for hh in range(8):
                            for c in range(2):
                                S.op("pe", I_mm(PC[:, hh * 128:(hh + 1) * 128], wukv4[:, c, hh, 0:128], ckT[:, c, :], c == 0, c == 1),
                                     reads=[b_ckT, b_w], writes=[self.bPC[hh // 4]])
                        S.op("dve", I_copy(ktt[:, 0:8, :], PC[:].rearrange("p (h n) -> p h n", n=128)),
                             reads=self.bPC, writes=[b_ktt])
                        self.stop("p1d")
                        for half in range(2):
                            for c in range(2):
                                S.op("pe", I_mm(PB[:, half * 512:(half + 1) * 512], ckT[:, c, :],
                                                wukv4[:, c, half * 4:(half + 1) * 4, 128:256], c == 0, c == 1),
                                     reads=[b_ckT, b_w], writes=self.bPB)
                        S.op("act", I_acopy(vt[:], PB[:]), reads=self.bPB, writes=[b_vt])
                        self.stop("p1e")
                        S.dma("sp", I_dma(self.KT[par, 0:8, :, t * 128:(t + 1) * 128].rearrange("g d n -> d g n"), ktt[:, 0:8, :]),
                              reads=[b_ktt], cowrites=[self.bKT[par]])
                        S.dma("sp", I_dma(self.VS[par, :, :, t, :].rearrange("g p d -> p g d"),
                                          vt[:].rearrange("p (g d) -> p g d", d=128)),
                              reads=[b_vt], cowrites=[self.bVS[par]])
                self.stop("p1")
                blocks = ([] if last else [[0, 1]]) + [[2 + 4 * b + i for i in range(4)] for b in range(4)]
                for tiles in blocks:
                    isctx = tiles[0] < 2
                    j = NS if isctx else s
                    nq = len(tiles) * 128
                    nkc = 2 if isctx else NT
                    G1 = G1c if isctx else G1l
                    bG1 = b_G1c if isctx else b_G1l
                    for ti, t in enumerate(tiles):
                        r0 = s * NR + t * 128
                        S.dma("sp", I_dma(xblk[:, ti, :], self.XA[r0:r0 + 128, :]), reads=[self.bXA[s]], writes=[b_xblk[ti]])
                        self.transpose_mod(xblk[:, ti, :], 1, 0, j, hTt, b_xblk[ti], b_hTt)
                        cos_t = cosT[:, t, :]; sin_t = sinT[:, t, :]
                        if gqa:
                            for half in range(2):
                                for k in range(KC):
                                    S.op("pe", I_mm(PB[:, half * 512:(half + 1) * 512], hTt[:, k, :],
                                                    wqkv[:, k, half * 512:(half + 1) * 512], k == 0, k == KC - 1),
                                         reads=[b_hTt[k], b_w], writes=self.bPB)
                            S.op("act", I_acopy(pf[:], PB[:]), reads=self.bPB, writes=[b_pf])
                            self.rms_rstd(W, pf, 8, 128, b_pf)
                            pf3 = pf[:].rearrange("p (h d) -> p h d", d=128)
                            pn3 = pn[:].rearrange("p (h d) -> p h d", d=128)
                            S.op("dve", I_tt(pn3, pf3, W["rstd"][:, 0:8].unsqueeze(2).to_broadcast([128, 8, 128]), ALU.mult),
                                 reads=[b_pf, W["b_rstd"]], writes=[b_pn])
                            S.op("pool", I_tt(pn3, pn3, qg[:].unsqueeze(1).to_broadcast([128, 8, 128]), ALU.mult),
                                 reads=[b_pn, b_w], writes=[b_pn])
                            pr3 = pr[:].rearrange("p (h d) -> p h d", d=128)
                            self.rope(W, pn3, 8, 128, cos_t, sin_t, pr3, b_pn, b_pr, b_w)
                            for hh in range(8):
                                S.op("pe", I_tr(PAb[:, hh * 128:(hh + 1) * 128], pr[:, hh * 128:(hh + 1) * 128], self.identB[:]),
                                     reads=[b_pr, self.bconst], writes=self.bPA)
                            S.op("dve", I_copy(qTb[:, :, ti * 128:(ti + 1) * 128], PAb[:, 0:1024].rearrange("p (h n) -> p h n", n=128)),
                                 reads=self.bPA, writes=[b_qTb])
                        else:
                            for c0, c1 in ((0, 512), (512, 768)):
                                for k in range(KC):
                                    S.op("pe", I_mm(PB[:, c0:c1], hTt[:, k, :], wdq[:, k, c0:c1], k == 0, k == KC - 1),
                                         reads=[b_hTt[k], b_w], writes=self.bPB)
                            S.op("act", I_acopy(pf[:, 0:768], PB[:, 0:768]), reads=self.bPB, writes=[b_pf])
                            self.rms_rstd(W, pf, 1, 768, b_pf)
                            S.op("dve", I_ts(pn[:, 0:768], pf[:, 0:768], W["rstd"][:, 0:1], None, ALU.mult),
                                 reads=[b_pf, W["b_rstd"]], writes=[b_pn])
                            S.op("pool", I_tt(pr[:, 0:768], pn[:, 0:768], mqg[:], ALU.mult), reads=[b_pn, b_w], writes=[b_pr])
                            for c in range(6):
                                S.op("pe", I_tr(PAb[:, c * 128:(c + 1) * 128], pr[:, c * 128:(c + 1) * 128], self.identB[:]),
                                     reads=[b_pr, self.bconst], writes=self.bPA)
                            S.op("dve", I_copy(cqT[:, :, ti * 128:(ti + 1) * 128], PAb[:, 0:768].rearrange("p (c n) -> p c n", n=128)),
                                 reads=self.bPA, writes=[b_cqT])
                            wuq4 = wuq[:].rearrange("p c (h x) -> p c h x", x=192)
                            for c in range(6):
                                S.op("pe", I_mm(PB[:, 0:512], cqT[:, c, ti * 128:(ti + 1) * 128], wuq4[:, c, :, 128:192], c == 0, c == 5),
                                     reads=[b_cqT, b_w], writes=self.bPB)
                            S.op("act", I_acopy(pf[:, 0:512], PB[:, 0:512]), reads=self.bPB, writes=[b_pf])
                            src8 = pf[:, 0:512].rearrange("p (h d) -> p h d", d=64)
                            dst4 = qr2[:].rearrange("p (c x) -> p c x", x=256)
                            self.rope(W, src8[:, 0::2, :], 4, 64, cos_t, sin_t, dst4[:, :, 0:64], b_pf, b_qr2, b_w)
                            self.rope(W, src8[:, 1::2, :], 4, 64, cos_t, sin_t, dst4[:, :, 192:256], b_pf, b_qr2, b_w)
                            for c in range(8):
                                S.op("pe", I_tr(PAb[:, c * 128:(c + 1) * 128], qr2[:, c * 128:(c + 1) * 128], self.identB[:]),
                                     reads=[b_qr2, self.bconst], writes=self.bPA)
                            S.op("dve", I_copy(qrT[:, :, ti * 128:(ti + 1) * 128], PAb[:, 0:1024].rearrange("p (c n) -> p c n", n=128)),
                                 reads=self.bPA, writes=[b_qrT])
                    if not gqa:
                        wuq4 = wuq[:].rearrange("p c (h x) -> p c h x", x=192)
                        for hh in range(8):
                            pc = PC[:, (hh % 2) * 512:(hh % 2) * 512 + nq]
                            for c in range(6):
                                S.op("pe", I_mm(pc, wuq4[:, c, hh, 0:128], cqT[:, c, 0:nq], c == 0, c == 5),
                                     reads=[b_cqT, b_w], writes=[self.bPC[hh % 2]])
                            S.op("act", I_acopy(qTb[:, hh, 0:nq], pc), reads=[self.bPC[hh % 2]], writes=[b_qTb])
                    self.stop("q")
                    nk = nkc * 128
                    kvslot = {}

                    def load_kv(g):
                        nonlocal kvload
                        kb = kvload % 2; kvload += 1
                        kvslot[g] = kb
                        S.dma("sp", I_dma(kTg[kb][:, 0:nk], self.KT[par, g, :, 0:nk]), reads=[self.bKT[par]], writes=[b_kTg[kb]])
                        S.dma("sp", I_dma(Vg[kb][:, 0:nkc, :], self.VS[par, g, :, 0:nkc, :]), reads=[self.bVS[par]], writes=[b_Vg[kb]])

                    units = []
                    for g in range(NKH):
                        for hh in ([g * 4 + i for i in range(4)] if gqa else [g]):
                            for q0 in range(0, nq, QS):
                                units.append((g, hh, q0, min(QS, nq - q0)))

                    def emit_S(u, i, kc):
                        g, hh, q0, qn = u
                        kb = kvslot[g]
                        pc = PC[:, (kc % 2) * 512:(kc % 2) * 512 + qn]
                        if gqa:
                            S.op("pe", I_mm(pc, kTg[kb][:, kc * 128:(kc + 1) * 128], qTb[:, hh, q0:q0 + qn], True, True),
                                 reads=[b_kTg[kb], b_qTb], writes=[self.bPC[kc % 2]])
                        else:
                            S.op("pe", I_mm(pc, kTg[kb][:, kc * 128:(kc + 1) * 128], qTb[:, hh, q0:q0 + qn], True, False),
                                 reads=[b_kTg[kb], b_qTb], writes=[self.bPC[kc % 2]])
                            S.op("pe", I_mm(pc, krT[:, kc * 128:(kc + 1) * 128], qrT[:, hh, q0:q0 + qn], False, True),
                                 reads=[b_krT, b_qrT], writes=[self.bPC[kc % 2]])
                        S.op("act", I_act(PT[i % 2][:, kc, 0:qn], pc, AF.Exp, scale=SCALE), reads=[self.bPC[kc % 2]],
                             writes=[b_PT[i % 2][kc]])

                    def emit_PV(u, i, kc):
                        g, hh, q0, qn = u
                        kb = kvslot[g]
                        acc, bacc = (PD, self.bPD) if i % 2 == 0 else (PB, self.bPB)
                        S.op("pe", I_mm(acc[:, 0:qn], Vg[kb][:, kc, :], PT[i % 2][:, kc, 0:qn], kc == 0, kc == nkc - 1),
                             reads=[b_Vg[kb], b_PT[i % 2][kc]], writes=[bacc[0]])
                        S.op("pe", I_mm(acc[:, 512:512 + qn], self.onesB[:], PT[i % 2][:, kc, 0:qn], kc == 0, kc == nkc - 1),
                             reads=[self.bconst, b_PT[i % 2][kc]], writes=[bacc[1]])

                    def emit_norm(u, i):
                        g, hh, q0, qn = u
                        acc, bacc = (PD, self.bPD) if i % 2 == 0 else (PB, self.bPB)
                        S.op("dve", lambda h: h.reciprocal(out=rden[:, 0:qn], in_=acc[:, 512:512 + qn]),
                             reads=[bacc[1]], writes=[b_rden])
                        S.op("dve", I_tt(oTb[:, hh, q0:q0 + qn], acc[:, 0:qn], rden[:, 0:qn], ALU.mult),
                             reads=[bacc[0], b_rden], writes=[b_oTb])

                    load_kv(0)
                    for kc in range(nkc):
                        emit_S(units[0], 0, kc)
                    for i, u in enumerate(units):
                        nxt = units[i + 1] if i + 1 < len(units) else None
                        if (i == 0 or units[i - 1][0] != u[0]) and u[0] + 1 < NKH:
                            load_kv(u[0] + 1)
                        for kc in range(nkc):
                            if nxt is not None:
                                emit_S(nxt, i + 1, kc)
                            emit_PV(u, i, kc)
                        emit_norm(u, i)
                    self.stop("att")
                    for ti, t in enumerate(tiles):
                        r0 = s * NR + t * 128
                        for half in range(2):
                            for hh in range(8):
                                S.op("pe", I_mm(PB[:, half * 512:(half + 1) * 512], oTb[:, hh, ti * 128:(ti + 1) * 128],
                                                wo[:, hh, half * 512:(half + 1) * 512], hh == 0, hh == 7),
                                     reads=[b_oTb, b_w], writes=self.bPB)
                        xo = xblk[:, ti, :]; bxo = b_xblk[ti]
                        self.resid_ln(W, PB[:], xblk[:, ti, :], G1[:], lng[:], lnb[:], xo,
                                      self.bPB + [b_xblk[ti], bG1], bxo)
                        S.dma("sp", I_dma(self.XB[r0:r0 + 128, :], xo), reads=[bxo], cowrites=[self.bXB[s]])
                        S.dma("sp", I_dma(self.FA[r0:r0 + 128, :], self.zeros[:]), reads=[self.bconst],
                              cowrites=[self.bFAc if isctx else self.bFAl[s]])
                        self.transpose_mod(xo, 3, 2, j, h2T, bxo, b_h2T, split=False)
                        for k in range(KC):
                            S.op("pe", I_mm(PB[:, 0:E], h2T[:, k, :], wr[:, k, :], k == 0, k == KC - 1),
                                 reads=[b_h2T, b_w], writes=self.bPB)
                        S.op("dve", lambda h: h.tensor_reduce(out=sm["mx"][:], in_=PB[:, 0:E], axis=AX.X, op=ALU.max),
                             reads=self.bPB, writes=[b_sm["mx"]])
                        S.op("dve", I_ts(sm["nmx"][:], sm["mx"][:], -1.0, None, ALU.mult), reads=[b_sm["mx"]], writes=[b_sm["nmx"]])
                        S.op("act", I_act(sm["ex"][:], PB[:, 0:E], AF.Exp, scale=1.0, bias=sm["nmx"][:, 0:1], accum_out=sm["ssum"][:]),
                             reads=self.bPB + [b_sm["nmx"]], writes=[b_sm["ex"], b_sm["ssum"]])
                        S.op("dve", lambda h: h.reciprocal(out=sm["rs"][:], in_=sm["ssum"][:]), reads=[b_sm["ssum"]], writes=[b_sm["rs"]])
                        S.op("dve", I_ts(sm["aff"][:], sm["ex"][:], sm["rs"][:, 0:1], None, ALU.mult),
                             reads=[b_sm["ex"], b_sm["rs"]], writes=[b_sm["aff"]])
                        S.op("pe", I_tr(PA[0:E, 0:128], sm["aff"][:], self.identF[:]), reads=[b_sm["aff"], self.bconst], writes=self.bPA)
                        S.op("act", I_acopy(affT[:], PA[0:E, 0:128]), reads=self.bPA, writes=[b_affT])
                        if isctx:
                            dst = self.AFC[s * E:(s + 1) * E, t * 128:(t + 1) * 128]
                        else:
                            dst = self.AFL[s * E:(s + 1) * E, (t - 2) * 128:(t - 1) * 128]
                        S.dma("sp", I_dma(dst, affT[:]), reads=[b_affT], cowrites=[self.bAF])
            S.run_block()

    def phase_A1(self, l):
        from itertools import zip_longest
        nc, S, ext = self.nc, self.S, self.ext
        gqa = (l % 2 == 0)
        PA, PB, PC, PD = self.PA, self.PB, self.PC, self.PD
        PAb = PA[:].bitcast(BF16)
        with ExitStack() as es:
            T = lambda n, sh, dt: es.enter_context(nc.sbuf_tensor(f"{n}_L{l}", sh, dt))
            b_w = Buf()
            if gqa:
                wqkv = T("p_wqkv", [128, KC, 1536], BF16)
                for hf in range(2):
                    S.dma("pool", I_dma(wqkv[:, :, hf * 768:(hf + 1) * 768],
                                        ext[f"wqkv{l}"].rearrange("(k p) c -> p k c", p=128)[:, :, hf * 768:(hf + 1) * 768]), writes=[b_w])
                qkg = T("p_qkg", [128, 10, 128], F32)
                for hh in range(10):
                    src = ext[f"qg{l}"] if hh < 8 else ext[f"kg{l}"]
                    S.dma("sp", I_dma(qkg[:, hh, :], src[0, :].partition_broadcast(128)), writes=[b_w])
                cosT = T("p_cos", [128, NT, 64], F32); sinT = T("p_sin", [128, NT, 64], F32)
                S.dma("sp", I_dma(cosT[:], ext["cosg"]), writes=[b_w]); S.dma("sp", I_dma(sinT[:], ext["sing"]), writes=[b_w])
            else:
                wdq = T("p_wdq", [128, KC, 768], BF16); wuq = T("p_wuq", [128, 6, 1536], BF16)
                wdkv = T("p_wdkv", [128, KC, 320], BF16); wukv = T("p_wukv", [128, 2, 2048], BF16)
                S.dma("pool", I_dma(wdq[:], ext[f"wdq{l}"].rearrange("(k p) c -> p k c", p=128)), writes=[b_w])
                for hf in range(2):
                    S.dma("pool", I_dma(wuq[:, :, hf * 768:(hf + 1) * 768],
                                        ext[f"wuq{l}"].rearrange("(k p) c -> p k c", p=128)[:, :, hf * 768:(hf + 1) * 768]), writes=[b_w])
                S.dma("pool", I_dma(wdkv[:], ext[f"wdkv{l}"].rearrange("(k p) c -> p k c", p=128)), writes=[b_w])
                for hf in range(2):
                    S.dma("pool", I_dma(wukv[:, :, hf * 1024:(hf + 1) * 1024],
                                        ext[f"wukv{l}"].rearrange("(k p) c -> p k c", p=128)[:, :, hf * 1024:(hf + 1) * 1024]), writes=[b_w])
                mqg = T("p_mqg", [128, 768], F32); mkvg = T("p_mkvg", [128, 256], F32)
                S.dma("sp", I_dma(mqg[:], ext[f"mqg{l}"][0, :].partition_broadcast(128)), writes=[b_w])
                S.dma("sp", I_dma(mkvg[:], ext[f"mkvg{l}"][0, :].partition_broadcast(128)), writes=[b_w])
                cosT = T("p_cos", [128, NT, 32], F32); sinT = T("p_sin", [128, NT, 32], F32)
                S.dma("sp", I_dma(cosT[:], ext["cosm"]), writes=[b_w]); S.dma("sp", I_dma(sinT[:], ext["sinm"]), writes=[b_w])
                wukv4 = wukv[:].rearrange("p c (h x) -> p c h x", x=256)
                wuq4 = wuq[:].rearrange("p c (h x) -> p c h x", x=192)
            epsc = T("p_eps", [128, 1], F32); b_eps = Buf()
            S.op("pool", I_memset(epsc[:], EPS), writes=[b_eps])

            def mkW(tag, n):
                W = {"epsc": epsc, "b_eps": b_eps}
                for nm, sh in (("sq", [128, n]), ("ss", [128, 16]), ("lnv", [128, 16]), ("rstd", [128, 16]),
                               ("ra", [128, n // 2]), ("rb", [128, n // 2]), ("rc", [128, n // 2]), ("rd", [128, n // 2])):
                    W[nm] = T(f"p_{tag}_{nm}", sh, F32)
                    W["b_" + nm] = Buf()
                return W

            NSLOT = 4 if gqa else 3
            slots = []
            for i in range(NSLOT):
                d = {"xin": T(f"p_xin{i}", [128, D], F32), "b_xin": Buf(),
                     "hTt": T(f"p_hTt{i}", [128, KC, 128], BF16), "b_hTt": bufs(KC),
                     "vt": T(f"p_vt{i}", [128, 1024], BF16), "b_vt": Buf(),
                     "qtt": T(f"p_qtt{i}", [128, 8, 128], BF16), "b_qtt": Buf(),
                     "ktt": T(f"p_ktt{i}", [128, 9, 128], BF16), "b_ktt": Buf()}
                if gqa:
                    d["W"] = mkW(f"w{i}", 1280)
                    d["pf"] = T(f"p_pf{i}", [128, 1280], F32); d["b_pf"] = Buf()
                    d["pr"] = T(f"p_pr{i}", [128, 1280], BF16); d["b_pr"] = Buf()
                else:
                    d["Wk"] = mkW(f"wk{i}", 256); d["Wq"] = mkW(f"wq{i}", 768)
                    d["pfk"] = T(f"p_pfk{i}", [128, 320], F32); d["b_pfk"] = Buf()
                    d["prk"] = T(f"p_prk{i}", [128, 256], BF16); d["b_prk"] = Buf()
                    d["kr2"] = T(f"p_kr2{i}", [128, 128], BF16); d["b_kr2"] = Buf()
                    d["ckT"] = T(f"p_ckT{i}", [128, 2, 128], BF16); d["b_ckT"] = Buf()
                    d["pfq"] = T(f"p_pfq{i}", [128, 768], F32); d["b_pfq"] = Buf()
                    d["prq"] = T(f"p_prq{i}", [128, 768], BF16); d["b_prq"] = Buf()
                    d["cqT"] = T(f"p_cqT{i}", [128, 6, 128], BF16); d["b_cqT"] = Buf()
                    d["pfr"] = T(f"p_pfr{i}", [128, 512], F32); d["b_pfr"] = Buf()
                    d["qr2"] = T(f"p_qr2{i}", [128, 1024], BF16); d["b_qr2"] = Buf()
                    d["qrt"] = T(f"p_qrt{i}", [128, 8, 128], BF16); d["b_qrt"] = Buf()
                    S.op("pool", I_memset(d["qr2"][:], 0.0), writes=[d["b_qr2"]])
                slots.append(d)

            def chain_gqa(s, t, d):
                j = NS if t < 2 else s
                r0 = s * NR + t * 128
                W = d["W"]; pf = d["pf"]; pr = d["pr"]; sq = W["sq"]
                S.dma("sp", I_dma(d["xin"][:], self.XA[r0:r0 + 128, :]), reads=[self.bXA[s]], writes=[d["b_xin"]])
                self.transpose_mod(d["xin"], 1, 0, j, d["hTt"], d["b_xin"], d["b_hTt"])
                yield
                for half in range(2):
                    for k in range(KC):
                        S.op("pe", I_mm(PB[:, half * 512:(half + 1) * 512], d["hTt"][:, k, :], wqkv[:, k, half * 512:(half + 1) * 512],
                                        k == 0, k == KC - 1), reads=[d["b_hTt"][k], b_w], writes=[self.bPB[half]])
                for k in range(KC):
                    S.op("pe", I_mm(PC[:, 0:512], d["hTt"][:, k, :], wqkv[:, k, 1024:1536], k == 0, k == KC - 1),
                         reads=[d["b_hTt"][k], b_w], writes=[self.bPC[0]])
                S.op("act", I_acopy(pf[:, 0:1024], PB[:]), reads=self.bPB, writes=[d["b_pf"]])
                S.op("act", I_acopy(pf[:, 1024:1280], PC[:, 0:256]), reads=[self.bPC[0]], writes=[d["b_pf"]])
                S.op("act", I_acopy(d["vt"][:, 0:256], PC[:, 256:512]), reads=[self.bPC[0]], writes=[d["b_vt"]])
                yield
                self.rms_rstd(W, pf, 10, 128, d["b_pf"])
                yield
                pf3 = pf[:].rearrange("p (h d) -> p h d", d=128); pn3 = sq[:].rearrange("p (h d) -> p h d", d=128)
                S.op("dve", I_tt(pn3, pf3, W["rstd"][:, 0:10].unsqueeze(2).to_broadcast([128, 10, 128]), ALU.mult),
                     reads=[d["b_pf"], W["b_rstd"]], writes=[W["b_sq"]])
                S.op("dve", I_tt(pn3, pn3, qkg[:], ALU.mult), reads=[W["b_sq"], b_w], writes=[W["b_sq"]])
                yield
                pr3 = pr[:].rearrange("p (h d) -> p h d", d=128)
                self.rope(W, pn3, 10, 128, cosT[:, t, :], sinT[:, t, :], pr3, W["b_sq"], d["b_pr"], b_w)
                yield
                for hh in range(10):
                    S.op("pe", I_tr(PAb[:, hh * 128:(hh + 1) * 128], pr[:, hh * 128:(hh + 1) * 128], self.identB[:]),
                         reads=[d["b_pr"], self.bconst], writes=self.bPA)
                S.op("dve", I_copy(d["qtt"][:], PAb[:, 0:1024].rearrange("p (h n) -> p h n", n=128)), reads=self.bPA, writes=[d["b_qtt"]])
                S.op("dve", I_copy(d["ktt"][:, 0:2, :], PAb[:, 1024:1280].rearrange("p (h n) -> p h n", n=128)), reads=self.bPA, writes=[d["b_ktt"]])
                yield
                cs = slice(t * 128, (t + 1) * 128)
                S.dma("pool", I_dma(self.QT[s, :, :, cs].rearrange("h d n -> d h n"), d["qtt"][:]), reads=[d["b_qtt"]], cowrites=[self.bQT[s]])
                S.dma("pool", I_dma(self.KT[s, 0:2, :, cs].rearrange("g d n -> d g n"), d["ktt"][:, 0:2, :]), reads=[d["b_ktt"]], cowrites=[self.bKT[s]])
                S.dma("pool", I_dma(self.VS[s, 0:2, :, t, :].rearrange("g p d -> p g d"), d["vt"][:, 0:256].rearrange("p (g d) -> p g d", d=128)),
                      reads=[d["b_vt"]], cowrites=[self.bVS[s]])

            def chain_mla_k(s, t, d):
                W = d["Wk"]; pf = d["pfk"]; pr = d["prk"]; kr2 = d["kr2"]; ckT = d["ckT"]; ktt = d["ktt"]; vt = d["vt"]
                PBb = PB[:, 512:1024].bitcast(BF16)
                for k in range(KC):
                    S.op("pe", I_mm(PB[:, 0:320], d["hTt"][:, k, :], wdkv[:, k, :], k == 0, k == KC - 1),
                         reads=[d["b_hTt"][k], b_w], writes=[self.bPB[0]])
                S.op("act", I_acopy(pf[:], PB[:, 0:320]), reads=[self.bPB[0]], writes=[d["b_pfk"]])
                yield
                self.rms_rstd(W, pf, 1, 256, d["b_pfk"])
                yield
                S.op("dve", I_ts(W["sq"][:, 0:256], pf[:, 0:256], W["rstd"][:, 0:1], None, ALU.mult), reads=[d["b_pfk"], W["b_rstd"]], writes=[W["b_sq"]])
                S.op("dve", I_tt(pr[:], W["sq"][:, 0:256], mkvg[:], ALU.mult), reads=[W["b_sq"], b_w], writes=[d["b_prk"]])
                yield
                kr3 = kr2[:, 0:64].rearrange("p (h d) -> p h d", d=64)
                self.rope(W, pf[:, 256:320].rearrange("p (h d) -> p h d", d=64), 1, 64, cosT[:, t, :], sinT[:, t, :], kr3, d["b_pfk"], d["b_kr2"], b_w)
                S.op("pool", I_copy(kr2[:, 64:128], kr2[:, 0:64]), reads=[d["b_kr2"]], writes=[d["b_kr2"]])
                yield
                for c in range(2):
                    S.op("pe", I_tr(PBb[:, c * 128:(c + 1) * 128], pr[:, c * 128:(c + 1) * 128], self.identB[:]),
                         reads=[d["b_prk"], self.bconst], writes=[self.bPB[1]])
                S.op("pe", I_tr(PBb[:, 256:384], kr2[:], self.identB[:]), reads=[d["b_kr2"], self.bconst], writes=[self.bPB[1]])
                S.op("dve", I_copy(ckT[:], PBb[:, 0:256].rearrange("p (c n) -> p c n", n=128)), reads=[self.bPB[1]], writes=[d["b_ckT"]])
                S.op("dve", I_copy(ktt[:, 8, :], PBb[:, 256:384]), reads=[self.bPB[1]], writes=[d["b_ktt"]])
                yield
                for hh in range(8):
                    for c in range(2):
                        S.op("pe", I_mm(PD[:, hh * 128:(hh + 1) * 128], wukv4[:, c, hh, 0:128], ckT[:, c, :], c == 0, c == 1),
                             reads=[d["b_ckT"], b_w], writes=[self.bPD[hh // 4]])
                S.op("dve", I_copy(ktt[:, 0:8, :], PD[:].rearrange("p (h n) -> p h n", n=128)), reads=self.bPD, writes=[d["b_ktt"]])
                yield
                for half in range(2):
                    for c in range(2):
                        S.op("pe", I_mm(PB[:, half * 512:(half + 1) * 512], ckT[:, c, :], wukv4[:, c, half * 4:(half + 1) * 4, 128:256], c == 0, c == 1),
                             reads=[d["b_ckT"], b_w], writes=[self.bPB[half]])
                S.op("act", I_acopy(vt[:], PB[:]), reads=self.bPB, writes=[d["b_vt"]])
                yield
                cs = slice(t * 128, (t + 1) * 128)
                S.dma("pool", I_dma(self.KT[s, :, :, cs].rearrange("g d n -> d g n"), ktt[:]), reads=[d["b_ktt"]], cowrites=[self.bKT[s]])
                S.dma("pool", I_dma(self.VS[s, :, :, t, :].rearrange("g p d -> p g d"), vt[:].rearrange("p (g d) -> p g d", d=128)),
                      reads=[d["b_vt"]], cowrites=[self.bVS[s]])

            def chain_mla_q(s, t, d):
                W = d["Wq"]; pf = d["pfq"]; pr = d["prq"]; cqT = d["cqT"]; pfr = d["pfr"]; qr2 = d["qr2"]
                for c0, c1 in ((0, 512), (512, 768)):
                    for k in range(KC):
                        S.op("pe", I_mm(PC[:, c0:c1], d["hTt"][:, k, :], wdq[:, k, c0:c1], k == 0, k == KC - 1),
                             reads=[d["b_hTt"][k], b_w], writes=[self.bPC[c0 // 512]])
                S.op("act", I_acopy(pf[:], PC[:, 0:768]), reads=self.bPC, writes=[d["b_pfq"]])
                yield
                self.rms_rstd(W, pf, 1, 768, d["b_pfq"])
                yield
                S.op("dve", I_ts(W["sq"][:, 0:768], pf[:], W["rstd"][:, 0:1], None, ALU.mult), reads=[d["b_pfq"], W["b_rstd"]], writes=[W["b_sq"]])
                S.op("dve", I_tt(pr[:], W["sq"][:, 0:768], mqg[:], ALU.mult), reads=[W["b_sq"], b_w], writes=[d["b_prq"]])
                yield
                for c in range(6):
                    S.op("pe", I_tr(PAb[:, c * 128:(c + 1) * 128], pr[:, c * 128:(c + 1) * 128], self.identB[:]),
                         reads=[d["b_prq"], self.bconst], writes=[self.bPA[0]])
                S.op("dve", I_copy(cqT[:], PAb[:, 0:768].rearrange("p (c n) -> p c n", n=128)), reads=[self.bPA[0]], writes=[d["b_cqT"]])
                yield
                for hh in range(8):
                    for c in range(6):
                        S.op("pe", I_mm(PC[:, hh * 128:(hh + 1) * 128], wuq4[:, c, hh, 0:128], cqT[:, c, :], c == 0, c == 5),
                             reads=[d["b_cqT"], b_w], writes=[self.bPC[hh // 4]])
                S.op("act", I_acopy(d["qtt"][:], PC[:].rearrange("p (h n) -> p h n", n=128)), reads=self.bPC, writes=[d["b_qtt"]])
                for c in range(6):
                    S.op("pe", I_mm(PD[:, 0:512], cqT[:, c, :], wuq4[:, c, :, 128:192], c == 0, c == 5),
                         reads=[d["b_cqT"], b_w], writes=[self.bPD[0]])
                S.op("act", I_acopy(pfr[:], PD[:, 0:512]), reads=[self.bPD[0]], writes=[d["b_pfr"]])
                yield
                src8 = pfr[:].rearrange("p (h d) -> p h d", d=64)
                dst4 = qr2[:].rearrange("p (c x) -> p c x", x=256)
                self.rope(W, src8[:, 0::2, :], 4, 64, cosT[:, t, :], sinT[:, t, :], dst4[:, :, 0:64], d["b_pfr"], d["b_qr2"], b_w)
                yield
                self.rope(W, src8[:, 1::2, :], 4, 64, cosT[:, t, :], sinT[:, t, :], dst4[:, :, 192:256], d["b_pfr"], d["b_qr2"], b_w)
                yield
                for c in range(8):
                    S.op("pe", I_tr(PAb[:, 1024 + c * 128:1024 + (c + 1) * 128], qr2[:, c * 128:(c + 1) * 128], self.identB[:]),
                         reads=[d["b_qr2"], self.bconst], writes=[self.bPA[1]])
                S.op("dve", I_copy(d["qrt"][:], PAb[:, 1024:2048].rearrange("p (c n) -> p c n", n=128)), reads=[self.bPA[1]], writes=[d["b_qrt"]])
                yield
                cs = slice(t * 128, (t + 1) * 128)
                S.dma("pool", I_dma(self.QT[s, :, :, cs].rearrange("h d n -> d h n"), d["qtt"][:]), reads=[d["b_qtt"]], cowrites=[self.bQT[s]])
                S.dma("pool", I_dma(self.QR[s, :, :, cs].rearrange("h d n -> d h n"), d["qrt"][:]), reads=[d["b_qrt"]], cowrites=[self.bQT[s]])

            def chain_mla(s, t, d):
                j = NS if t < 2 else s
                r0 = s * NR + t * 128
                S.dma("sp", I_dma(d["xin"][:], self.XA[r0:r0 + 128, :]), reads=[self.bXA[s]], writes=[d["b_xin"]])
                self.transpose_mod(d["xin"], 1, 0, j, d["hTt"], d["b_xin"], d["b_hTt"])
                yield
                for _ in zip_longest(chain_mla_k(s, t, d), chain_mla_q(s, t, d)):
                    yield

            chain = chain_gqa if gqa else chain_mla
            work = [(s, t) for s in range(NS) for t in range(NT)]
            for w0 in range(0, len(work), NSLOT):
                gens = [chain(s, t, slots[i]) for i, (s, t) in enumerate(work[w0:w0 + NSLOT])]
                for _ in zip_longest(*gens):
                    pass
            S.run_block()

    def phase_A2(self, l, last):
        nc, S, ext = self.nc, self.S, self.ext
        gqa = (l % 2 == 0)
        PA, PB, PC, PD = self.PA, self.PB, self.PC, self.PD
        SCALE = 128 ** -0.5 if gqa else 192 ** -0.5
        NKH = 2 if gqa else 8
        QS = 512
        with ExitStack() as es:
            T = lambda n, sh, dt: es.enter_context(nc.sbuf_tensor(f"{n}_L{l}", sh, dt))
            wo = T("a_wo", [128, 8, D], BF16)
            b_w = Buf()
            S.dma("pool", I_dma(wo[:], ext[f"wo{l}"].rearrange("(h p) c -> p h c", p=128)), writes=[b_w])
            lng = T("a_lng", [128, D], F32); lnb = T("a_lnb", [128, D], F32)
            wr = T("a_wr", [128, KC, E], F32)
            S.dma("sp", I_dma(lng[:], ext[f"lnmg{l}"][0, :].partition_broadcast(128)), writes=[b_w])
            S.dma("sp", I_dma(lnb[:], ext[f"lnmb{l}"][0, :].partition_broadcast(128)), writes=[b_w])
            S.dma("sp", I_dma(wr[:], ext[f"rw{l}"].rearrange("(k p) e -> p k e", p=128)), writes=[b_w])
            G1c = T("a_G1c", [128, D], F32); G1l = T("a_G1l", [128, D], F32)
            b_G1c, b_G1l = bufs(2)
            S.dma("sp", I_dma(G1c[:], self.GS[0, NS]), reads=[self.bGS], writes=[b_G1c])
            W = {}
            for n, sh in (("t", [128, D]), ("st", [128, 2, 6]), ("mv", [128, 2]), ("lnv1", [128, 1]),
                          ("rstd1", [128, 1]), ("nmr", [128, 1]), ("epsc", [128, 1])):
                W[n] = T("a_" + n, sh, F32)
            for n in ("t", "st", "mv", "lnv1", "rstd1", "nmr", "eps"):
                W["b_" + n] = Buf()
            W["b_ln"] = b_w
            S.op("pool", I_memset(W["epsc"][:], EPS), writes=[W["b_eps"]])
            xblk = T("a_xblk", [128, 4, D], F32); b_xblk = bufs(4)
            qTb = [T(f"a_qTb{i}", [128, 8, 512], BF16) for i in range(2)]; b_qTb = bufs(2)
            oTb = T("a_oTb", [128, 8, 512], BF16); b_oTb = Buf()
            kTg = [T(f"a_kTg{i}", [128, NR], BF16) for i in range(2)]; b_kTg = bufs(2)
            Vg = [T(f"a_Vg{i}", [128, NT, 128], BF16) for i in range(2)]; b_Vg = bufs(2)
            PT = [T(f"a_PT{i}", [128, NT, QS], BF16) for i in range(2)]; b_PT = [bufs(NT) for _ in range(2)]
            rden = T("a_rden", [128, 512], F32); b_rden = Buf()
            h2T = T("a_h2T", [128, KC, 128], F32); b_h2T = Buf()
            sm = {n: T("a_sm_" + n, sh, F32) for n, sh in (("mx", [128, 1]), ("nmx", [128, 1]), ("ex", [128, E]),
                                                           ("ssum", [128, 1]), ("rs", [128, 1]), ("aff", [128, E]))}
            b_sm = {n: Buf() for n in sm}
            affT = T("a_affT", [16, 128], F32); b_affT = Buf()
            if not gqa:
                qrT = [T(f"a_qrT{i}", [128, 8, 512], BF16) for i in range(2)]; b_qrT = bufs(2)
                krT = T("a_krT", [128, NR], BF16); b_krT = Buf()
            CW = []
            for i in range(4):
                d = {n: T(f"a_c{i}_{n}", sh, F32) for n, sh in (("t", [128, D]), ("st", [128, 2, 6]), ("mv", [128, 2]), ("lnv", [128, 1]),
                                                               ("rstd", [128, 1]), ("nmr", [128, 1]), ("h2T", [128, KC, 128]),
                                                               ("mx", [128, 1]), ("nmx", [128, 1]), ("ex", [128, E]), ("ssum", [128, 1]),
                                                               ("rs", [128, 1]), ("aff", [128, E]), ("affT", [16, 128]))}
                for n in ("t", "st", "mv", "lnv", "rstd", "nmr", "mx", "nmx", "ex", "ssum", "rs", "aff", "affT"):
                    d["b_" + n] = Buf()
                d["b_h2T"] = bufs(KC)
                CW.append(d)

            def chain_block(s, tiles, isctx, j, G1, bG1):
                n = len(tiles)
                info = []
                for ti, t in enumerate(tiles):
                    P, bP = (PB, self.bPB) if ti % 2 == 0 else (PD, self.bPD)
                    PT_, bPT_ = (PA, self.bPA) if ti % 2 == 0 else (PC, self.bPC)
                    info.append((ti, t, s * NR + t * 128, CW[ti], P, bP, PT_, bPT_))
                for ti, t, r0, d, P, bP, PX, bPX in info:
                    for half in range(2):
                        for hh in range(8):
                            S.op("pe", I_mm(P[:, half * 512:(half + 1) * 512], oTb[:, hh, ti * 128:(ti + 1) * 128],
                                            wo[:, hh, half * 512:(half + 1) * 512], hh == 0, hh == 7),
                                 reads=[b_oTb, b_w], writes=[bP[half]])
                    S.op("dve", I_tt(d["t"][:], P[:], G1[:], ALU.mult), reads=bP + [bG1], writes=[d["b_t"]])
                yield
                for ti, t, r0, d, P, bP, PX, bPX in info:
                    S.op("dve", I_stt(d["t"][:], xblk[:, ti, :], ALPHA, d["t"][:], ALU.mult, ALU.add), reads=[b_xblk[ti], d["b_t"]], writes=[d["b_t"]])
                yield
                for ti, t, r0, d, P, bP, PX, bPX in info:
                    S.op("dve", (lambda d: lambda h: h.bn_stats(out=d["st"][:, 0, :], in_=d["t"][:, 0:512]))(d), reads=[d["b_t"]], writes=[d["b_st"]])
                    S.op("dve", (lambda d: lambda h: h.bn_stats(out=d["st"][:, 1, :], in_=d["t"][:, 512:1024]))(d), reads=[d["b_t"]], writes=[d["b_st"]])
                yield
                for ti, t, r0, d, P, bP, PX, bPX in info:
                    S.op("dve", (lambda d: lambda h: h.bn_aggr(out=d["mv"][:], in_=d["st"][:].rearrange("p a b -> p (a b)")))(d), reads=[d["b_st"]], writes=[d["b_mv"]])
                yield
                for ti, t, r0, d, P, bP, PX, bPX in info:
                    S.op("act", I_act(d["lnv"][:], d["mv"][:, 1:2], AF.Ln, scale=1.0, bias=W["epsc"][:, 0:1]), reads=[d["b_mv"], W["b_eps"]], writes=[d["b_lnv"]])
                yield
                for ti, t, r0, d, P, bP, PX, bPX in info:
                    S.op("act", I_act(d["rstd"][:], d["lnv"][:], AF.Exp, scale=-0.5), reads=[d["b_lnv"]], writes=[d["b_rstd"]])
                yield
                for ti, t, r0, d, P, bP, PX, bPX in info:
                    S.op("dve", I_ts(d["nmr"][:], d["mv"][:, 0:1], d["rstd"][:, 0:1], -1.0, ALU.mult, ALU.mult), reads=[d["b_mv"], d["b_rstd"]], writes=[d["b_nmr"]])
                yield
                for ti, t, r0, d, P, bP, PX, bPX in info:
                    S.op("act", I_act(d["t"][:], d["t"][:], AF.Identity, scale=d["rstd"][:, 0:1], bias=d["nmr"][:, 0:1]),
                         reads=[d["b_t"], d["b_rstd"], d["b_nmr"]], writes=[d["b_t"]])
                yield
                for ti, t, r0, d, P, bP, PX, bPX in info:
                    S.op("dve", I_tt(d["t"][:], d["t"][:], lng[:], ALU.mult), reads=[d["b_t"], b_w], writes=[d["b_t"]])
                yield
                for ti, t, r0, d, P, bP, PX, bPX in info:
                    S.op("pool", I_tt(d["t"][:], d["t"][:], lnb[:], ALU.add), reads=[d["b_t"], b_w], writes=[d["b_t"]])
                yield
                for ti, t, r0, d, P, bP, PX, bPX in info:
                    S.dma("sp", I_dma(self.XB[r0:r0 + 128, :], d["t"][:]), reads=[d["b_t"]], cowrites=[self.bXB[s]])
                    S.dma("sp", I_dma(self.FA[r0:r0 + 128, :], self.zeros[:]), reads=[self.bconst],
                          cowrites=[self.bFAc if isctx else self.bFAl[s]])
                yield
                for ti, t, r0, d, P, bP, PX, bPX in info:
                    for k in range(KC):
                        S.op("pe", I_tr(PX[:, k * 128:(k + 1) * 128], d["t"][:, k * 128:(k + 1) * 128], self.identF[:]),
                             reads=[d["b_t"], self.bconst], writes=[bPX[k // 4]])
                    for k in range(KC):
                        S.op("act", I_act(d["h2T"][:, k, :], PX[:, k * 128:(k + 1) * 128], AF.Identity,
                                          scale=self.modT[:, 3, k, j:j + 1], bias=self.modT[:, 2, k, j:j + 1]),
                             reads=[bPX[k // 4], self.bmodT], writes=[d["b_h2T"][k]])
                yield
                for ti, t, r0, d, P, bP, PX, bPX in info:
                    for k in range(KC):
                        S.op("pe", I_mm(P[:, 32 * (ti // 2):32 * (ti // 2) + E], d["h2T"][:, k, :], wr[:, k, :], k == 0, k == KC - 1),
                             reads=[d["b_h2T"][k], b_w], writes=[bP[0]])
                    S.op("dve", (lambda d, P, lc: lambda h: h.tensor_reduce(out=d["mx"][:], in_=P[:, lc:lc + E], axis=AX.X, op=ALU.max))(d, P, 32 * (ti // 2)),
                         reads=[bP[0]], writes=[d["b_mx"]])
                yield
                for ti, t, r0, d, P, bP, PX, bPX in info:
                    S.op("dve", I_ts(d["nmx"][:], d["mx"][:], -1.0, None, ALU.mult), reads=[d["b_mx"]], writes=[d["b_nmx"]])
                yield
                for ti, t, r0, d, P, bP, PX, bPX in info:
                    S.op("act", I_act(d["ex"][:], P[:, 32 * (ti // 2):32 * (ti // 2) + E], AF.Exp, scale=1.0, bias=d["nmx"][:, 0:1], accum_out=d["ssum"][:]),
                         reads=[bP[0], d["b_nmx"]], writes=[d["b_ex"], d["b_ssum"]])
                yield
                for ti, t, r0, d, P, bP, PX, bPX in info:
                    S.op("dve", (lambda d: lambda h: h.reciprocal(out=d["rs"][:], in_=d["ssum"][:]))(d), reads=[d["b_ssum"]], writes=[d["b_rs"]])
                yield
                for ti, t, r0, d, P, bP, PX, bPX in info:
                    S.op("dve", I_ts(d["aff"][:], d["ex"][:], d["rs"][:, 0:1], None, ALU.mult), reads=[d["b_ex"], d["b_rs"]], writes=[d["b_aff"]])
                yield
                for ti, t, r0, d, P, bP, PX, bPX in info:
                    S.op("pe", I_tr(P[0:E, 512:640], d["aff"][:], self.identF[:]), reads=[d["b_aff"], self.bconst], writes=[bP[1]])
                    S.op("act", I_acopy(d["affT"][:], P[0:E, 512:640]), reads=[bP[1]], writes=[d["b_affT"]])
                yield
                for ti, t, r0, d, P, bP, PX, bPX in info:
                    if isctx:
                        dst = self.AFC[s * E:(s + 1) * E, t * 128:(t + 1) * 128]
                    else:
                        dst = self.AFL[s * E:(s + 1) * E, (t - 2) * 128:(t - 1) * 128]
                    S.dma("sp", I_dma(dst, d["affT"][:]), reads=[d["b_affT"]], cowrites=[self.bAF])

            kvload = 0
            blkcnt = 0
            blocks_of = lambda: ([] if last else [[0, 1]]) + [[2 + 4 * b_ + i for i in range(4)] for b_ in range(4)]
            allblocks = [(s_, tl) for s_ in range(NS) for tl in blocks_of()]
            pre_kb = {}

            def prefetch(bi_):
                nonlocal kvload
                s_, tl = allblocks[bi_]
                nq_ = len(tl) * 128
                nkc_ = 2 if tl[0] < 2 else NT
                c0_ = tl[0] * 128
                qb_ = bi_ % 2
                S.dma("sp", I_dma(qTb[qb_][:, :, 0:nq_], self.QT[s_, :, :, c0_:c0_ + nq_].rearrange("h d n -> d h n")),
                      reads=[self.bQT[s_]], writes=[b_qTb[qb_]])
                if not gqa:
                    S.dma("sp", I_dma(qrT[qb_][:, :, 0:nq_], self.QR[s_, :, :, c0_:c0_ + nq_].rearrange("h d n -> d h n")),
                          reads=[self.bQT[s_]], writes=[b_qrT[qb_]])
                kb = kvload % 2; kvload += 1
                pre_kb[bi_] = kb
                S.dma("sp", I_dma(kTg[kb][:, 0:nkc_ * 128], self.KT[s_, 0, :, 0:nkc_ * 128]), reads=[self.bKT[s_]], writes=[b_kTg[kb]])
                S.dma("sp", I_dma(Vg[kb][:, 0:nkc_, :], self.VS[s_, 0, :, 0:nkc_, :]), reads=[self.bVS[s_]], writes=[b_Vg[kb]])

            bi = 0
            prefetch(0)
            for s in range(NS):
                S.dma("sp", I_dma(G1l[:], self.GS[0, s]), reads=[self.bGS], writes=[b_G1l])
                if not gqa:
                    S.dma("sp", I_dma(krT[:], self.KT[s, 8, :, :]), reads=[self.bKT[s]], writes=[b_krT])
                blocks = ([] if last else [[0, 1]]) + [[2 + 4 * b + i for i in range(4)] for b in range(4)]
                for tiles in blocks:
                    isctx = tiles[0] < 2
                    j = NS if isctx else s
                    nq = len(tiles) * 128
                    nkc = 2 if isctx else NT
                    G1 = G1c if isctx else G1l
                    bG1 = b_G1c if isctx else b_G1l
                    assert allblocks[bi] == (s, tiles)
                    qb = bi % 2
                    for ti, t in enumerate(tiles):
                        r0 = s * NR + t * 128
                        S.dma("sp", I_dma(xblk[:, ti, :], self.XA[r0:r0 + 128, :]), reads=[self.bXA[s]], writes=[b_xblk[ti]])
                    nk = nkc * 128
                    kvslot = {0: pre_kb[bi]}

                    def load_kv(g):
                        nonlocal kvload
                        kb = kvload % 2; kvload += 1
                        kvslot[g] = kb
                        S.dma("sp", I_dma(kTg[kb][:, 0:nk], self.KT[s, g, :, 0:nk]), reads=[self.bKT[s]], writes=[b_kTg[kb]])
                        S.dma("sp", I_dma(Vg[kb][:, 0:nkc, :], self.VS[s, g, :, 0:nkc, :]), reads=[self.bVS[s]], writes=[b_Vg[kb]])

                    units = []
                    for g in range(NKH):
                        for hh in ([g * 4 + i for i in range(4)] if gqa else [g]):
                            for q0 in range(0, nq, QS):
                                units.append((g, hh, q0, min(QS, nq - q0)))

                    def emit_S(u, i, kc):
                        g, hh, q0, qn = u
                        kb = kvslot[g]
                        pc = PC[:, (kc % 2) * 512:(kc % 2) * 512 + qn]
                        if gqa:
                            S.op("pe", I_mm(pc, kTg[kb][:, kc * 128:(kc + 1) * 128], qTb[qb][:, hh, q0:q0 + qn], True, True),
                                 reads=[b_kTg[kb], b_qTb[qb]], writes=[self.bPC[kc % 2]])
                        else:
                            S.op("pe", I_mm(pc, kTg[kb][:, kc * 128:(kc + 1) * 128], qTb[qb][:, hh, q0:q0 + qn], True, False),
                                 reads=[b_kTg[kb], b_qTb[qb]], writes=[self.bPC[kc % 2]])
                            S.op("pe", I_mm(pc, krT[:, kc * 128:(kc + 1) * 128], qrT[qb][:, hh, q0:q0 + qn], False, True),
                                 reads=[b_krT, b_qrT[qb]], writes=[self.bPC[kc % 2]])
                        S.op("act", I_act(PT[i % 2][:, kc, 0:qn], pc, AF.Exp, scale=SCALE), reads=[self.bPC[kc % 2]],
                             writes=[b_PT[i % 2][kc]])

                    def emit_PV(u, i, kc):
                        g, hh, q0, qn = u
                        kb = kvslot[g]
                        acc, bacc = (PD, self.bPD) if i % 2 == 0 else (PB, self.bPB)
                        S.op("pe", I_mm(acc[:, 0:qn], Vg[kb][:, kc, :], PT[i % 2][:, kc, 0:qn], kc == 0, kc == nkc - 1),
                             reads=[b_Vg[kb], b_PT[i % 2][kc]], writes=[bacc[0]])
                        S.op("pe", I_mm(acc[:, 512:512 + qn], self.onesB[:], PT[i % 2][:, kc, 0:qn], kc == 0, kc == nkc - 1),
                             reads=[self.bconst, b_PT[i % 2][kc]], writes=[bacc[1]])

                    def emit_norm(u, i):
                        g, hh, q0, qn = u
                        acc, bacc = (PD, self.bPD) if i % 2 == 0 else (PB, self.bPB)
                        S.op("dve", lambda h: h.reciprocal(out=rden[:, 0:qn], in_=acc[:, 512:512 + qn]),
                             reads=[bacc[1]], writes=[b_rden])
                        S.op("dve", I_tt(oTb[:, hh, q0:q0 + qn], acc[:, 0:qn], rden[:, 0:qn], ALU.mult),
                             reads=[bacc[0], b_rden], writes=[b_oTb])

                    for kc in range(nkc):
                        emit_S(units[0], 0, kc)
                    for i, u in enumerate(units):
                        nxt = units[i + 1] if i + 1 < len(units) else None
                        if (i == 0 or units[i - 1][0] != u[0]) and u[0] + 1 < NKH:
                            load_kv(u[0] + 1)
                        for kc in range(nkc):
                            if nxt is not None:
                                emit_S(nxt, i + 1, kc)
                            emit_PV(u, i, kc)
                        emit_norm(u, i)
                    if bi + 1 < len(allblocks):
                        prefetch(bi + 1)
                    bi += 1
                    for _ in chain_block(s, tiles, isctx, j, G1, bG1):
                        pass
            S.run_block()

    def phase_R(self, l, last):
        nc, S, ext = self.nc, self.S, self.ext
        PA = self.PA
        es = self.es_R = ExitStack()
        T = lambda n, sh, dt: es.enter_context(nc.sbuf_tensor(f"{n}_L{l}", sh, dt))
        self.idxT = T("r_idxT", [128, 2, 64], I32)
        self.gT = T("r_gT", [128, 2, 64], F32)
        self.idxC = T("r_idxC", [128, E], I32)
        self.gC = T("r_gC", [128, E], F32)
        self.b_idx = Buf()
        with ExitStack() as es2:
            T2 = lambda n, sh, dt: es2.enter_context(nc.sbuf_tensor(f"{n}_L{l}", sh, dt))
            aw = T2("r_aw", [64, SEQ], F32); tv = T2("r_tv", [64, CAPL], F32); ti = T2("r_ti", [64, CAPL], U32)
            tf = T2("r_tf", [64, CAPL], F32); offs = T2("r_offs", [64, 2], F32)
            b_aw, b_tv, b_ti, b_tf, b_offs = bufs(5)
            S.dma("sp", I_dma(offs[:], ext["offs"]), writes=[b_offs])
            S.dma("sp", I_dma(aw[:], self.AFL), reads=[self.bAF], writes=[b_aw])
            for it in range(CAPL // 8):
                sl = slice(it * 8, it * 8 + 8)
                S.op("dve", lambda h, sl=sl: h.max(out=tv[:, sl], in_=aw[:]), reads=[b_aw], writes=[b_tv])
                S.op("dve", lambda h, sl=sl: h.max_index(out=ti[:, sl], in_max=tv[:, sl], in_values=aw[:]), reads=[b_aw, b_tv], writes=[b_ti])
                S.op("dve", lambda h, sl=sl: h.match_replace(out=aw[:], in_to_replace=tv[:, sl], in_values=aw[:], imm_value=-1.0),
                     reads=[b_tv, b_ti, b_aw], writes=[b_aw])
            S.op("dve", I_copy(tf[:], ti[:]), reads=[b_ti], writes=[b_tf])
            S.op("dve", I_ts(tf[:], tf[:], offs[:, 0:1], None, ALU.add), reads=[b_tf, b_offs], writes=[b_tf])
            for rb in range(2):
                S.op("pe", I_tr(PA[:, rb * 64:(rb + 1) * 64], tf[:, rb * 128:(rb + 1) * 128], self.identF[0:64, 0:64]),
                     reads=[b_tf, self.bconst], writes=self.bPA)
                S.op("pe", I_tr(PA[:, 128 + rb * 64:128 + (rb + 1) * 64], tv[:, rb * 128:(rb + 1) * 128], self.identF[0:64, 0:64]),
                     reads=[b_tv, self.bconst], writes=self.bPA)
            S.op("dve", I_copy(self.idxT[:], PA[:, 0:128].rearrange("p (r c) -> p r c", c=64)), reads=self.bPA, writes=[self.b_idx])
            S.op("act", I_acopy(self.gT[:], PA[:, 128:256].rearrange("p (r c) -> p r c", c=64)), reads=self.bPA, writes=[self.b_idx])
            if not last:
                ac = T2("r_ac", [64, CTX], F32); cv = T2("r_cv", [64, CAPC], F32); ci = T2("r_ci", [64, CAPC], U32)
                cf = T2("r_cf", [64, CAPC], F32); ciT = T2("r_ciT", [32, 64], I32); cgT = T2("r_cgT", [32, 64], F32)
                b_ac, b_cv, b_ci, b_cf, b_ciT = bufs(5)
                S.dma("sp", I_dma(ac[:], self.AFC), reads=[self.bAF], writes=[b_ac])
                for it in range(CAPC // 8):
                    sl = slice(it * 8, it * 8 + 8)
                    S.op("dve", lambda h, sl=sl: h.max(out=cv[:, sl], in_=ac[:]), reads=[b_ac], writes=[b_cv])
                    S.op("dve", lambda h, sl=sl: h.max_index(out=ci[:, sl], in_max=cv[:, sl], in_values=ac[:]), reads=[b_ac, b_cv], writes=[b_ci])
                    S.op("dve", lambda h, sl=sl: h.match_replace(out=ac[:], in_to_replace=cv[:, sl], in_values=ac[:], imm_value=-1.0),
                         reads=[b_cv, b_ci, b_ac], writes=[b_ac])
                S.op("dve", I_copy(cf[:], ci[:]), reads=[b_ci], writes=[b_cf])
                S.op("dve", I_ts(cf[:], cf[:], offs[:, 1:2], None, ALU.add), reads=[b_cf, b_offs], writes=[b_cf])
                S.op("pe", I_tr(PA[0:32, 256:320], cf[:], self.identF[0:64, 0:64]), reads=[b_cf, self.bconst], writes=self.bPA)
                S.op("pe", I_tr(PA[0:32, 320:384], cv[:], self.identF[0:64, 0:64]), reads=[b_cv, self.bconst], writes=self.bPA)
                S.op("dve", I_copy(ciT[:], PA[0:32, 256:320]), reads=self.bPA, writes=[b_ciT])
                S.op("act", I_acopy(cgT[:], PA[0:32, 320:384]), reads=self.bPA, writes=[b_ciT])
                for s in range(NS):
                    S.dma("sp", I_dma(self.idxC[s * 32:(s + 1) * 32, :], ciT[:, s * E:(s + 1) * E]), reads=[b_ciT], writes=[self.b_idx])
                    S.dma("sp", I_dma(self.gC[s * 32:(s + 1) * 32, :], cgT[:, s * E:(s + 1) * E]), reads=[b_ciT], writes=[self.b_idx])
            S.run_block()

    def phase_C(self, l, last):
        nc, S, ext = self.nc, self.S, self.ext
        PA, PB, PC, PD = self.PA, self.PB, self.PC, self.PD
        ntile = 2 * NS + (0 if last else 1)
        ncols = ntile * 128
        cblocks = [(c0, min(c0 + 512, ncols)) for c0 in range(0, ncols, 512)]
        with ExitStack() as es:
            T = lambda n, sh, dt: es.enter_context(nc.sbuf_tensor(f"{n}_L{l}", sh, dt))
            wg = [T(f"c_wg{i}", [128, KC, D], BF16) for i in range(2)]
            wu = [T(f"c_wu{i}", [128, KC, D], BF16) for i in range(2)]
            wd = [T(f"c_wd{i}", [128, KC, D], BF16) for i in range(2)]
            b_wt = bufs(2)
            xs = [T(f"c_xs{i}", [128, D], F32) for i in range(9)]; b_xs = bufs(9)
            xsT2 = [T(f"c_xsT{i}", [128, KC, 9 * 128], BF16) for i in range(2)]; b_xsT2 = [bufs(KC) for _ in range(2)]
            hT = T("c_hT", [128, KC, 9 * 128], BF16); b_hT = Buf()
            sil = [T(f"c_sil{i}", [128, 512], F32) for i in range(2)]; b_sil = bufs(2)
            yg = [T(f"c_yg{i}", [128, D], F32) for i in range(2)]; b_yg = bufs(2)

            def load_w(e):
                i = e % 2
                for wt, nm in ((wg[i], "wg"), (wu[i], "wu"), (wd[i], "wd")):
                    S.dma("pool", I_dma(wt[:], ext[f"{nm}{l}"][e].rearrange("(k p) f -> p k f", p=128)), writes=[b_wt[i]])

            def tile_info(jt):
                if jt < 2 * NS:
                    s, rb = jt // 2, jt % 2
                    return s, self.idxT[:, rb, s * E:(s + 1) * E], self.gT[:, rb, s * E:(s + 1) * E], self.bFAl[s], self.bXB[s:s + 1]
                return NS, self.idxC[:, :], self.gC[:, :], self.bFAc, self.bXB

            def gathers(e):
                for jt in range(ntile):
                    jmod, idx, gate, bfa, bxb = tile_info(jt)
                    S.dma("pool", (lambda x, idx, e: (lambda h: h.indirect_dma_start(
                        out=x[:], out_offset=None, in_=self.XB,
                        in_offset=bass.IndirectOffsetOnAxis(ap=idx[:, e:e + 1], axis=0))))(xs[jt], idx, e),
                          reads=[self.b_idx] + list(bxb), writes=[b_xs[jt]])

            def prep(e):
                xsT = xsT2[e % 2]; b_xsT = b_xsT2[e % 2]
                for jt in range(ntile):
                    jmod, idx, gate, bfa, bxb = tile_info(jt)
                    x = xs[jt]; bx = b_xs[jt]
                    for k in range(KC):
                        S.op("pe", I_tr(PA[:, k * 128:(k + 1) * 128], x[:, k * 128:(k + 1) * 128], self.identF[:]),
                             reads=[bx, self.bconst], writes=[self.bPA[k // 4]])
                    for k in range(KC):
                        sc = self.modT[:, 3, k, jmod:jmod + 1]; sh = self.modT[:, 2, k, jmod:jmod + 1]
                        S.op("act", I_act(xsT[:, k, jt * 128:(jt + 1) * 128], PA[:, k * 128:(k + 1) * 128], AF.Identity, scale=sc, bias=sh),
                             reads=[self.bPA[k // 4], self.bmodT], writes=[b_xsT[k]])
                    yield

            load_w(0)
            gathers(0)
            for _ in prep(0):
                pass
            cnt = 0
            for e in range(E):
                i = e % 2
                xsT = xsT2[e % 2]; b_xsT = b_xsT2[e % 2]
                if e + 1 < E:
                    gathers(e + 1)
                    load_w(e + 1)
                pg = prep(e + 1) if e + 1 < E else iter(())
                gcnt = 0
                for f in range(KC):
                    for (c0, c1) in cblocks:
                        w_ = c1 - c0
                        pb = gcnt % 2; gcnt += 1
                        pa_ = PB[:, pb * 512:pb * 512 + w_]; pu_ = PC[:, pb * 512:pb * 512 + w_]
                        for k in range(KC):
                            S.op("pe", I_mm(pa_, wg[i][:, k, f * 128:(f + 1) * 128], xsT[:, k, c0:c1], k == 0, k == KC - 1),
                                 reads=[b_wt[i], b_xsT[k]], writes=[self.bPBh[pb]])
                        for k in range(KC):
                            S.op("pe", I_mm(pu_, wu[i][:, k, f * 128:(f + 1) * 128], xsT[:, k, c0:c1], k == 0, k == KC - 1),
                                 reads=[b_wt[i], b_xsT[k]], writes=[self.bPC[pb]])
                        S.op("act", I_act(sil[pb][:, 0:w_], pa_, AF.Silu), reads=[self.bPBh[pb]], writes=[b_sil[pb]])
                        S.op("dve", I_tt(hT[:, f, c0:c1], pu_, sil[pb][:, 0:w_], ALU.mult),
                             reads=[self.bPC[pb], b_sil[pb]], writes=[b_hT])
                        if gcnt % 2 == 0:
                            next(pg, None)
                for _ in pg:
                    pass
                for jt in range(ntile):
                    jmod, idx, gate, bfa, bxb = tile_info(jt)
                    for half in range(2):
                        for f in range(KC):
                            S.op("pe", I_mm(PD[:, half * 512:(half + 1) * 512], hT[:, f, jt * 128:(jt + 1) * 128],
                                            wd[i][:, f, half * 512:(half + 1) * 512], f == 0, f == KC - 1),
                                 reads=[b_hT, b_wt[i]], writes=[self.bPD[half]])
                    y = yg[jt % 2]; by = b_yg[jt % 2]
                    S.op("act", I_act(y[:, 0:512], PD[:, 0:512], AF.Identity, scale=gate[:, e:e + 1]),
                         reads=[self.bPD[0], self.b_idx], writes=[by])
                    S.op("dve", I_ts(y[:, 512:1024], PD[:, 512:1024], gate[:, e:e + 1], None, ALU.mult),
                         reads=[self.bPD[1], self.b_idx], writes=[by])
                    S.dma("pool", (lambda y, idx, e: (lambda h: h.indirect_dma_start(
                        out=self.FA, out_offset=bass.IndirectOffsetOnAxis(ap=idx[:, e:e + 1], axis=0),
                        in_=y[:], in_offset=None, compute_op=ALU.add)))(y, idx, e),
                          reads=[by, self.b_idx], writes=[bfa])
            S.run_block()
        self.es_R.close()

    def phase_D(self, l, last, is_out):
        nc, S, ext = self.nc, self.S, self.ext
        GD = 6
        with ExitStack() as es:
            T = lambda n, sh, dt: es.enter_context(nc.sbuf_tensor(f"{n}_L{l}", sh, dt))
            lng = T("d_lng", [128, D], F32); lnb = T("d_lnb", [128, D], F32)
            G2c = T("d_G2c", [128, D], F32); G2l = T("d_G2l", [128, D], F32)
            epsc = T("d_eps", [128, 1], F32)
            b_w, b_G2c, b_G2l, b_eps = bufs(4)
            S.dma("sp", I_dma(lng[:], ext[f"lnfg{l}"][0, :].partition_broadcast(128)), writes=[b_w])
            S.dma("sp", I_dma(lnb[:], ext[f"lnfb{l}"][0, :].partition_broadcast(128)), writes=[b_w])
            S.dma("sp", I_dma(G2c[:], self.GS[1, NS]), reads=[self.bGS], writes=[b_G2c])
            S.op("pool", I_memset(epsc[:], EPS), writes=[b_eps])
            slots = []
            for i in range(GD):
                d = {n: T(f"d_{n}{i}", sh, F32) for n, sh in (("x1", [128, D]), ("f", [128, D]), ("t", [128, D]), ("st", [128, 2, 6]),
                                                              ("mv", [128, 2]), ("lnv", [128, 1]), ("rstd", [128, 1]), ("nmr", [128, 1]))}
                for n in ("x1", "f", "t", "st", "mv", "lnv", "rstd", "nmr"):
                    d["b_" + n] = Buf()
                slots.append(d)
            for s in range(NS):
                S.dma("sp", I_dma(G2l[:], self.GS[1, s]), reads=[self.bGS], writes=[b_G2l])
                tl = list(range(2 if last else 0, NT))
                for g0 in range(0, len(tl), GD):
                    grp = [(tl[g0 + i], slots[i]) for i in range(min(GD, len(tl) - g0))]
                    info = []
                    for t, d in grp:
                        isctx = t < 2
                        r0 = s * NR + t * 128
                        info.append((t, d, isctx, r0, (G2c if isctx else G2l), (b_G2c if isctx else b_G2l)))
                    for t, d, isctx, r0, G2, bG2 in info:
                        S.dma("sp", I_dma(d["x1"][:], self.XB[r0:r0 + 128, :]), reads=[self.bXB[s]], writes=[d["b_x1"]])
                        S.dma("sp", I_dma(d["f"][:], self.FA[r0:r0 + 128, :]), reads=[self.bFAc if isctx else self.bFAl[s]], writes=[d["b_f"]])
                    for t, d, isctx, r0, G2, bG2 in info:
                        S.op("pool", I_tt(d["t"][:], d["f"][:], G2[:], ALU.mult), reads=[d["b_f"], bG2], writes=[d["b_t"]])
                    for t, d, isctx, r0, G2, bG2 in info:
                        S.op("dve", I_stt(d["t"][:], d["x1"][:], ALPHA, d["t"][:], ALU.mult, ALU.add), reads=[d["b_x1"], d["b_t"]], writes=[d["b_t"]])
                    for t, d, isctx, r0, G2, bG2 in info:
                        S.op("dve", (lambda d: lambda h: h.bn_stats(out=d["st"][:, 0, :], in_=d["t"][:, 0:512]))(d), reads=[d["b_t"]], writes=[d["b_st"]])
                        S.op("dve", (lambda d: lambda h: h.bn_stats(out=d["st"][:, 1, :], in_=d["t"][:, 512:1024]))(d), reads=[d["b_t"]], writes=[d["b_st"]])
                    for t, d, isctx, r0, G2, bG2 in info:
                        S.op("dve", (lambda d: lambda h: h.bn_aggr(out=d["mv"][:], in_=d["st"][:].rearrange("p a b -> p (a b)")))(d), reads=[d["b_st"]], writes=[d["b_mv"]])
                    for t, d, isctx, r0, G2, bG2 in info:
                        S.op("act", I_act(d["lnv"][:], d["mv"][:, 1:2], AF.Ln, scale=1.0, bias=epsc[:, 0:1]), reads=[d["b_mv"], b_eps], writes=[d["b_lnv"]])
                    for t, d, isctx, r0, G2, bG2 in info:
                        S.op("act", I_act(d["rstd"][:], d["lnv"][:], AF.Exp, scale=-0.5), reads=[d["b_lnv"]], writes=[d["b_rstd"]])
                    for t, d, isctx, r0, G2, bG2 in info:
                        S.op("dve", I_ts(d["nmr"][:], d["mv"][:, 0:1], d["rstd"][:, 0:1], -1.0, ALU.mult, ALU.mult), reads=[d["b_mv"], d["b_rstd"]], writes=[d["b_nmr"]])
                    for t, d, isctx, r0, G2, bG2 in info:
                        S.op("act", I_act(d["t"][:], d["t"][:], AF.Identity, scale=d["rstd"][:, 0:1], bias=d["nmr"][:, 0:1]),
                             reads=[d["b_t"], d["b_rstd"], d["b_nmr"]], writes=[d["b_t"]])
                    for t, d, isctx, r0, G2, bG2 in info:
                        S.op("dve", I_tt(d["t"][:], d["t"][:], lng[:], ALU.mult), reads=[d["b_t"], b_w], writes=[d["b_t"]])
                    for t, d, isctx, r0, G2, bG2 in info:
                        S.op("pool", I_tt(d["t"][:], d["t"][:], lnb[:], ALU.add), reads=[d["b_t"], b_w], writes=[d["b_t"]])
                    for t, d, isctx, r0, G2, bG2 in info:
                        if is_out:
                            if self.final:
                                if isctx:
                                    continue
                                dst = self.y[s * SEQ + (t - 2) * 128: s * SEQ + (t - 1) * 128, :]
                            else:
                                dst = self.y[r0:r0 + 128, :]
                            self.out_toks.append(S.dma("pool", I_dma(dst, d["t"][:]), reads=[d["b_t"]], cowrites=[self.bY]))
                        else:
                            S.dma("pool", I_dma(self.XA[r0:r0 + 128, :], d["t"][:]), reads=[d["b_t"]], cowrites=[self.bXA[s]])
            S.run_block(final_waits=self.out_toks if is_out else ())


_PROG_CACHE = {}


def get_prog(layers, first, final):
    key = (tuple(layers), first, final)
    if key not in _PROG_CACHE:
        _PROG_CACHE[key] = Prog(list(layers), first, final)
    return _PROG_CACHE[key]


def const_inputs():
    cg, sg = rope_tables(128)
    cm, sm = rope_tables(64)
    offs = np.zeros((64, 2), np.float32)
    for s in range(NS):
        offs[s * E:(s + 1) * E, 0] = s * NR + CTX
        offs[s * E:(s + 1) * E, 1] = s * NR
    return dict(identf=np.eye(128, dtype=np.float32), cosg=cg, sing=sg, cosm=cm, sinm=sm, offs=offs)


def layer_inputs(l, inp):
    j = l // 2
    m = {}
    m[f"ada_w{l}"] = inp["ada_w"][l]
    m[f"ada_b{l}"] = inp["ada_b"][l][None, :]
    m[f"ada_bT{l}"] = np.ascontiguousarray(inp["ada_b"][l].reshape(48, 128).T)
    m[f"lnmg{l}"] = inp["ln_mix_g"][l][None, :]; m[f"lnmb{l}"] = inp["ln_mix_b"][l][None, :]
    m[f"lnfg{l}"] = inp["ln_ffn_g"][l][None, :]; m[f"lnfb{l}"] = inp["ln_ffn_b"][l][None, :]
    m[f"rw{l}"] = inp["router_w"][l]
    m[f"wg{l}"] = inp["expert_w_gate"][l]; m[f"wu{l}"] = inp["expert_w_up"][l]; m[f"wd{l}"] = inp["expert_w_down"][l]
    if l % 2 == 0:
        m[f"wqkv{l}"] = inp["gqa_w_qkv"][j]; m[f"qg{l}"] = inp["gqa_q_g"][j][None, :]
        m[f"kg{l}"] = inp["gqa_k_g"][j][None, :]; m[f"wo{l}"] = inp["gqa_w_o"][j]
    else:
        m[f"wdq{l}"] = inp["mla_w_dq"][j]; m[f"mqg{l}"] = inp["mla_q_g"][j][None, :]; m[f"wuq{l}"] = inp["mla_w_uq"][j]
        m[f"wdkv{l}"] = inp["mla_w_dkv"][j]; m[f"mkvg{l}"] = inp["mla_kv_g"][j][None, :]
        m[f"wukv{l}"] = inp["mla_w_ukv"][j]; m[f"wo{l}"] = inp["mla_w_o"][j]
    return {k: np.ascontiguousarray(v, dtype=np.float32) for k, v in m.items()}


LAUNCH_GROUPS = [[0, 1, 2, 3]]


def kernel(**inp):
    inp = {k: np.asarray(v) for k, v in inp.items()}
    consts = const_inputs()
    xa = []
    ccs = []
    for c in range(NCORES):
        rows = []
        for s in range(NS):
            b = c * NS + s
            rows.append(inp["ctx"][b]); rows.append(inp["x"][b])
        xa.append(np.ascontiguousarray(np.concatenate(rows, 0), dtype=np.float32))
        cc = np.concatenate([inp["c"][c * NS:(c + 1) * NS], inp["c_ctx"][None, :]], 0)
        ccs.append(np.ascontiguousarray(cc.reshape(NS + 1, KC, 128).transpose(2, 1, 0), dtype=np.float32))
    out = None
    for gi, layers in enumerate(LAUNCH_GROUPS):
        final = layers[-1] == DEPTH - 1
        prog = get_prog(layers, gi == 0, final)
        wl = {}
        for l in layers:
            wl.update(layer_inputs(l, inp))
        in_maps = []
        for c in range(NCORES):
            m = dict(consts); m.update(wl)
            m["xa_in"] = xa[c]; m["ccT"] = ccs[c]
            in_maps.append(m)
        res = run_bass_kernel_spmd(prog.nc, in_maps, core_ids=list(range(NCORES)))
        if final:
            out = np.concatenate([r["y"].reshape(NS, SEQ, D) for r in res.results], 0)
        else:
            xa = [np.ascontiguousarray(r["xa_out"]) for r in res.results]
    return out.astype(np.float32)
```
